# Optimizing a Trainium2 kernel written in Bass

```python
import math
import jax, jax.numpy as jnp
from jax import lax
import numpy as np

D_MODEL = 2048
BATCH = 8
SEQ = 4096
DEPTH = 4

N_MIXERS = 4
EPS = 1e-6
D_FF = 4 * D_MODEL
QBLOCK = 128

A_HEAD_DIM = 128
A_HEADS = D_MODEL // (2 * A_HEAD_DIM)

R_HEADS = D_MODEL // 256
R_KDIM = D_MODEL // R_HEADS
R_VDIM = 2 * R_KDIM
R_CHUNK = 128

M_D_INNER = 2 * D_MODEL
M_HEADDIM = 64
M_HEADS = M_D_INNER // M_HEADDIM
M_GROUPS = 8
M_DSTATE = 128
M_CONV = 4
M_CHUNK = 256
M_CONV_DIM = M_D_INNER + 2 * M_GROUPS * M_DSTATE
M_IN_DIM = M_D_INNER + M_CONV_DIM + M_HEADS

D_HEAD_DIM = 128
D_HEADS = D_MODEL // D_HEAD_DIM
D_PATTERNS = ((128, 1), (512, 4), (2048, 16))
D_SPAN = 128
D_NGROUPS = len(D_PATTERNS)

kernel_name = 'hybrid_interleaved_diff_ret_ssd_dilated'

F32 = jnp.float32


def rmsnorm(x, w):
    xf = x.astype(F32)
    y = xf * lax.rsqrt(jnp.mean(xf * xf, axis=-1, keepdims=True) + EPS)
    return (y * w.astype(F32)).astype(x.dtype)


def diff_attention(h, w_in, w_out, q_norm_w, k_norm_w, lq1, lk1, lq2, lk2, subln_w, layer_idx):
    B_, S, _ = h.shape
    lambda_init = 0.8 - 0.6 * math.exp(-0.3 * layer_idx)
    q, k, v = jnp.split(h @ w_in, [D_MODEL, 2 * D_MODEL], axis=-1)
    q = rmsnorm(q.reshape(B_, S, 2 * A_HEADS, A_HEAD_DIM), q_norm_w).astype(F32)
    k = rmsnorm(k.reshape(B_, S, 2 * A_HEADS, A_HEAD_DIM), k_norm_w).astype(F32)
    v = v.reshape(B_, S, A_HEADS, 2 * A_HEAD_DIM).astype(F32)
    lam = (jnp.exp(jnp.sum(lq1.astype(F32) * lk1.astype(F32)))
           - jnp.exp(jnp.sum(lq2.astype(F32) * lk2.astype(F32))) + lambda_init)
    scale = A_HEAD_DIM ** -0.5
    nb = S // QBLOCK
    qb = q.reshape(B_, nb, QBLOCK, 2 * A_HEADS, A_HEAD_DIM).transpose(1, 0, 2, 3, 4)
    kpos = jnp.arange(S)

    def block(args):
        qi, bi = args
        s = jnp.einsum('bqhd,bkhd->bhqk', qi, k) * scale
        qpos = bi * QBLOCK + jnp.arange(QBLOCK)
        s = jnp.where(kpos[None, :] <= qpos[:, None], s, -jnp.inf)
        p = jax.nn.softmax(s, axis=-1).reshape(B_, A_HEADS, 2, QBLOCK, S)
        a = p[:, :, 0] - lam * p[:, :, 1]
        return jnp.einsum('bhqk,bkhd->bqhd', a, v)

    o = lax.map(block, (qb, jnp.arange(nb)))
    o = o.transpose(1, 0, 2, 3, 4).reshape(B_, S, A_HEADS, 2 * A_HEAD_DIM)
    o = rmsnorm(o, subln_w) * (1.0 - lambda_init)
    return o.reshape(B_, S, D_MODEL) @ w_out


def _rotate_every_two(t):
    t1 = t[..., ::2]
    t2 = t[..., 1::2]
    return jnp.stack([-t2, t1], axis=-1).reshape(t.shape)


def retention(h, w_in, gn_w, w_out):
    B_, S, _ = h.shape
    C = R_CHUNK
    nc = S // C
    proj = (h @ w_in).astype(F32)
    q, k, v, g = jnp.split(proj, [D_MODEL, 2 * D_MODEL, 2 * D_MODEL + R_HEADS * R_VDIM], axis=-1)
    q = q.reshape(B_, S, R_HEADS, R_KDIM)
    k = k.reshape(B_, S, R_HEADS, R_KDIM) * (R_KDIM ** -0.5)
    v = v.reshape(B_, S, R_HEADS, R_VDIM)
    angle = jnp.repeat(1.0 / (10000.0 ** jnp.linspace(0.0, 1.0, R_KDIM // 2, dtype=F32)), 2)
    ang = jnp.arange(S, dtype=F32)[:, None] * angle[None, :]
    sin, cos = jnp.sin(ang)[:, None, :], jnp.cos(ang)[:, None, :]
    q = q * cos + _rotate_every_two(q) * sin
    k = k * cos + _rotate_every_two(k) * sin
    log_g = jnp.log(1.0 - 2.0 ** (-5.0 - jnp.arange(R_HEADS, dtype=F32)))
    idx = jnp.arange(C, dtype=F32)
    rel = idx[:, None] - idx[None, :]
    dmask = jnp.where(rel[None] >= 0, jnp.exp(jnp.maximum(rel, 0.0)[None] * log_g[:, None, None]), 0.0)
    xi = jnp.exp((idx + 1.0)[None, :] * log_g[:, None])
    zeta = jnp.exp((C - 1.0 - idx)[None, :] * log_g[:, None])
    cdecay = jnp.exp(C * log_g)

    def chunks(t):
        return t.reshape(B_, nc, C, R_HEADS, t.shape[-1]).transpose(1, 0, 3, 2, 4)

    def step(R, inp):
        qc, kc, vc = inp
        s = jnp.einsum('bhid,bhjd->bhij', qc, kc) * dmask[None]
        o = (jnp.einsum('bhij,bhjv->bhiv', s, vc)
             + jnp.einsum('bhid,bhdv->bhiv', qc * xi[None, :, :, None], R))
        R = R * cdecay[None, :, None, None] + jnp.einsum('bhjd,bhjv->bhdv', kc * zeta[None, :, :, None], vc)
        return R, o

    R0 = jnp.zeros((B_, R_HEADS, R_KDIM, R_VDIM), F32)
    _, o = lax.scan(step, R0, (chunks(q), chunks(k), chunks(v)))
    o = o.transpose(1, 0, 3, 2, 4).reshape(B_, S, R_HEADS, R_VDIM)
    o = rmsnorm(o, gn_w.reshape(R_HEADS, R_VDIM)).reshape(B_, S, R_HEADS * R_VDIM)
    return (jax.nn.silu(g) * o) @ w_out


def ssd_chunked(xdt, a, Bm, Cm):
    B_, S = xdt.shape[:2]
    pad = (-S) % M_CHUNK
    L = S + pad
    nc = L // M_CHUNK
    Rh = M_HEADS // M_GROUPS
    padt = lambda t: jnp.pad(t, [(0, 0), (0, pad)] + [(0, 0)] * (t.ndim - 2))
    xc = padt(xdt).reshape(B_, nc, M_CHUNK, M_GROUPS, Rh, M_HEADDIM).transpose(1, 0, 2, 3, 4, 5)
    ac = padt(a).reshape(B_, nc, M_CHUNK, M_GROUPS, Rh).transpose(1, 0, 2, 3, 4)
    bc = padt(Bm).reshape(B_, nc, M_CHUNK, M_GROUPS, M_DSTATE).transpose(1, 0, 2, 3, 4)
    cc = padt(Cm).reshape(B_, nc, M_CHUNK, M_GROUPS, M_DSTATE).transpose(1, 0, 2, 3, 4)
    tril = jnp.tril(jnp.ones((M_CHUNK, M_CHUNK), bool))

    def step(state, inp):
        x_, a_, b_, c_ = inp
        acs = jnp.cumsum(a_, axis=1)
        diff = acs[:, :, None] - acs[:, None, :]
        Lmat = jnp.exp(jnp.where(tril[None, :, :, None, None], diff, -jnp.inf))
        cb = jnp.einsum('bign,bjgn->bijg', c_, b_)
        y = (jnp.einsum('bijg,bijgr,bjgrp->bigrp', cb, Lmat, x_)
             + jnp.einsum('bign,bgrpn,bigr->bigrp', c_, state, jnp.exp(acs)))
        decay_end = jnp.exp(acs[:, -1:] - acs)
        state = (state * jnp.exp(acs[:, -1])[..., None, None]
                 + jnp.einsum('bjgn,bjgr,bjgrp->bgrpn', b_, decay_end, x_))
        return state, y

    s0 = jnp.zeros((B_, M_GROUPS, Rh, M_HEADDIM, M_DSTATE), F32)
    _, y = lax.scan(step, s0, (xc, ac, bc, cc))
    y = y.transpose(1, 0, 2, 3, 4, 5).reshape(B_, L, M_HEADS, M_HEADDIM)
    return y[:, :S]


def mamba2(h, w_in, conv_w, conv_b, dt_bias, a_log, d_skip, norm_w, w_out):
    B_, S, _ = h.shape
    zxbcdt = (h @ w_in).astype(F32)
    z, xbc, dt = jnp.split(zxbcdt, [M_D_INNER, M_D_INNER + M_CONV_DIM], axis=-1)
    xbc = lax.conv_general_dilated(
        xbc, conv_w.astype(F32).reshape(M_CONV, 1, M_CONV_DIM), window_strides=(1,),
        padding=[(M_CONV - 1, 0)], dimension_numbers=('NWC', 'WIO', 'NWC'),
        feature_group_count=M_CONV_DIM)
    xbc = jax.nn.silu(xbc + conv_b.astype(F32))
    xs, Bm, Cm = jnp.split(xbc, [M_D_INNER, M_D_INNER + M_GROUPS * M_DSTATE], axis=-1)
    xs = xs.reshape(B_, S, M_HEADS, M_HEADDIM)
    Bm = Bm.reshape(B_, S, M_GROUPS, M_DSTATE)
    Cm = Cm.reshape(B_, S, M_GROUPS, M_DSTATE)
    dt = jax.nn.softplus(dt + dt_bias.astype(F32))
    A = -jnp.exp(a_log.astype(F32))
    y = ssd_chunked(xs * dt[..., None], dt * A, Bm, Cm)
    y = y + xs * d_skip.astype(F32)[:, None]
    y = y.reshape(B_, S, M_D_INNER) * jax.nn.silu(z)
    y = rmsnorm(y.reshape(B_, S, M_GROUPS, M_D_INNER // M_GROUPS),
                norm_w.reshape(M_GROUPS, M_D_INNER // M_GROUPS)).reshape(B_, S, M_D_INNER)
    return y @ w_out


def strided_window_attention(q, k, v, dil):
    B_, S, H, E = q.shape
    Ls = S // dil
    pad = (-Ls) % D_SPAN
    nb = (Ls + pad) // D_SPAN

    def to_blocks(t):
        t = t.reshape(B_, Ls, dil, H, E).transpose(0, 2, 1, 3, 4)
        t = jnp.pad(t, ((0, 0), (0, 0), (0, pad), (0, 0), (0, 0)))
        return t.reshape(B_, dil, nb, D_SPAN, H, E)

    def with_prev(t):
        prev = jnp.pad(t, ((0, 0), (0, 0), (1, 0), (0, 0), (0, 0), (0, 0)))[:, :, :-1]
        return jnp.concatenate([prev, t], axis=3)

    qb = to_blocks(q)
    kk = with_prev(to_blocks(k))
    vv = with_prev(to_blocks(v))
    s = jnp.einsum('bdnqhe,bdnkhe->bdnhqk', qb, kk) * (E ** -0.5)
    qi = jnp.arange(D_SPAN) + D_SPAN
    ki = jnp.arange(2 * D_SPAN)
    rel = qi[:, None] - ki[None, :]
    band = (rel >= 0) & (rel <= D_SPAN)
    has_prev = (jnp.arange(nb)[:, None] > 0) | (ki[None, :] >= D_SPAN)
    mask = band[None] & has_prev[:, None, :]
    s = jnp.where(mask[None, None, :, None], s, -jnp.inf)
    lse = jax.nn.logsumexp(s, axis=-1)
    p = jnp.exp(s - lse[..., None])
    o = jnp.einsum('bdnhqk,bdnkhe->bdnqhe', p, vv)
    o = o.reshape(B_, dil, nb * D_SPAN, H, E)[:, :, :Ls].transpose(0, 2, 1, 3, 4).reshape(B_, S, H, E)
    lse = lse.transpose(0, 1, 2, 4, 3).reshape(B_, dil, nb * D_SPAN, H)[:, :, :Ls]
    lse = lse.transpose(0, 2, 1, 3).reshape(B_, S, H)
    return o, lse


def dilated_attention(h, w_in, q_norm_w, k_norm_w, w_out):
    B_, S, _ = h.shape
    qkv = (h @ w_in).reshape(B_, S, D_NGROUPS, 3, D_HEADS, D_HEAD_DIM)
    outs, lses = [], []
    for g, (window, dil) in enumerate(D_PATTERNS):
        q = rmsnorm(qkv[:, :, g, 0], q_norm_w[g]).astype(F32)
        k = rmsnorm(qkv[:, :, g, 1], k_norm_w[g]).astype(F32)
        v = qkv[:, :, g, 2].astype(F32)
        o, lse = strided_window_attention(q, k, v, dil)
        outs.append(o)
        lses.append(lse)
    wts = jax.nn.softmax(jnp.stack(lses, 0), axis=0)
    o = jnp.einsum('gbsh,gbshe->bshe', wts, jnp.stack(outs, 0))
    return o.reshape(B_, S, D_HEADS * D_HEAD_DIM) @ w_out


def sqrelu_mlp(h, w1, w2):
    return jnp.square(jax.nn.relu(h @ w1)) @ w2


def setup_inputs(seed: int = 0) -> dict:
    key = jax.random.key(seed)
    it = iter(list(jax.random.split(key, 48)))
    nrm = lambda shape, scale: jax.random.normal(next(it), shape, F32) * scale
    gain = lambda shape: 1.0 + 0.02 * jax.random.normal(next(it), shape, F32)
    dt = jnp.exp(jax.random.uniform(next(it), (M_HEADS,), F32) * (math.log(0.1) - math.log(0.001)) + math.log(0.001))
    dt_bias = dt + jnp.log(-jnp.expm1(-dt))
    a_log = jnp.log(jax.random.uniform(next(it), (M_HEADS,), F32, 1.0, 16.0))
    return {
        'x': nrm((BATCH, SEQ, D_MODEL), 1.0),
        'norm1_w': gain((DEPTH, D_MODEL)),
        'norm2_w': gain((DEPTH, D_MODEL)),
        'mlp_w1': nrm((DEPTH, D_MODEL, D_FF), D_MODEL ** -0.5),
        'mlp_w2': nrm((DEPTH, D_FF, D_MODEL), D_FF ** -0.5),
        'a_w_in': nrm((D_MODEL, 3 * D_MODEL), D_MODEL ** -0.5),
        'a_q_norm_w': gain((A_HEAD_DIM,)),
        'a_k_norm_w': gain((A_HEAD_DIM,)),
        'a_lambda_q1': nrm((A_HEAD_DIM,), 0.1),
        'a_lambda_k1': nrm((A_HEAD_DIM,), 0.1),
        'a_lambda_q2': nrm((A_HEAD_DIM,), 0.1),
        'a_lambda_k2': nrm((A_HEAD_DIM,), 0.1),
        'a_subln_w': gain((2 * A_HEAD_DIM,)),
        'a_w_out': nrm((D_MODEL, D_MODEL), D_MODEL ** -0.5),
        'b_w_in': nrm((D_MODEL, 2 * D_MODEL + 2 * R_HEADS * R_VDIM), D_MODEL ** -0.5),
        'b_gn_w': gain((R_HEADS * R_VDIM,)),
        'b_w_out': nrm((R_HEADS * R_VDIM, D_MODEL), (R_HEADS * R_VDIM) ** -0.5),
        'c_w_in': nrm((D_MODEL, M_IN_DIM), D_MODEL ** -0.5),
        'c_conv_w': nrm((M_CONV, M_CONV_DIM), M_CONV ** -0.5),
        'c_conv_b': nrm((M_CONV_DIM,), 0.02),
        'c_dt_bias': dt_bias,
        'c_a_log': a_log,
        'c_d_skip': 1.0 + 0.1 * jax.random.normal(next(it), (M_HEADS,), F32),
        'c_norm_w': gain((M_D_INNER,)),
        'c_w_out': nrm((M_D_INNER, D_MODEL), M_D_INNER ** -0.5),
        'd_w_in': nrm((D_MODEL, D_NGROUPS * 3 * D_HEADS * D_HEAD_DIM), D_MODEL ** -0.5),
        'd_q_norm_w': gain((D_NGROUPS, D_HEAD_DIM)),
        'd_k_norm_w': gain((D_NGROUPS, D_HEAD_DIM)),
        'd_w_out': nrm((D_HEADS * D_HEAD_DIM, D_MODEL), (D_HEADS * D_HEAD_DIM) ** -0.5),
    }


def reference(x, norm1_w, norm2_w, mlp_w1, mlp_w2,
              a_w_in, a_q_norm_w, a_k_norm_w, a_lambda_q1, a_lambda_k1, a_lambda_q2, a_lambda_k2,
              a_subln_w, a_w_out,
              b_w_in, b_gn_w, b_w_out,
              c_w_in, c_conv_w, c_conv_b, c_dt_bias, c_a_log, c_d_skip, c_norm_w, c_w_out,
              d_w_in, d_q_norm_w, d_k_norm_w, d_w_out):
    for i in range(DEPTH):
        h = rmsnorm(x, norm1_w[i])
        m = i % N_MIXERS
        if m == 0:
            out = diff_attention(h, a_w_in, a_w_out, a_q_norm_w, a_k_norm_w, a_lambda_q1,
                                 a_lambda_k1, a_lambda_q2, a_lambda_k2, a_subln_w, i)
        elif m == 1:
            out = retention(h, b_w_in, b_gn_w, b_w_out)
        elif m == 2:
            out = mamba2(h, c_w_in, c_conv_w, c_conv_b, c_dt_bias, c_a_log, c_d_skip, c_norm_w, c_w_out)
        else:
            out = dilated_attention(h, d_w_in, d_q_norm_w, d_k_norm_w, d_w_out)
        x = x + out.astype(x.dtype)
        h = rmsnorm(x, norm2_w[i])
        x = x + sqrelu_mlp(h, mlp_w1[i], mlp_w2[i]).astype(x.dtype)
    return x
```

```python
import numpy as np
from contextlib import ExitStack
import concourse.bass as bass
import concourse.mybir as mybir

F32 = mybir.dt.float32
BF16 = mybir.dt.bfloat16
AF = mybir.ActivationFunctionType
ALU = mybir.AluOpType
AX = mybir.AxisListType

SEM_CAP = 30000
N_DMA_SEMS = 8
SAME_ENG_SYNC = True


class Sched:
    def __init__(self, nc):
        self.nc = nc
        self.ops = []
        self.last_w = {}
        self.readers = {}
        self.since_barrier = []

    def add(self, eng, fn, reads=(), writes=(), dma=False):
        idx = len(self.ops)
        deps = set()
        for r in reads:
            w = self.last_w.get(r)
            if w is not None:
                deps.add(w)
        for r in writes:
            w = self.last_w.get(r)
            if w is not None:
                deps.add(w)
            deps.update(self.readers.get(r, ()))
        for r in reads:
            self.readers.setdefault(r, []).append(idx)
        for r in writes:
            self.last_w[r] = idx
            self.readers[r] = []
        deps.discard(idx)
        self.ops.append(dict(eng=eng, fn=fn, deps=deps, dma=dma, load=(dma and len(writes) > 0)))
        self.since_barrier.append(idx)
        return idx

    def dma(self, out, in_, reads=(), writes=(), q='sp', **kw):
        return self.add(q, lambda e: e.dma_start(out=out, in_=in_, **kw), reads, writes, dma=True)

    def barrier(self):
        lastc = {}
        prev = []
        for i in self.since_barrier:
            o = self.ops[i]
            if o['dma']:
                prev.append(i)
            elif o['fn'] is not None:
                lastc[o['eng']] = i
        prev += list(lastc.values())
        self.since_barrier = []
        self.last_w = {}
        self.readers = {}
        for eng in ['pe', 'act', 'dve', 'pool', 'sp']:
            idx = len(self.ops)
            self.ops.append(dict(eng=eng, fn=None, deps=set(prev), dma=False))
            self.since_barrier.append(idx)

    def emit(self):
        nc = self.nc
        ops = self.ops
        needed = [False] * len(ops)
        for i, op in enumerate(ops):
            for d in op['deps']:
                od = ops[d]
                if od['dma']:
                    needed[d] = True
                elif od['eng'] == op['eng'] and (od['eng'] == 'pe' or not SAME_ENG_SYNC):
                    continue
                else:
                    needed[d] = True
        order = {e: [] for e in ['pe', 'act', 'dve', 'pool', 'sp']}
        for i, op in enumerate(ops):
            order[op['eng']].append(i)
        for e in ('sp', 'pool'):
            out = []
            for i in order[e]:
                op = ops[i]
                pos = len(out)
                if op['dma'] and op.get('load'):
                    md = max(op['deps']) if op['deps'] else -1
                    while pos > 0:
                        p = out[pos - 1]
                        po = ops[p]
                        if po['dma'] and not po.get('load') and p > md:
                            pos -= 1
                        else:
                            break
                out.insert(pos, i)
            order[e] = out
        dma_n = {}
        dma_rr = {}
        events = [None] * len(ops)
        pre_wait = [None] * len(ops)
        for e, lst in order.items():
            cnt = 0
            for i in lst:
                op = ops[i]
                if op['dma']:
                    k = dma_rr.get(e, 0)
                    dma_rr[e] = (k + 1) % N_DMA_SEMS
                    key = ('dma', e, k)
                    n = dma_n.get(key, 0)
                    if n > 0:
                        pre_wait[i] = (key, 16 * n)
                    n += 1
                    if 16 * n > SEM_CAP:
                        raise RuntimeError('dma sem cap exceeded; raise N_DMA_SEMS')
                    dma_n[key] = n
                    events[i] = (key, 16 * n)
                elif needed[i]:
                    cnt += 1
                    epoch, val = divmod(cnt - 1, SEM_CAP)
                    events[i] = (('eng', e, epoch), val + 1)
        keys = []
        for ev in events:
            if ev is not None and ev[0] not in keys:
                keys.append(ev[0])
        self.stack = ExitStack()
        sems = {}
        for k in keys:
            sems[k] = self.stack.enter_context(nc.semaphore('_'.join(str(x) for x in k)))
        self.n_sems = len(keys)
        streams = {e: [] for e in order}
        n_wait = 0
        for e, lst in order.items():
            seen = {}
            for i in lst:
                op = ops[i]
                waits = {}
                if pre_wait[i] is not None:
                    k, v = pre_wait[i]
                    waits[k] = max(waits.get(k, 0), v)
                for d in op['deps']:
                    od = ops[d]
                    if not od['dma'] and od['eng'] == e and (e == 'pe' or not SAME_ENG_SYNC):
                        continue
                    ev = events[d]
                    assert ev is not None, (i, d)
                    k, v = ev
                    waits[k] = max(waits.get(k, 0), v)
                wl = []
                for k, v in waits.items():
                    if seen.get(k, 0) >= v:
                        continue
                    seen[k] = v
                    wl.append((sems[k], v))
                n_wait += len(wl)
                inc = None
                if events[i] is not None:
                    inc = (sems[events[i][0]], 16 if op['dma'] else 1)
                streams[e].append((wl, op['fn'], inc))
        self.n_wait = n_wait

        def run(engobj, lst):
            for wl, fn, inc in lst:
                for s, v in wl:
                    engobj.wait_ge(s, v)
                if fn is None:
                    continue
                ins = fn(engobj)
                if inc is not None:
                    ins.then_inc(inc[0], inc[1])

        with nc.Block() as block:
            @block.tensor
            def _(t):
                run(t, streams['pe'])

            @block.scalar
            def _(t):
                run(t, streams['act'])

            @block.vector
            def _(t):
                run(t, streams['dve'])

            @block.gpsimd
            def _(t):
                run(t, streams['pool'])

            @block.sync
            def _(t):
                run(t, streams['sp'])
        self.stack.close()


class SbufAlloc:
    def __init__(self, nc, base=16640, limit=229376 - 64):
        self.nc = nc
        self.base = base
        self.off = base
        self.limit = limit
        self.uid = 0
        self.marks = []

    def push(self):
        self.marks.append(self.off)

    def pop(self):
        self.off = self.marks.pop()

    def tile(self, shape, dtype, name='t'):
        esz = 4 if dtype == F32 else 2
        n = 1
        for s in shape[1:]:
            n *= s
        nbytes = (n * esz + 63) // 64 * 64
        self.uid += 1
        t = self.nc.alloc_sbuf_tensor_at(f'{name}_{self.uid}', list(shape), dtype, offset=self.off)
        self.off += nbytes
        assert self.off <= self.limit, (name, self.off)
        return t


import math
import numpy as np
from concourse.bass_utils import run_bass_kernel_spmd

SL = 4096
D = 2048
NT = 32
EPS = 1e-6
TB = 1024


class Ctx:
    pass


def bc(ap, shape, axis):
    return ap.unsqueeze(axis).to_broadcast(list(shape))


def setup_consts(C):
    nc, S, A = C.nc, C.S, C.A
    C.ps = [nc.alloc_psum_tensor(f"ps{i}", [128, 512], F32) for i in range(8)]
    C.psb = [p[:].bitcast(BF16) for p in C.ps]
    C.identf = A.tile([128, 128], F32, 'identf')
    C.ident = A.tile([128, 128], BF16, 'ident')
    C.mle_f = A.tile([128, 128], F32, 'mlef')
    C.mge_f = A.tile([128, 128], F32, 'mgef')
    C.mask4 = A.tile([128, 4, 128], BF16, 'mask4')
    C.mle4 = A.tile([128, 4, 128], BF16, 'mle4')
    C.mle = A.tile([128, 128], BF16, 'mle')
    C.gmask = A.tile([128, 384], F32, 'gmask')
    for t, pat, cm, op in [(C.identf, [[-1, 128]], 1, ALU.is_equal),
                           (C.mle_f, [[1, 128]], -1, ALU.is_ge),
                           (C.mge_f, [[-1, 128]], 1, ALU.is_ge)]:
        S.add('pool', lambda e, t=t: e.memset(t[:], 1.0), writes=[t.name])
        S.add('pool', lambda e, t=t, pat=pat, cm=cm, op=op: e.affine_select(t[:], t[:], pat, op, 0.0, base=0, channel_multiplier=cm),
              reads=[t.name], writes=[t.name])
    S.add('dve', lambda e: e.tensor_copy(C.ident[:], C.identf[:]), reads=[C.identf.name], writes=['c1'])
    S.add('dve', lambda e: e.tensor_copy(C.mle[:], C.mle_f[:]), reads=[C.mle_f.name], writes=['c2'])
    for k in range(4):
        src = C.mge_f if k % 2 == 0 else C.mle_f
        S.add('dve', lambda e, k=k, src=src: e.tensor_copy(C.mask4[:, k, :], src[:]), reads=[src.name], writes=[('c3', k)])
        S.add('dve', lambda e, k=k: e.tensor_copy(C.mle4[:, k, :], C.mle_f[:]), reads=[C.mle_f.name], writes=[('c4', k)])
    S.add('pool', lambda e: e.memset(C.gmask[:, 128:256], 1.0), writes=['c8'])
    S.add('dve', lambda e: e.tensor_copy(C.gmask[:, 0:128], C.mle_f[:]), reads=[C.mle_f.name], writes=['c9'])
    S.add('dve', lambda e: e.tensor_copy(C.gmask[:, 256:384], C.mle_f[:]), reads=[C.mle_f.name], writes=['c10'])
    S.barrier()


class BgConv:
    def __init__(self, C):
        self.C = C
        self.jobs = []
        self.done = set()
        self.i = 0
        A = C.A
        self.f = [A.tile([128, 2048], F32, 'bgf') for _ in range(2)]
        self.b = [A.tile([128, 2048], BF16, 'bgb') for _ in range(2)]

    def add(self, name, src, dst):
        sv, dv = flat128(src), flat128(dst)
        F_ = sv.shape[1]
        steps = []
        for f0 in range(0, F_, 2048):
            steps.append((sv, dv, f0, min(2048, F_ - f0)))
        self.jobs.append([name, steps])

    def _one(self, eng='pool'):
        S = self.C.S
        name, steps = self.jobs[0]
        sv, dv, f0, fsz = steps.pop(0)
        s = self.i % 2
        self.i += 1
        f, b = self.f[s], self.b[s]
        S.dma(f[:, :fsz], sv[:, f0:f0 + fsz], writes=[('bgf', s)])
        if eng == 'act':
            S.add('act', lambda e: e.copy(b[:, :fsz], f[:, :fsz]), reads=[('bgf', s)], writes=[('bgb', s)])
        else:
            S.add(eng, lambda e: e.tensor_copy(b[:, :fsz], f[:, :fsz]), reads=[('bgf', s)], writes=[('bgb', s)])
        S.dma(dv[:, f0:f0 + fsz], b[:, :fsz], reads=[('bgb', s)])
        if not steps:
            self.done.add(name)
            self.jobs.pop(0)

    def step(self, n):
        for _ in range(n):
            if not self.jobs:
                return
            self._one()

    def require(self, name):
        if name in self.done:
            return
        k = 0
        while name not in self.done:
            self._one(['pool', 'act', 'dve'][k % 3])
            k += 1
        self.C.S.barrier()


def flat128(ap2d):
    return ap2d.rearrange("k n -> (k n)").rearrange("(p f) -> p f", p=128)


def phase_convert(C, pairs):
    S, A = C.S, C.A
    A.push()
    CH = 4096
    fs = [A.tile([128, CH], F32, 'cvf') for _ in range(3)]
    bs = [A.tile([128, CH], BF16, 'cvb') for _ in range(3)]
    i = 0
    for src, dst in pairs:
        sv, dv = flat128(src), flat128(dst)
        F_ = sv.shape[1]
        for f0 in range(0, F_, CH):
            fsz = min(CH, F_ - f0)
            s = i % 3
            eng = ['act', 'dve', 'pool'][i % 3]
            S.dma(fs[s][:, :fsz], sv[:, f0:f0 + fsz], writes=[('cf', s)])
            if eng == 'act':
                S.add('act', lambda e, s=s, fsz=fsz: e.copy(bs[s][:, :fsz], fs[s][:, :fsz]), reads=[('cf', s)], writes=[('cb', s)])
            else:
                S.add(eng, lambda e, s=s, fsz=fsz: e.tensor_copy(bs[s][:, :fsz], fs[s][:, :fsz]), reads=[('cf', s)], writes=[('cb', s)])
            S.dma(dv[:, f0:f0 + fsz], bs[s][:, :fsz], reads=[('cb', s)])
            i += 1
    A.pop()
    S.barrier()


def rstd_ops(C, ss, rs, n, reads, tag):
    S = C.S
    S.add('act', lambda e: e.activation(rs, ss, AF.Sqrt, bias=EPS, scale=1.0 / n), reads=reads, writes=[('rs', tag)])
    S.add('dve', lambda e: e.reciprocal(rs, rs), reads=[('rs', tag)], writes=[('rs', tag)])


def phase_norm(C, x_src, w_ap, hT):
    S, A, ps, psb = C.S, C.A, C.ps, C.psb
    A.push()
    wt = A.tile([128, D], F32, 'nw')
    S.dma(wt[:], w_ap.partition_broadcast(128), writes=['nw'])
    xs = [A.tile([128, D], F32, 'nx') for _ in range(2)]
    sq = A.tile([128, D], BF16, 'nsq')
    hs = [A.tile([128, D], BF16, 'nh') for _ in range(2)]
    st = [A.tile([128, 16, 512], BF16, 'nst') for _ in range(2)]
    sm = A.tile([128, 4], F32, 'nsm')
    hTv = hT.rearrange("(c p) t -> p c t", p=128)
    for g in range(8):
        gs = g % 2
        toks = []
        for j in range(4):
            t = g * 4 + j
            s = t % 2
            S.dma(xs[s][:], x_src[t * 128:(t + 1) * 128, :], writes=[('x', s)])
            S.add('act', lambda e, s=s: e.activation(sq[:], xs[s][:], AF.Square, accum_out=sm[:, s:s + 1]), reads=[('x', s)], writes=[('ss', s)])
            rstd_ops(C, sm[:, s:s + 1], sm[:, 2 + s:3 + s], D, [('ss', s)], ('n', s))
            S.add('dve', lambda e, s=s: e.scalar_tensor_tensor(hs[s][:], xs[s][:], sm[:, 2 + s:3 + s], wt[:], ALU.mult, ALU.mult),
                  reads=[('x', s), ('rs', ('n', s)), 'nw'], writes=[('h', s)])
            for half in range(2):
                bank = 6 + half
                for c in range(8):
                    cc = half * 8 + c
                    S.add('pe', lambda e, s=s, c=c, cc=cc, bank=bank: e.transpose(psb[bank][:, c * 128:(c + 1) * 128], hs[s][:, cc * 128:(cc + 1) * 128], C.ident[:]),
                          reads=[('h', s)], writes=[('ps', bank)])
                dst = st[gs][:, half * 8:(half + 1) * 8, j * 128:(j + 1) * 128]
                srcv = psb[bank][:, 0:1024].rearrange("p (c t) -> p c t", c=8)
                tk = ('st', gs, j, half)
                toks.append(tk)
                if half == 0:
                    S.add('act', lambda e, dst=dst, srcv=srcv: e.copy(dst, srcv), reads=[('ps', bank)], writes=[tk])
                else:
                    S.add('dve', lambda e, dst=dst, srcv=srcv: e.tensor_copy(dst, srcv), reads=[('ps', bank)], writes=[tk])
        S.dma(hTv[:, :, g * 512:(g + 1) * 512], st[gs][:], reads=toks)
    A.pop()
    S.barrier()


def phase_linear(C, aT, Wb, K, nblocks, epi, on_tb=None, on_block=None, end_block=None, fm=False, lin_banks=(0, 1, 2, 3), finish=None, bg_steps=2):
    S, A, ps = C.S, C.A, C.ps
    KC = K // 128
    C.bg.require(C.Wn[id(Wb)])
    A.push()
    NA = 2 if KC == 16 else 1
    ats = [A.tile([128, KC, TB], BF16, 'lat') for _ in range(NA)]
    wt = [A.tile([128, KC, 512], BF16, 'lwt') for _ in range(2)]
    aTv = aT.rearrange("(c p) t -> p c t", p=128)
    Wv = Wb.rearrange("(c p) n -> p c n", p=128)
    C.lin_rr = 0
    wi = 0
    def load_at(tb):
        sa = tb % NA
        for kc in range(KC):
            S.dma(ats[sa][:, kc, :], aTv[:, kc, tb * TB:(tb + 1) * TB], writes=[('at', sa, kc)])

    load_at(0)
    for tb in range(SL // TB):
        sa = tb % NA
        at = ats[sa]
        if NA == 1 and tb > 0:
            load_at(tb)
        if NA == 2 and tb + 1 < SL // TB:
            load_at(tb + 1)
        if on_tb:
            on_tb(tb)
        for bi, (n0, nsz) in enumerate(nblocks):
            sw = wi % 2
            wi += 1
            S.dma(wt[sw][:, :, :nsz], Wv[:, :, n0:n0 + nsz], writes=[('wt', sw)])
            if on_block:
                on_block(tb, bi)
            if not fm:
                for ti in range(TB // 128):
                    bank = lin_banks[C.lin_rr % len(lin_banks)]
                    C.lin_rr += 1
                    for kc in range(KC):
                        S.add('pe', lambda e, bank=bank, kc=kc, ti=ti, sw=sw, nsz=nsz, at=at: e.matmul(ps[bank][:, :nsz], at[:, kc, ti * 128:(ti + 1) * 128], wt[sw][:, kc, :nsz], start=(kc == 0), stop=(kc == KC - 1)),
                              reads=[('at', sa, kc), ('wt', sw)], writes=[('ps', bank)])
                    epi(tb, bi, ti, bank, nsz)
            else:
                for j in range((nsz + 127) // 128):
                    msz = min(128, nsz - j * 128)
                    for th in range(TB // 512):
                        bank = lin_banks[C.lin_rr % len(lin_banks)]
                        C.lin_rr += 1
                        for kc in range(KC):
                            S.add('pe', lambda e, bank=bank, kc=kc, th=th, sw=sw, j=j, msz=msz, at=at: e.matmul(ps[bank][:msz, :512], wt[sw][:, kc, j * 128:j * 128 + msz], at[:, kc, th * 512:(th + 1) * 512], start=(kc == 0), stop=(kc == KC - 1)),
                                  reads=[('at', sa, kc), ('wt', sw)], writes=[('ps', bank)])
                        epi(tb, bi, j, th, bank, msz)
            C.bg.step(bg_steps)
            if end_block:
                end_block(tb, bi)
    if finish:
        finish()
    A.pop()
    S.barrier()


def phase_outproj(C, oT, Wb, K, x_src, x_dst):
    S, A, ps = C.S, C.A, C.ps
    A.push()
    xp = [A.tile([128, 8, 512], F32, 'xp') for _ in range(2)]
    st = {}

    def on_block(tb, bi):
        s = (tb * 4 + bi) % 2
        st['s'] = s
        src = x_src[tb * TB:(tb + 1) * TB, bi * 512:(bi + 1) * 512].rearrange("(t p) n -> p t n", p=128)
        S.dma(xp[s][:], src, writes=[('xp', s, ti) for ti in range(8)])

    def epi(tb, bi, ti, bank, nsz):
        s = st['s']
        S.add('dve', lambda e: e.tensor_tensor(xp[s][:, ti, :], ps[bank][:, :512], xp[s][:, ti, :], ALU.add),
              reads=[('ps', bank), ('xp', s, ti)], writes=[('xp', s, ti)])

    def end_block(tb, bi):
        s = st['s']
        dst = x_dst[tb * TB:(tb + 1) * TB, bi * 512:(bi + 1) * 512].rearrange("(t p) n -> p t n", p=128)
        S.dma(dst, xp[s][:], reads=[('xp', s, ti) for ti in range(8)])

    phase_linear(C, oT, Wb, K, [(i * 512, 512) for i in range(4)], epi, on_block=on_block, end_block=end_block, bg_steps=0)
    A.pop()


def phase_mlp(C, hT, W1b, W2b, x_io):
    S, A, ps = C.S, C.A, C.ps
    C.bg.require(C.Wn[id(W1b)])
    C.bg.require(C.Wn[id(W2b)])
    A.push()
    at = A.tile([128, 16, TB], BF16, 'mat')
    acc = A.tile([128, 8, D], F32, 'macc')
    w1 = [A.tile([128, 16, 512], BF16, 'mw1') for _ in range(2)]
    w2 = [A.tile([128, 4, D], BF16, 'mw2') for _ in range(2)]
    uT = [A.tile([128, 4, TB], BF16, 'muT') for _ in range(2)]
    rl = [A.tile([128, 512], F32, 'mrl')]
    hTv = hT.rearrange("(c p) t -> p c t", p=128)
    W1v = W1b.rearrange("(c p) n -> p c n", p=128)
    W2v = W2b.rearrange("(c p) n -> p c n", p=128)
    rr = 0
    ri = 0
    for tb in range(SL // TB):
        for kc in range(16):
            S.dma(at[:, kc, :], hTv[:, kc, tb * TB:(tb + 1) * TB], writes=[('at', kc)])
        for ti in range(8):
            r0 = tb * TB + ti * 128
            S.dma(acc[:, ti, :], x_io[r0:r0 + 128, :], writes=[('acc', ti, dc) for dc in range(4)])
        for fg in range(16):
            s = fg % 2
            S.dma(w1[s][:], W1v[:, :, fg * 512:(fg + 1) * 512], writes=[('w1', s)])
            S.dma(w2[s][:], W2v[:, fg * 4:(fg + 1) * 4, :], writes=[('w2', s)])
            C.bg.step(3)
            for fc in range(4):
                for th in range(2):
                    bank = rr % 4
                    rr += 1
                    for kc in range(16):
                        S.add('pe', lambda e, bank=bank, kc=kc, s=s, fc=fc, th=th: e.matmul(ps[bank][:, :512], w1[s][:, kc, fc * 128:(fc + 1) * 128], at[:, kc, th * 512:(th + 1) * 512], start=(kc == 0), stop=(kc == 15)),
                              reads=[('at', kc), ('w1', s)], writes=[('ps', bank)])
                    k = 0
                    ri += 1
                    S.add('act', lambda e, bank=bank, k=k: e.activation(rl[k][:], ps[bank][:, :512], AF.Relu), reads=[('ps', bank)], writes=[('rl', k)])
                    S.add('dve', lambda e, bank=bank, k=k, s=s, fc=fc, th=th: e.tensor_tensor(uT[s][:, fc, th * 512:(th + 1) * 512], ps[bank][:, :512], rl[k][:], ALU.mult),
                          reads=[('ps', bank), ('rl', k)], writes=[('uT', s, fc, th)])
            for ti in range(8):
                for dc in range(4):
                    bank = rr % 4
                    rr += 1
                    for fc in range(4):
                        S.add('pe', lambda e, bank=bank, s=s, fc=fc, ti=ti, dc=dc: e.matmul(ps[bank][:, :512], uT[s][:, fc, ti * 128:(ti + 1) * 128], w2[s][:, fc, dc * 512:(dc + 1) * 512], start=(fc == 0), stop=(fc == 3)),
                              reads=[('uT', s, fc, ti // 4), ('w2', s)], writes=[('ps', bank)])
                    S.add('dve', lambda e, bank=bank, ti=ti, dc=dc: e.tensor_tensor(acc[:, ti, dc * 512:(dc + 1) * 512], ps[bank][:, :512], acc[:, ti, dc * 512:(dc + 1) * 512], ALU.add),
                          reads=[('ps', bank), ('acc', ti, dc)], writes=[('acc', ti, dc)])
        for ti in range(8):
            r0 = tb * TB + ti * 128
            S.dma(x_io[r0:r0 + 128, :], acc[:, ti, :], reads=[('acc', ti, dc) for dc in range(4)])
    A.pop()
    S.barrier()


LAG = 2


class Deferred:
    def __init__(self):
        self.q = []
        self.u = 0

    def push(self, fn):
        self.q.append((self.u, fn))

    def tick(self):
        self.u += 1
        while self.q and self.u - self.q[0][0] >= LAG:
            self.q.pop(0)[1]()

    def flush(self):
        while self.q:
            self.q.pop(0)[1]()


def make_qkv_epi(C, specs, qkT, vdst, wq, wk):
    S, A, ps, psb = C.S, C.A, C.ps, C.psb
    sq = [A.tile([128, 512], F32, 'esq') for _ in range(3)]
    tn = [A.tile([128, 512], F32, 'etn') for _ in range(3)]
    qn = [A.tile([128, 512], BF16, 'eqn') for _ in range(4)]
    sm = A.tile([128, 24], F32, 'esm')
    qst = [A.tile([128, 4, TB], BF16, 'eqst') for _ in range(2)]
    vst = [A.tile([128, 8, 512], BF16, 'evst') for _ in range(2)]
    st = {'k': 0, 'blk': 0}
    dq = Deferred()
    qkTv = qkT.rearrange("m d t -> d m t")

    def on_block(tb, bi):
        st['slot'] = st['blk'] % 2
        st['blk'] += 1

    def epi(tb, bi, ti, bank, nsz):
        kind, a0 = specs[bi]
        slot = st['slot']
        if kind == 'v':
            S.add('act', lambda e: e.copy(vst[slot][:, ti, :], ps[bank][:, :512]), reads=[('ps', bank)], writes=[('vst', slot, ti)])
            dq.tick()
            return
        wn = wq if kind == 'q' else wk
        cnt = st['k']
        st['k'] += 1
        k = cnt % 3
        kq = cnt % 4
        S.add('act', lambda e: e.activation(sq[k][:], ps[bank][:, :512], AF.Square), reads=[('ps', bank)], writes=[('sq', k)])
        ssa = sm[:, k * 8:k * 8 + 4]
        rsa = sm[:, k * 8 + 4:k * 8 + 8]
        S.add('dve', lambda e: e.tensor_reduce(ssa, sq[k][:].rearrange("p (g d) -> p g d", g=4), AX.X, ALU.add), reads=[('sq', k)], writes=[('ss', k)])
        rstd_ops(C, ssa, rsa, 128, [('ss', k)], ('e', k))
        S.add('dve', lambda e: e.tensor_tensor(tn[k][:].rearrange("p (g d) -> p g d", g=4), ps[bank][:, :512].rearrange("p (g d) -> p g d", g=4), bc(rsa, [128, 4, 128], 2), ALU.mult),
              reads=[('ps', bank), ('rs', ('e', k))], writes=[('tn', k)])
        S.add('pool', lambda e: e.tensor_tensor(qn[kq][:].rearrange("p (g d) -> p g d", g=4), tn[k][:].rearrange("p (g d) -> p g d", g=4), bc(wn[:], [128, 4, 128], 1), ALU.mult),
              reads=[('tn', k)], writes=[('qn', kq)])

        def tail():
            tbk = 6 + cnt % 2
            for c in range(4):
                S.add('pe', lambda e, c=c: e.transpose(psb[tbk][:, c * 128:(c + 1) * 128], qn[kq][:, c * 128:(c + 1) * 128], C.ident[:]), reads=[('qn', kq)], writes=[('ps', tbk)])
            dst = qst[slot][:, :, ti * 128:(ti + 1) * 128]
            srcv = psb[tbk][:, 0:512].rearrange("p (c t) -> p c t", c=4)
            if cnt % 2 == 0:
                S.add('act', lambda e: e.copy(dst, srcv), reads=[('ps', tbk)], writes=[('qst', slot, ti)])
            else:
                S.add('dve', lambda e: e.tensor_copy(dst, srcv), reads=[('ps', tbk)], writes=[('qst', slot, ti)])

        dq.push(tail)
        dq.tick()

    def end_block(tb, bi):
        kind, a0 = specs[bi]
        slot = st['slot']
        if kind == 'v':
            dst = vdst[tb * TB:(tb + 1) * TB, a0:a0 + 512].rearrange("(t p) n -> p t n", p=128)
            S.dma(dst, vst[slot][:], reads=[('vst', slot, ti) for ti in range(8)])
        else:
            dq.push(lambda: S.dma(qkTv[:, a0:a0 + 4, tb * TB:(tb + 1) * TB], qst[slot][:], reads=[('qst', slot, ti) for ti in range(8)]))

    return epi, on_block, end_block, dq.flush


def load_bcast(C, ap1d, n, name):
    t = C.A.tile([128, n], F32, name)
    C.S.dma(t[:], ap1d.partition_broadcast(128), writes=[name])
    return t


def layer_diff(C, I, hT, x_src, x_dst, layer_idx):
    nc, S, A, ps, psb = C.nc, C.S, C.A, C.ps, C.psb
    lam_init = 0.8 - 0.6 * math.exp(-0.3 * layer_idx)
    qkT = nc.dram_tensor("a_qkT", [32, 128, SL], BF16).ap()
    vv = nc.dram_tensor("a_v", [SL, D], BF16).ap()
    oT = nc.dram_tensor("a_oT", [D, SL], BF16).ap()
    A.push()
    wq = load_bcast(C, I['a_q_norm_w'], 128, 'awq')
    wk = load_bcast(C, I['a_k_norm_w'], 128, 'awk')
    specs = [('q', 4 * i) for i in range(4)] + [('k', 16 + 4 * i) for i in range(4)] + [('v', 512 * i) for i in range(4)]
    epi, on_block, end_block, fin = make_qkv_epi(C, specs, qkT, vv, wq, wk)
    phase_linear(C, hT, C.Wb['a_w_in'], D, [(i * 512, 512) for i in range(12)], epi, on_block=on_block, end_block=end_block, finish=fin)
    A.pop()
    A.push()
    lt = A.tile([128, 4, 128], F32, 'alt')
    for i, nm in enumerate(['a_lambda_q1', 'a_lambda_k1', 'a_lambda_q2', 'a_lambda_k2']):
        S.dma(lt[:, i, :], I[nm].partition_broadcast(128), writes=[('lt', i)])
    lsm = A.tile([128, 8], F32, 'alsm')
    ljunk = A.tile([128, 128], F32, 'alj')
    for i in range(2):
        S.add('dve', lambda e, i=i: e.tensor_tensor(ljunk[:], lt[:, 2 * i, :], lt[:, 2 * i + 1, :], ALU.mult), reads=[('lt', 2 * i), ('lt', 2 * i + 1)], writes=['lj'])
        S.add('dve', lambda e, i=i: e.tensor_reduce(lsm[:, i:i + 1], ljunk[:], AX.X, ALU.add), reads=['lj'], writes=[('ld', i)])
        S.add('act', lambda e, i=i: e.activation(lsm[:, 2 + i:3 + i], lsm[:, i:i + 1], AF.Exp), reads=[('ld', i)], writes=[('le', i)])
    S.add('dve', lambda e: e.tensor_tensor(lsm[:, 4:5], lsm[:, 3:4], lsm[:, 2:3], ALU.subtract), reads=[('le', 0), ('le', 1)], writes=['l4'])
    S.add('dve', lambda e: e.tensor_scalar(lsm[:, 5:6], lsm[:, 4:5], -lam_init, None, ALU.add), reads=['l4'], writes=['neglam'])
    neglam = lsm[:, 5:6]
    wsub = load_bcast(C, I['a_subln_w'], 256, 'awsub')
    S.add('dve', lambda e: e.tensor_scalar(wsub[:], wsub[:], 1.0 - lam_init, None, ALU.mult), reads=['awsub'], writes=['awsub'])
    qT = [A.tile([128, 2, SL], BF16, 'aqT') for _ in range(2)]
    kT = [A.tile([128, 2, SL], BF16, 'akT') for _ in range(2)]
    Va = [A.tile([128, 32, 257], BF16, 'aVa') for _ in range(2)]
    E = [A.tile([128, 512], BF16, 'aE') for _ in range(5)]
    on0 = [A.tile([128, 256], F32, 'aon0') for _ in range(4)]
    dif = [A.tile([128, 256], F32, 'adif') for _ in range(2)]
    ob = [A.tile([128, 256], BF16, 'aob') for _ in range(2)]
    junk = A.tile([128, 256], BF16, 'ajunk')
    sm = A.tile([128, 16], F32, 'asm')
    ost = [A.tile([128, 2, 512], BF16, 'aost') for _ in range(2)]
    qkTv = qkT.rearrange("m d t -> d m t")
    oTv = oT.rearrange("(c p) t -> p c t", p=128)
    scale = 128 ** -0.5
    for s in range(2):
        S.add('pool', lambda e, s=s: e.memset(Va[s][:, :, 256:257], 1.0), writes=[('Vone', s)])
    ei = 0
    sti = 0
    di = 0
    bi_ = 0
    for hp in range(8):
        s = hp % 2
        S.dma(qT[s][:], qkTv[:, 2 * hp:2 * hp + 2, :], writes=[('qT', s)])
        S.dma(kT[s][:], qkTv[:, 16 + 2 * hp:16 + 2 * hp + 2, :], writes=[('kT', s)])
        S.dma(Va[s][:, :, 0:256], vv[:, hp * 256:(hp + 1) * 256].rearrange("(t p) n -> p t n", p=128), writes=[('V', s)])
        for qb in range(8):
            oslot = bi_ % 2
            bi_ += 1
            for m in range(2):
                nkt = 4 * (qb + 1)
                pend = {}

                def emit_st(kt, s=s, m=m, qb=qb):
                    nonlocal sti, ei
                    sb = 4 + (sti % 3)
                    sti += 1
                    S.add('pe', lambda e, sb=sb, kt=kt, s=s, m=m, qb=qb: e.matmul(ps[sb][:, :512], kT[s][:, m, kt * 128:(kt + 1) * 128], qT[s][:, m, qb * 512:(qb + 1) * 512], start=True, stop=True),
                          reads=[('kT', s), ('qT', s)], writes=[('ps', sb)])
                    pend[kt] = sb

                emit_st(0)
                if nkt > 1:
                    emit_st(1)
                for kt in range(nkt):
                    if kt + 2 < nkt:
                        emit_st(kt + 2)
                    sb = pend.pop(kt)
                    k = ei % 5
                    ei += 1
                    S.add('act', lambda e, sb=sb, k=k: e.activation(E[k][:], ps[sb][:, :512], AF.Exp, scale=scale), reads=[('ps', sb)], writes=[('E', k)])
                    jd = kt - 4 * qb
                    if jd >= 0:
                        S.add('pool', lambda e, k=k, jd=jd: e.tensor_tensor(E[k][:, jd * 128:(jd + 1) * 128], E[k][:, jd * 128:(jd + 1) * 128], C.mle[:], ALU.mult), reads=[('E', k)], writes=[('E', k)])
                    for j in range(4):
                        if jd > j:
                            continue
                        last_kt = 4 * qb + j
                        S.add('pe', lambda e, j=j, k=k, s=s, kt=kt, last_kt=last_kt: e.matmul(ps[j][:, :257], E[k][:, j * 128:(j + 1) * 128], Va[s][:, kt, :], start=(kt == 0), stop=(kt == last_kt)),
                              reads=[('E', k), ('V', s), ('Vone', s)], writes=[('ps', j)])
                for j in range(4):
                    rl_ = sm[:, j:j + 1]
                    S.add('dve', lambda e, j=j, rl_=rl_: e.reciprocal(rl_, ps[j][:, 256:257]), reads=[('ps', j)], writes=[('rl', j)])
                    if m == 0:
                        S.add('dve', lambda e, j=j, rl_=rl_: e.tensor_scalar(on0[j][:], ps[j][:, 0:256], rl_, None, ALU.mult), reads=[('ps', j), ('rl', j)], writes=[('on0', j)])
                    else:
                        d = di % 2
                        di += 1
                        rl2 = sm[:, 4 + j:5 + j]
                        S.add('dve', lambda e, rl2=rl2, rl_=rl_: e.tensor_tensor(rl2, rl_, neglam, ALU.mult), reads=[('rl', j), 'neglam'], writes=[('rl2', j)])
                        S.add('dve', lambda e, j=j, d=d, rl2=rl2: e.scalar_tensor_tensor(dif[d][:], ps[j][:, 0:256], rl2, on0[j][:], ALU.mult, ALU.add),
                              reads=[('ps', j), ('rl2', j), ('on0', j)], writes=[('dif', d)])
                        ssd = sm[:, 8 + d:9 + d]
                        rsd = sm[:, 10 + d:11 + d]
                        S.add('act', lambda e, d=d, ssd=ssd: e.activation(junk[:], dif[d][:], AF.Square, accum_out=ssd), reads=[('dif', d)], writes=[('ssd', d)])
                        rstd_ops(C, ssd, rsd, 256, [('ssd', d)], ('a', d))
                        S.add('dve', lambda e, d=d, rsd=rsd: e.scalar_tensor_tensor(ob[d][:], dif[d][:], rsd, wsub[:], ALU.mult, ALU.mult), reads=[('dif', d), ('rs', ('a', d)), 'awsub'], writes=[('ob', d)])
                        for c in range(2):
                            S.add('pe', lambda e, d=d, c=c: e.transpose(psb[7][:, c * 128:(c + 1) * 128], ob[d][:, c * 128:(c + 1) * 128], C.ident[:]), reads=[('ob', d)], writes=[('ps', 7)])
                        S.add('act', lambda e, oslot=oslot, j=j: e.copy(ost[oslot][:, :, j * 128:(j + 1) * 128], psb[7][:, 0:256].rearrange("p (c t) -> p c t", c=2)),
                              reads=[('ps', 7)], writes=[('ost', oslot, j)])
            S.dma(oTv[:, 2 * hp:2 * hp + 2, qb * 512:(qb + 1) * 512], ost[oslot][:], reads=[('ost', oslot, j) for j in range(4)])
    A.pop()
    S.barrier()
    phase_outproj(C, oT, C.Wb['a_w_out'], D, x_src, x_dst)


WSHAPES = {
    'a_w_in': (D, 3 * D), 'a_w_out': (D, D),
    'b_w_in': (D, 12288), 'b_w_out': (4096, D),
    'c_w_in': (D, 10304), 'c_w_out': (4096, D),
    'd_w_in': (D, 18432), 'd_w_out': (D, D),
}
SMALL = {
    'a_q_norm_w': [128], 'a_k_norm_w': [128], 'a_lambda_q1': [128], 'a_lambda_k1': [128], 'a_lambda_q2': [128],
    'a_lambda_k2': [128], 'a_subln_w': [256], 'b_gn_w': [4096], 'c_conv_wT': [6144, 4], 'c_conv_bT': [128, 48],
    'c_dt_bias': [64], 'c_a_log': [64], 'c_d_skip': [64], 'c_norm_w': [4096], 'd_q_norm_w': [3, 128], 'd_k_norm_w': [3, 128],
    'r_cos': [SL, 256], 'r_ssin': [SL, 256], 'r_qs': [128, 8], 'r_ks': [128, 8],
}
LAYER_W = {0: ['a_w_in', 'a_w_out'], 1: ['b_w_in', 'b_w_out'], 2: ['c_w_in', 'c_w_out'], 3: ['d_w_in', 'd_w_out']}
LAYER_S = {0: ['a_q_norm_w', 'a_k_norm_w', 'a_lambda_q1', 'a_lambda_k1', 'a_lambda_q2', 'a_lambda_k2', 'a_subln_w'],
           1: ['b_gn_w', 'r_cos', 'r_ssin', 'r_qs', 'r_ks'],
           2: ['c_conv_wT', 'c_conv_bT', 'c_dt_bias', 'c_a_log', 'c_d_skip', 'c_norm_w'],
           3: ['d_q_norm_w', 'd_k_norm_w']}


def build(layers=(0, 1, 2, 3), mlp=True, dbg=False):
    nc = bass.Bass("TRN2", target_bir_lowering=False)
    C = Ctx()
    C.dbg = dbg
    C.nc = nc
    C.S = Sched(nc)
    C.A = SbufAlloc(nc)
    I = {}

    def inp(name, shape):
        I[name] = nc.dram_tensor(name, list(shape), F32, kind="ExternalInput").ap()

    inp('x', [SL, D])
    inp('norm1_w', [4, D])
    inp('norm2_w', [4, D])
    y = nc.dram_tensor("y", [SL, D], F32, kind="ExternalOutput").ap()
    C.Wb = {}
    C.Wn = {}
    pairs = []
    for li in layers:
        for nm in LAYER_W[li]:
            inp(nm, WSHAPES[nm])
            dst = nc.dram_tensor(nm + '_bf', list(WSHAPES[nm]), BF16).ap()
            C.Wb[nm] = dst
            C.Wn[id(dst)] = nm
            pairs.append((I[nm], dst))
        for nm in LAYER_S[li]:
            inp(nm, SMALL[nm])
        if mlp:
            inp(f'mlp_w1_{li}', [D, 4 * D])
            inp(f'mlp_w2_{li}', [4 * D, D])
            for nm, shp in [(f'mlp_w1_{li}', [D, 4 * D]), (f'mlp_w2_{li}', [4 * D, D])]:
                dst = nc.dram_tensor(nm + '_bf', shp, BF16).ap()
                C.Wb[nm] = dst
                C.Wn[id(dst)] = nm
                pairs.append((I[nm], dst))
    setup_consts(C)
    C.bg = BgConv(C)
    for src, dst in pairs:
        C.bg.add(C.Wn[id(dst)], src, dst)
    hT = nc.dram_tensor("hT", [D, SL], BF16).ap()
    x_src = I['x']
    for li in layers:
        phase_norm(C, x_src, I['norm1_w'][li], hT)
        [layer_diff, layer_ret, layer_ssd, layer_dil][li](C, I, hT, x_src, y, li)
        x_src = y
        if mlp:
            phase_norm(C, y, I['norm2_w'][li], hT)
            phase_mlp(C, hT, C.Wb[f'mlp_w1_{li}'], C.Wb[f'mlp_w2_{li}'], y)
    C.S.emit()
    C.I = I
    return nc, C


def host_consts():
    out = {}
    theta = np.repeat(1.0 / (10000.0 ** np.linspace(0.0, 1.0, 128, dtype=np.float32)), 2).astype(np.float32)
    ang = (np.arange(SL, dtype=np.float32)[:, None] * theta[None, :]).astype(np.float32)
    cos = np.cos(ang.astype(np.float64)).astype(np.float32)
    sin = np.sin(ang.astype(np.float64)).astype(np.float32)
    ssin = sin.copy()
    ssin[:, 0::2] *= -1.0
    out['r_cos'] = cos
    out['r_ssin'] = ssin
    gam = 1.0 - 2.0 ** (-5.0 - np.arange(8, dtype=np.float64))
    i1 = np.arange(128, dtype=np.float64)[:, None] + 1.0
    out['r_qs'] = (gam[None, :] ** i1).astype(np.float32)
    out['r_ks'] = ((gam[None, :] ** (-i1)) * (256 ** -0.5)).astype(np.float32)
    return out


def make_in_map(inputs, b, layers=(0, 1, 2, 3), mlp=True):
    hc = host_consts()
    m = {'x': np.ascontiguousarray(inputs['x'][b]), 'norm1_w': np.asarray(inputs['norm1_w']), 'norm2_w': np.asarray(inputs['norm2_w'])}
    for li in layers:
        for nm in LAYER_W[li]:
            m[nm] = np.asarray(inputs[nm])
        for nm in LAYER_S[li]:
            if nm in hc:
                m[nm] = hc[nm]
            elif nm == 'c_conv_wT':
                m[nm] = np.ascontiguousarray(np.asarray(inputs['c_conv_w']).T)
            elif nm == 'c_conv_bT':
                m[nm] = np.ascontiguousarray(np.asarray(inputs['c_conv_b']).reshape(48, 128).T)
            else:
                m[nm] = np.asarray(inputs[nm])
        if mlp:
            m[f'mlp_w1_{li}'] = np.ascontiguousarray(inputs['mlp_w1'][li])
            m[f'mlp_w2_{li}'] = np.ascontiguousarray(inputs['mlp_w2'][li])
    return m


def layer_ret(C, I, hT, x_src, x_dst, layer_idx):
    nc, S, A, ps, psb = C.nc, C.S, C.A, C.ps, C.psb
    qT = nc.dram_tensor("b_qT", [D, SL], BF16).ap()
    kT = nc.dram_tensor("b_kT", [D, SL], BF16).ap()
    ktm = nc.dram_tensor("b_ktm", [SL, D], BF16).ap()
    vtm = nc.dram_tensor("b_vtm", [SL, 4096], BF16).ap()
    sgt = nc.dram_tensor("b_sgt", [SL, 4096], BF16).ap()
    oT = nc.dram_tensor("b_oT", [4096, SL], BF16).ap()
    qTv = qT.rearrange("(c p) t -> p c t", p=128)
    kTv = kT.rearrange("(c p) t -> p c t", p=128)
    oTv = oT.rearrange("(c p) t -> p c t", p=128)
    A.push()
    qs = A.tile([128, 8], F32, 'bqs')
    ks = A.tile([128, 8], F32, 'bks')
    S.dma(qs[:], I['r_qs'], writes=['bqs'])
    S.dma(ks[:], I['r_ks'], writes=['bks'])
    cst = A.tile([128, 8, 256], F32, 'bcos')
    sst = A.tile([128, 8, 256], F32, 'bsin')
    t1 = [A.tile([128, 512], F32, 'bt1') for _ in range(3)]
    t2 = [A.tile([128, 512], F32, 'bt2') for _ in range(3)]
    qn = [A.tile([128, 512], BF16, 'bqn') for _ in range(4)]
    dq = Deferred()
    qst = [A.tile([128, 4, TB], BF16, 'bqst') for _ in range(2)]
    vst = [A.tile([128, 8, 512], BF16, 'bvst') for _ in range(2)]
    st = {'k': 0, 'blk': 0}

    def on_tb(tb):
        S.dma(cst[:], I['r_cos'][tb * TB:(tb + 1) * TB, :].rearrange("(t p) n -> p t n", p=128), writes=['bcos'])
        S.dma(sst[:], I['r_ssin'][tb * TB:(tb + 1) * TB, :].rearrange("(t p) n -> p t n", p=128), writes=['bsin'])

    def on_block(tb, bi):
        st['slot'] = st['blk'] % 2
        st['blk'] += 1

    def epi(tb, bi, ti, bank, nsz):
        slot = st['slot']
        if bi >= 8:
            if bi < 16:
                S.add('act', lambda e: e.copy(vst[slot][:, ti, :], ps[bank][:, :512]), reads=[('ps', bank)], writes=[('vst', slot, ti)])
            else:
                S.add('act', lambda e: e.activation(vst[slot][:, ti, :], ps[bank][:, :512], AF.Silu), reads=[('ps', bank)], writes=[('vst', slot, ti)])
            dq.tick()
            return
        isq = bi < 4
        sc = qs if isq else ks
        scn = 'bqs' if isq else 'bks'
        cnt = st['k']
        st['k'] += 1
        k = cnt % 3
        kq = cnt % 4
        pv3 = ps[bank][:, :512].rearrange("p (h d) -> p h d", h=2)
        pv4 = ps[bank][:, :512].rearrange("p (h i two) -> p h i two", h=2, two=2)
        t14 = t2[k][:].rearrange("p (h i two) -> p h i two", h=2, two=2)
        cosb = bc(cst[:, ti, :], [128, 2, 256], 1)
        sn3 = sst[:, ti, :].rearrange("p (i two) -> p i two", two=2)
        S.add('dve', lambda e: e.tensor_tensor(t1[k][:].rearrange("p (h d) -> p h d", h=2), pv3, cosb, ALU.mult), reads=[('ps', bank), 'bcos'], writes=[('t1', k)])
        S.add('dve', lambda e: e.tensor_tensor(t14[:, :, :, 0], pv4[:, :, :, 1], bc(sn3[:, :, 0], [128, 2, 128], 1), ALU.mult), reads=[('ps', bank), 'bsin'], writes=[('t2a', k)])
        S.add('dve', lambda e: e.tensor_tensor(t14[:, :, :, 1], pv4[:, :, :, 0], bc(sn3[:, :, 1], [128, 2, 128], 1), ALU.mult), reads=[('ps', bank), 'bsin'], writes=[('t2b', k)])
        S.add('pool', lambda e: e.tensor_tensor(t1[k][:], t1[k][:], t2[k][:], ALU.add), reads=[('t1', k), ('t2a', k), ('t2b', k)], writes=[('t1', k)])
        for hh in range(2):
            h = (bi % 4) * 2 + hh
            S.add('act', lambda e, hh=hh, h=h: e.mul(qn[kq][:, hh * 256:(hh + 1) * 256], t1[k][:, hh * 256:(hh + 1) * 256], sc[:, h:h + 1]), reads=[('t1', k), scn], writes=[('qn', kq, hh)])
        if not isq:
            S.add('pool', lambda e: e.tensor_copy(vst[slot][:, ti, :], qn[kq][:]), reads=[('qn', kq, 0), ('qn', kq, 1)], writes=[('vst', slot, ti)])

        def tail():
            tbk = 6 + cnt % 2
            for c in range(4):
                S.add('pe', lambda e, c=c: e.transpose(psb[tbk][:, c * 128:(c + 1) * 128], qn[kq][:, c * 128:(c + 1) * 128], C.ident[:]), reads=[('qn', kq, c // 2)], writes=[('ps', tbk)])
            dst = qst[slot][:, :, ti * 128:(ti + 1) * 128]
            srcv = psb[tbk][:, 0:512].rearrange("p (c t) -> p c t", c=4)
            S.add('dve', lambda e: e.tensor_copy(dst, srcv), reads=[('ps', tbk)], writes=[('qst', slot, ti)])

        dq.push(tail)
        dq.tick()

    def end_block(tb, bi):
        slot = st['slot']
        rows = slice(tb * TB, (tb + 1) * TB)
        if bi < 8:
            dstT = (qTv if bi < 4 else kTv)[:, (bi % 4) * 4:(bi % 4) * 4 + 4, rows]
            dq.push(lambda: S.dma(dstT, qst[slot][:], reads=[('qst', slot, ti) for ti in range(8)]))
        if bi >= 4:
            if bi < 8:
                dst = ktm[rows, (bi - 4) * 512:(bi - 3) * 512]
            elif bi < 16:
                dst = vtm[rows, (bi - 8) * 512:(bi - 7) * 512]
            else:
                dst = sgt[rows, (bi - 16) * 512:(bi - 15) * 512]
            S.dma(dst.rearrange("(t p) n -> p t n", p=128), vst[slot][:], reads=[('vst', slot, ti) for ti in range(8)])

    phase_linear(C, hT, C.Wb['b_w_in'], D, [(i * 512, 512) for i in range(24)], epi, on_tb=on_tb, on_block=on_block, end_block=end_block, finish=dq.flush)
    A.pop()
    A.push()
    gnw = load_bcast(C, I['b_gn_w'], 4096, 'bgnw')
    qTc = [A.tile([128, 16, 128], BF16, 'bqTc') for _ in range(2)]
    kTc = [A.tile([128, 16, 128], BF16, 'bkTc') for _ in range(2)]
    kc_ = [A.tile([128, D], BF16, 'bkc') for _ in range(2)]
    vc_ = [A.tile([128, 4096], BF16, 'bvc') for _ in range(2)]
    gc_ = [A.tile([128, 4096], BF16, 'bgc') for _ in range(2)]
    R = A.tile([128, 16, 512], F32, 'bR')
    Rb = A.tile([128, 16, 512], BF16, 'bRb')
    SM = [A.tile([128, 512], BF16, 'bSM') for _ in range(2)]
    tmp = [A.tile([128, 512], F32, 'btmp') for _ in range(2)]
    obf = [A.tile([128, 512], BF16, 'bobf') for _ in range(4)]
    rdq = Deferred()
    junk = A.tile([128, 512], BF16, 'bjunk')
    sm = A.tile([128, 8], F32, 'bsm')
    ost = [A.tile([128, 32, 256], BF16, 'bost') for _ in range(2)]
    ki = 0
    k2i = 0
    def loads(c):
        s = c % 2
        cols = slice(c * 128, (c + 1) * 128)
        S.dma(qTc[s][:], qTv[:, :, cols], writes=[('qTc', s)])
        S.dma(kTc[s][:], kTv[:, :, cols], writes=[('kTc', s)])
        S.dma(kc_[s][:], ktm[cols, :], writes=[('kc', s)])
        S.dma(vc_[s][:], vtm[cols, :], writes=[('vc', s)])
        S.dma(gc_[s][:], sgt[cols, :], writes=[('gc', s)])

    def emit_ST(c, hg):
        s = c % 2
        b = hg % 2
        for hh in range(4):
            h = hg * 4 + hh
            for dc in range(2):
                S.add('pe', lambda e, b=b, hh=hh, h=h, dc=dc, s=s: e.matmul(ps[b][:, hh * 128:(hh + 1) * 128], kTc[s][:, h * 2 + dc, :], qTc[s][:, h * 2 + dc, :], start=(dc == 0), stop=(dc == 1)),
                      reads=[('kTc', s), ('qTc', s)], writes=[('ps', b)])
        return b

    loads(0)
    pend = emit_ST(0, 0)
    for c in range(32):
        s = c % 2
        oslot = (c // 2) % 2
        for hg in range(2):
            b = pend
            if hg == 0:
                pend = emit_ST(c, 1)
            elif c + 1 < 32:
                loads(c + 1)
                pend = emit_ST(c + 1, 0)
            k = ki % 2
            ki += 1
            S.add('dve', lambda e, b=b, k=k: e.tensor_tensor(SM[k][:], ps[b][:, :512], C.mle4[:].rearrange("p a b -> p (a b)"), ALU.mult), reads=[('ps', b)], writes=[('SM', k)])
            for hh in range(4):
                h = hg * 4 + hh
                gC = (1.0 - 2.0 ** (-5.0 - h)) ** 128
                ob_ = 2 + (h % 2)
                S.add('pe', lambda e, ob_=ob_, k=k, hh=hh, h=h, s=s, c=c: e.matmul(ps[ob_][:, :512], SM[k][:, hh * 128:(hh + 1) * 128], vc_[s][:, h * 512:(h + 1) * 512], start=True, stop=(c == 0)),
                      reads=[('SM', k), ('vc', s)], writes=[('ps', ob_)])
                if c > 0:
                    for dc in range(2):
                        S.add('pe', lambda e, ob_=ob_, h=h, dc=dc, s=s: e.matmul(ps[ob_][:, :512], qTc[s][:, h * 2 + dc, :], Rb[:, h * 2 + dc, :], start=False, stop=(dc == 1)),
                              reads=[('qTc', s), ('Rb', h, dc)], writes=[('ps', ob_)])
                if c < 31:
                    for dc in range(2):
                        kvb = 4 + dc
                        S.add('pe', lambda e, kvb=kvb, h=h, dc=dc, s=s: e.matmul(ps[kvb][:, :512], kc_[s][:, h * 256 + dc * 128:h * 256 + (dc + 1) * 128], vc_[s][:, h * 512:(h + 1) * 512], start=True, stop=True),
                              reads=[('kc', s), ('vc', s)], writes=[('ps', kvb)])
                        Rs = R[:, h * 2 + dc, :]
                        if c == 0:
                            S.add('dve', lambda e, kvb=kvb, Rs=Rs, gC=gC: e.tensor_scalar(Rs, ps[kvb][:, :512], gC, None, ALU.mult), reads=[('ps', kvb)], writes=[('R', h, dc)])
                        else:
                            S.add('act', lambda e, Rs=Rs, gC=gC: e.mul(Rs, Rs, gC), reads=[('R', h, dc)], writes=[('R', h, dc)])
                            S.add('dve', lambda e, kvb=kvb, Rs=Rs, gC=gC: e.scalar_tensor_tensor(Rs, ps[kvb][:, :512], gC, Rs, ALU.mult, ALU.add), reads=[('ps', kvb), ('R', h, dc)], writes=[('R', h, dc)])
                        S.add('act', lambda e, Rs=Rs, h=h, dc=dc: e.copy(Rb[:, h * 2 + dc, :], Rs), reads=[('R', h, dc)], writes=[('Rb', h, dc)])
                k2 = k2i % 2
                k2i += 1
                ssa = sm[:, k2:k2 + 1]
                rsa = sm[:, 2 + k2:3 + k2]
                S.add('act', lambda e, ob_=ob_, ssa=ssa: e.activation(junk[:], ps[ob_][:, :512], AF.Square, accum_out=ssa), reads=[('ps', ob_)], writes=[('ssb', k2)])
                rstd_ops(C, ssa, rsa, 512, [('ssb', k2)], ('b', k2))
                S.add('dve', lambda e, ob_=ob_, k2=k2, rsa=rsa, h=h: e.scalar_tensor_tensor(tmp[k2][:], ps[ob_][:, :512], rsa, gnw[:, h * 512:(h + 1) * 512], ALU.mult, ALU.mult),
                      reads=[('ps', ob_), ('rs', ('b', k2)), 'bgnw'], writes=[('tmp', k2)])
                ko = (k2i - 1) % 4
                S.add('dve', lambda e, k2=k2, ko=ko, h=h, s=s: e.tensor_tensor(obf[ko][:], tmp[k2][:], gc_[s][:, h * 512:(h + 1) * 512], ALU.mult), reads=[('tmp', k2), ('gc', s)], writes=[('obf', ko)])

                def tail(ko=ko, h=h, c=c, oslot=oslot, par=(k2i - 1) % 2):
                    tbk = 6 + par
                    for cc in range(4):
                        S.add('pe', lambda e, tbk=tbk, cc=cc, ko=ko: e.transpose(psb[tbk][:, cc * 128:(cc + 1) * 128], obf[ko][:, cc * 128:(cc + 1) * 128], C.ident[:]), reads=[('obf', ko)], writes=[('ps', tbk)])
                    dst = ost[oslot][:, h * 4:(h + 1) * 4, (c % 2) * 128:(c % 2 + 1) * 128]
                    srcv = psb[tbk][:, 0:512].rearrange("p (c t) -> p c t", c=4)
                    if par == 0:
                        S.add('act', lambda e, dst=dst, srcv=srcv: e.copy(dst, srcv), reads=[('ps', tbk)], writes=[('ost', oslot, c % 2, h)])
                    else:
                        S.add('dve', lambda e, dst=dst, srcv=srcv: e.tensor_copy(dst, srcv), reads=[('ps', tbk)], writes=[('ost', oslot, c % 2, h)])

                rdq.push(tail)
                rdq.tick()
        if c % 2 == 1:
            rdq.push(lambda c=c, oslot=oslot: S.dma(oTv[:, :, (c - 1) * 128:(c + 1) * 128], ost[oslot][:], reads=[('ost', oslot, cc, h) for cc in range(2) for h in range(8)]))
    rdq.flush()
    A.pop()
    S.barrier()
    phase_outproj(C, oT, C.Wb['b_w_out'], 4096, x_src, x_dst)


def emit_T16(C, hsrc, rtoks, stt, j, toks, tag):
    S, psb = C.S, C.psb
    for half in range(2):
        bank = 6 + half
        for c in range(8):
            cc = half * 8 + c
            S.add('pe', lambda e, c=c, cc=cc, bank=bank: e.transpose(psb[bank][:, c * 128:(c + 1) * 128], hsrc[:, cc * 128:(cc + 1) * 128], C.ident[:]),
                  reads=rtoks, writes=[('ps', bank)])
        dst = stt[:, half * 8:(half + 1) * 8, j * 128:(j + 1) * 128]
        srcv = psb[bank][:, 0:1024].rearrange("p (c t) -> p c t", c=8)
        tk = (tag, j, half)
        toks.append(tk)
        if half == 0:
            S.add('act', lambda e, dst=dst, srcv=srcv: e.copy(dst, srcv), reads=[('ps', bank)], writes=[tk])
        else:
            S.add('dve', lambda e, dst=dst, srcv=srcv: e.tensor_copy(dst, srcv), reads=[('ps', bank)], writes=[tk])


def layer_dil(C, I, hT, x_src, x_dst, layer_idx):
    nc, S, A, ps, psb = C.nc, C.S, C.A, C.ps, C.psb
    qkT = nc.dram_tensor("d_qkT", [3, 32, 128, SL], BF16).ap()
    vv = nc.dram_tensor("d_v", [3, SL, D], BF16).ap()
    Og = nc.dram_tensor("d_Og", [3, SL, 16, 129], F32).ap()
    oT = nc.dram_tensor("d_oT", [D, SL], BF16).ap()
    specs = [('q', 4 * i) for i in range(4)] + [('k', 16 + 4 * i) for i in range(4)] + [('v', 512 * i) for i in range(4)]
    for g3 in range(3):
        A.push()
        wq = load_bcast(C, I['d_q_norm_w'][g3], 128, 'dwq')
        wk = load_bcast(C, I['d_k_norm_w'][g3], 128, 'dwk')
        epi, on_block, end_block, fin = make_qkv_epi(C, specs, qkT[g3], vv[g3], wq, wk)
        phase_linear(C, hT, C.Wb['d_w_in'], D, [(g3 * 6144 + i * 512, 512) for i in range(12)], epi, on_block=on_block, end_block=end_block, finish=fin)
        A.pop()
    A.push()
    qTh = [A.tile([128, SL], BF16, 'dqT') for _ in range(2)]
    kTh = [A.tile([128, SL], BF16, 'dkT') for _ in range(2)]
    Vh = [A.tile([128, 32, 129], BF16, 'dVh') for _ in range(2)]
    E = [A.tile([128, 512], BF16, 'dE') for _ in range(3)]
    Ost = [A.tile([128, 32, 129], F32, 'dOst') for _ in range(2)]
    for s in range(2):
        S.add('pool', lambda e, s=s: e.memset(Vh[s][:, :, 128:129], 1.0), writes=[('Vone', s)])
    scale = 128 ** -0.5
    it = 0
    sti = 0
    ei = 0
    oi = 0
    for g3, dil in enumerate([1, 4, 16]):
        nblk = SL // dil // 128
        for h in range(16):
            s = it % 2
            it += 1
            S.dma(qTh[s][:], qkT[g3, h], writes=[('qT', s)])
            S.dma(kTh[s][:], qkT[g3, 16 + h], writes=[('kT', s)])
            vsrc = vv[g3][:, h * 128:(h + 1) * 128].rearrange("(nb a r) d -> a r nb d", a=128, r=dil)
            for r in range(dil):
                S.dma(Vh[s][:, r * nblk:(r + 1) * nblk, 0:128], vsrc[:, r, :, :], writes=[('V', s, r)])
            vtoks = [('V', s, r) for r in range(dil)] + [('Vone', s)]
            qv = qTh[s][:].rearrange("d (nb a r) -> d r nb a", a=128, r=dil)
            kv = kTh[s][:].rearrange("d (nb a r) -> d r nb a", a=128, r=dil)
            units = [(r, nb0) for r in range(dil) for nb0 in range(0, nblk, 2)]

            def emit_st(u, s=s, kv=kv, qv=qv):
                nonlocal sti
                r, nb0 = units[u]
                sb = sti % 3
                sti += 1
                for uu in range(2):
                    nb = nb0 + uu
                    if nb > 0:
                        S.add('pe', lambda e, sb=sb, uu=uu, r=r, nb=nb: e.matmul(ps[sb][:, (2 * uu) * 128:(2 * uu + 1) * 128], kv[:, r, nb - 1, :], qv[:, r, nb, :], start=True, stop=True),
                              reads=[('kT', s), ('qT', s)], writes=[('ps', sb)])
                    S.add('pe', lambda e, sb=sb, uu=uu, r=r, nb=nb: e.matmul(ps[sb][:, (2 * uu + 1) * 128:(2 * uu + 2) * 128], kv[:, r, nb, :], qv[:, r, nb, :], start=True, stop=True),
                          reads=[('kT', s), ('qT', s)], writes=[('ps', sb)])
                return sb

            pend = emit_st(0)
            for ui, (r, nb0) in enumerate(units):
                sb = pend
                if ui + 1 < len(units):
                    pend = emit_st(ui + 1)
                k = ei % 3
                ei += 1
                S.add('act', lambda e, sb=sb, k=k: e.activation(E[k][:], ps[sb][:, :512], AF.Exp, scale=scale), reads=[('ps', sb)], writes=[('E', k)])
                S.add('dve', lambda e, k=k: e.tensor_tensor(E[k][:], E[k][:], C.mask4[:].rearrange("p a b -> p (a b)"), ALU.mult), reads=[('E', k)], writes=[('E', k)])
                ob = 4 + (oi % 2)
                oi += 1
                for u in range(2):
                    nb = nb0 + u
                    tau = r * nblk + nb
                    if nb > 0:
                        S.add('pe', lambda e, ob=ob, u=u, k=k, tau=tau, s=s: e.matmul(ps[ob][:, u * 129:(u + 1) * 129], E[k][:, (2 * u) * 128:(2 * u + 1) * 128], Vh[s][:, tau - 1, :], start=True, stop=False),
                              reads=[('E', k)] + vtoks, writes=[('ps', ob)])
                    S.add('pe', lambda e, ob=ob, u=u, k=k, tau=tau, nb=nb, s=s: e.matmul(ps[ob][:, u * 129:(u + 1) * 129], E[k][:, (2 * u + 1) * 128:(2 * u + 2) * 128], Vh[s][:, tau, :], start=(nb == 0), stop=True),
                          reads=[('E', k)] + vtoks, writes=[('ps', ob)])
                tau0 = r * nblk + nb0
                dst = Ost[s][:, tau0:tau0 + 2, :].rearrange("p a b -> p (a b)")
                if oi % 2 == 0:
                    S.add('act', lambda e, ob=ob, dst=dst: e.copy(dst, ps[ob][:, 0:258]), reads=[('ps', ob)], writes=[('Ost', s, r)])
                else:
                    S.add('dve', lambda e, ob=ob, dst=dst: e.tensor_copy(dst, ps[ob][:, 0:258]), reads=[('ps', ob)], writes=[('Ost', s, r)])
            odst = Og[g3][:, h, :].rearrange("(nb b r) c -> b r nb c", b=128, r=dil)
            for r in range(dil):
                S.dma(odst[:, r, :, :], Ost[s][:, r * nblk:(r + 1) * nblk, :], reads=[('Ost', s, r)])
    A.pop()
    S.barrier()
    A.push()
    Ot = [[A.tile([128, 16, 129], F32, 'dOt') for _ in range(3)] for _ in range(2)]
    rl = A.tile([128, 2, 16], F32, 'drl')
    ob_ = [A.tile([128, D], BF16, 'dob') for _ in range(2)]
    stt = [A.tile([128, 16, 512], BF16, 'dst') for _ in range(2)]
    oTv = oT.rearrange("(c p) t -> p c t", p=128)
    for g in range(8):
        gs = g % 2
        toks = []
        for j in range(4):
            t = g * 4 + j
            s = t % 2
            for g3 in range(3):
                S.dma(Ot[s][g3][:], Og[g3][t * 128:(t + 1) * 128], writes=[('Ot', s, g3)])
            a = Ot[s][0]
            S.add('pool', lambda e, s=s, a=a: e.tensor_tensor(a[:], a[:], Ot[s][1][:], ALU.add), reads=[('Ot', s, 0), ('Ot', s, 1)], writes=[('Ot', s, 0)])
            S.add('dve', lambda e, s=s, a=a: e.tensor_tensor(a[:], a[:], Ot[s][2][:], ALU.add), reads=[('Ot', s, 0), ('Ot', s, 2)], writes=[('Ot', s, 0)])
            S.add('dve', lambda e, s=s, a=a: e.reciprocal(rl[:, s, :], a[:, :, 128]), reads=[('Ot', s, 0)], writes=[('rl', s)])
            S.add('dve', lambda e, s=s, a=a: e.tensor_tensor(ob_[s][:].rearrange("p (h d) -> p h d", h=16), a[:, :, 0:128], bc(rl[:, s, :], [128, 16, 128], 2), ALU.mult),
                  reads=[('Ot', s, 0), ('rl', s)], writes=[('ob', s)])
            emit_T16(C, ob_[s], [('ob', s)], stt[gs], j, toks, ('dst', gs))
        S.dma(oTv[:, :, g * 512:(g + 1) * 512], stt[gs][:], reads=toks)
    A.pop()
    S.barrier()
    phase_outproj(C, oT, C.Wb['d_w_out'], D, x_src, x_dst)


def layer_ssd(C, I, hT, x_src, x_dst, layer_idx):
    nc, S, A, ps, psb = C.nc, C.S, C.A, C.ps, C.psb
    kd = "ExternalOutput" if getattr(C, 'dbg', False) else "Internal"
    szd = nc.dram_tensor("c_sz", [SL, 4096], BF16, kind=kd).ap()
    xstm = nc.dram_tensor("c_xstm", [SL, 4096], BF16, kind=kd).ap()
    Btm = nc.dram_tensor("c_Btm", [SL, 1024], BF16, kind=kd).ap()
    bcT = nc.dram_tensor("c_bcT", [2048, SL], BF16, kind=kd).ap()
    acsC = nc.dram_tensor("c_acsC", [16, 64 * 256], F32, kind=kd).ap()
    dttm = nc.dram_tensor("c_dttm", [SL, 64], F32, kind=kd).ap()
    acstm = nc.dram_tensor("c_acstm", [SL, 64], F32, kind=kd).ap()
    oT = nc.dram_tensor("c_oT", [4096, SL], BF16, kind=kd).ap()
    W = C.Wb['c_w_in']
    A.push()
    vst = [A.tile([128, 8, 512], BF16, 'cvst') for _ in range(2)]
    st = {'blk': 0}

    def on_block(tb, bi):
        st['slot'] = st['blk'] % 2
        st['blk'] += 1

    def epi(tb, bi, ti, bank, nsz):
        slot = st['slot']
        S.add('act', lambda e: e.activation(vst[slot][:, ti, :], ps[bank][:, :512], AF.Silu), reads=[('ps', bank)], writes=[('vst', slot, ti)])

    def end_block(tb, bi):
        slot = st['slot']
        dst = szd[tb * TB:(tb + 1) * TB, bi * 512:(bi + 1) * 512].rearrange("(t p) n -> p t n", p=128)
        S.dma(dst, vst[slot][:], reads=[('vst', slot, ti) for ti in range(8)])

    phase_linear(C, hT, W, D, [(i * 512, 512) for i in range(8)], epi, on_block=on_block, end_block=end_block)
    A.pop()
    A.push()
    dtT = A.tile([64, SL], F32, 'cdtT')
    cw = A.tile([128, 48, 4], F32, 'ccw')
    cb = A.tile([128, 48], F32, 'ccb')
    dtb = A.tile([64, 1], F32, 'cdtb')
    halo = A.tile([128, 48, 3], F32, 'chalo')
    S.dma(cw[:], I['c_conv_wT'].rearrange("(c p) k -> p c k", p=128), writes=['ccw'])
    S.dma(cb[:], I['c_conv_bT'], writes=['ccb'])
    S.dma(dtb[:], I['c_dt_bias'].rearrange("(p o) -> p o", o=1), writes=['cdtb'])
    S.add('pool', lambda e: e.memset(halo[:], 0.0), writes=[('halo', i) for i in range(48)])
    xr = [A.tile([128, 515], F32, 'cxr') for _ in range(2)]
    acc = [A.tile([128, 512], F32, 'cacc') for _ in range(2)]
    so = [A.tile([128, 512], BF16, 'cso') for _ in range(4)]
    dq = Deferred()
    tms = [A.tile([128, 8, 128], BF16, 'ctms') for _ in range(2)]
    e1 = A.tile([64, 512], F32, 'ce1')
    st2 = {'k': 0, 'u': 0}
    bcTv = bcT

    def epi_fm(tb, bi, j, th, bank, msz):
        if bi == 12:
            cols = slice(tb * TB + th * 512, tb * TB + (th + 1) * 512)
            S.add('act', lambda e: e.activation(e1[:], ps[bank][:64, :512], AF.Exp, bias=dtb[:, 0:1]), reads=[('ps', bank), 'cdtb'], writes=['ce1'])
            S.add('act', lambda e: e.activation(dtT[:, cols], e1[:], AF.Ln, bias=1.0), reads=['ce1'], writes=[('dtT', tb, th)])
            return
        cidx = bi * 4 + j
        cnt = st2['k']
        st2['k'] += 1
        k = cnt % 2
        ko = cnt % 4
        S.add('pool', lambda e: e.tensor_copy(xr[k][:, 0:3], halo[:, cidx, :]), reads=[('halo', cidx)], writes=[('xrh', k)])
        S.add('act', lambda e: e.copy(xr[k][:, 3:515], ps[bank][:, :512]), reads=[('ps', bank)], writes=[('xrb', k)])
        S.add('pool', lambda e: e.tensor_copy(halo[:, cidx, :], xr[k][:, 512:515]), reads=[('xrb', k), ('xrh', k)], writes=[('halo', cidx)])
        S.add('act', lambda e: e.mul(acc[k][:], xr[k][:, 0:512], cw[:, cidx, 0:1]), reads=[('xrh', k), ('xrb', k), 'ccw'], writes=[('acc', k)])
        for q in range(1, 4):
            S.add('dve', lambda e, q=q: e.scalar_tensor_tensor(acc[k][:], xr[k][:, q:q + 512], cw[:, cidx, q:q + 1], acc[k][:], ALU.mult, ALU.add),
                  reads=[('xrh', k), ('xrb', k), ('acc', k)], writes=[('acc', k)])
        S.add('act', lambda e: e.activation(so[ko][:], acc[k][:], AF.Silu, bias=cb[:, cidx:cidx + 1]), reads=[('acc', k), 'ccb'], writes=[('so', ko)])
        if cidx >= 32:
            r0 = (cidx - 32) * 128
            S.dma(bcTv[r0:r0 + 128, tb * TB + th * 512:tb * TB + (th + 1) * 512], so[ko][:], reads=[('so', ko)])
        if cidx < 40:
            if th == 0:
                st2['slot'] = st2['u'] % 2
                st2['u'] += 1
            slot = st2['slot']

            def tail():
                tbk = 6 + cnt % 2
                for c in range(4):
                    S.add('pe', lambda e, c=c: e.transpose(psb[tbk][:, c * 128:(c + 1) * 128], so[ko][:, c * 128:(c + 1) * 128], C.ident[:]), reads=[('so', ko)], writes=[('ps', tbk)])
                S.add('dve', lambda e: e.tensor_copy(tms[slot][:, th * 4:(th + 1) * 4, :], psb[tbk][:, 0:512].rearrange("p (c t) -> p c t", c=4)), reads=[('ps', tbk)], writes=[('tms', slot, th)])
                if th == 1:
                    rows = slice(tb * TB, (tb + 1) * TB)
                    if cidx < 32:
                        dst = xstm[rows, cidx * 128:(cidx + 1) * 128]
                    else:
                        dst = Btm[rows, (cidx - 32) * 128:(cidx - 31) * 128]
                    S.dma(dst.rearrange("(t p) n -> p t n", p=128), tms[slot][:], reads=[('tms', slot, 0), ('tms', slot, 1)])

            dq.push(tail)
        dq.tick()

    nblocks = [(4096 + i * 512, 512) for i in range(12)] + [(10240, 64)]
    phase_linear(C, hT, W, D, nblocks, epi_fm, fm=True, finish=dq.flush)
    alog = A.tile([64, 2], F32, 'calog')
    S.dma(alog[:, 0:1], I['c_a_log'].rearrange("(p o) -> p o", o=1), writes=['alog'])
    S.add('act', lambda e: e.activation(alog[:, 1:2], alog[:, 0:1], AF.Exp), reads=['alog'], writes=['aexp'])
    S.add('dve', lambda e: e.tensor_scalar(alog[:, 1:2], alog[:, 1:2], -1.0, None, ALU.mult), reads=['aexp'], writes=['aneg'])
    aT = A.tile([64, SL], F32, 'caT')
    acsT = A.tile([64, SL], F32, 'cacsT')
    onesf = A.tile([64, 256], F32, 'cones')
    S.add('pool', lambda e: e.memset(onesf[:], 1.0), writes=['cones'])
    S.add('dve', lambda e: e.tensor_scalar(aT[:], dtT[:], alog[:, 1:2], None, ALU.mult), reads=['aneg'], writes=['aT'])
    for c in range(16):
        cols = slice(c * 256, (c + 1) * 256)
        S.add('dve', lambda e, cols=cols: e.tensor_tensor_scan(acsT[:, cols], onesf[:], aT[:, cols], 0.0, ALU.mult, ALU.add), reads=['aT', 'cones'], writes=[('acsT', c)])
        S.dma(acsC[c].rearrange("(h i) -> h i", h=64), acsT[:, cols], reads=[('acsT', c)])
    tst = A.tile([128, 2, 32, 64], F32, 'ctst')
    for wi_, srcT in enumerate([dtT, acsT]):
        for t8 in range(4):
            bank = (wi_ * 4 + t8) % 4
            for q in range(8):
                t = t8 * 8 + q
                S.add('pe', lambda e, bank=bank, q=q, t=t, srcT=srcT: e.transpose(ps[bank][:, q * 64:(q + 1) * 64], srcT[:, t * 128:(t + 1) * 128], C.identf[:64, :64]),
                      reads=[('acsT', t // 2)], writes=[('ps', bank)])
            S.add('act', lambda e, bank=bank, wi_=wi_, t8=t8: e.copy(tst[:, wi_, t8 * 8:(t8 + 1) * 8, :], ps[bank][:, :512].rearrange("p (q h) -> p q h", q=8)),
                  reads=[('ps', bank)], writes=[('tst', wi_, t8)])
        dstd = dttm if wi_ == 0 else acstm
        S.dma(dstd.rearrange("(t p) h -> p t h", p=128), tst[:, wi_, :, :], reads=[('tst', wi_, t8) for t8 in range(4)])
    A.pop()
    S.barrier()
    A.push()
    nw = load_bcast(C, I['c_norm_w'], 4096, 'cnw')
    dsk = load_bcast(C, I['c_d_skip'], 64, 'cdsk')
    xs_c = [A.tile([128, 2, 4096], BF16, 'cxs')] * 2
    sz_c = A.tile([128, 2, 4096], BF16, 'csz')
    Bt_c = [A.tile([128, 2, 1024], BF16, 'cBt') for _ in range(2)]
    BT_c = [A.tile([128, 8, 256], BF16, 'cBT') for _ in range(2)]
    CT_c = [A.tile([128, 8, 256], BF16, 'cCT') for _ in range(2)]
    sm_c = [A.tile([128, 2, 2, 64], F32, 'csm') for _ in range(2)]
    alb = [A.tile([128, 64], F32, 'calb') for _ in range(2)]
    ar = [A.tile([128, 2048], F32, 'car') for _ in range(2)]
    eacs = A.tile([128, 2, 64], F32, 'ceacs')
    w2 = A.tile([128, 2, 64], F32, 'cw2')
    eal = A.tile([128, 64], F32, 'ceal')
    stf = A.tile([128, 8, 512], F32, 'cstf')
    stb = A.tile([128, 8, 512], BF16, 'cstb')
    xdt = [A.tile([128, 2, 512], BF16, 'cxdt') for _ in range(2)]
    xde = [A.tile([128, 2, 512], BF16, 'cxde') for _ in range(2)]
    Gsb = [A.tile([128, 384], F32, 'cG') for _ in range(2)]
    tLb = [A.tile([128, 8, 128], F32, 'ctLb') for _ in range(2)]
    M0 = [A.tile([128, 8, 256], BF16, 'cM0') for _ in range(2)]
    M1 = [A.tile([128, 8, 128], BF16, 'cM1') for _ in range(2)]
    yt = [A.tile([128, 512], F32, 'cyt') for _ in range(2)]
    t2 = [A.tile([128, 512], F32, 'ct2') for _ in range(2)]
    ob = [A.tile([128, 512], BF16, 'cob') for _ in range(2)]
    junk = A.tile([128, 512], BF16, 'cjunk')
    smn = A.tile([128, 4], F32, 'csmn')
    ost = A.tile([128, 32, 256], BF16, 'cost')
    BTv = bcT[0:1024, :].rearrange("(g p) t -> p g t", p=128)
    CTv = bcT[1024:2048, :].rearrange("(g p) t -> p g t", p=128)
    oTv = oT.rearrange("(c p) t -> p c t", p=128)
    gi = 0
    li_ = 0
    yi = 0
    for c in range(16):
        s = c % 2
        rows = slice(c * 256, (c + 1) * 256)
        S.dma(xs_c[s][:], xstm[rows, :].rearrange("(t p) n -> p t n", p=128), writes=[('xs', 0)])
        S.dma(sz_c[:], szd[rows, :].rearrange("(t p) n -> p t n", p=128), writes=['sz'])
        S.dma(Bt_c[s][:], Btm[rows, :].rearrange("(t p) n -> p t n", p=128), writes=[('Bt', s)])
        S.dma(BT_c[s][:], BTv[:, :, rows], writes=[('BT', s)])
        S.dma(CT_c[s][:], CTv[:, :, rows], writes=[('CT', s)])
        S.dma(sm_c[s][:, 0, :, :], dttm[rows, :].rearrange("(t p) h -> p t h", p=128), writes=[('dtc', s)])
        S.dma(sm_c[s][:, 1, :, :], acstm[rows, :].rearrange("(t p) h -> p t h", p=128), writes=[('acsc', s)])
        S.dma(alb[s][:], acstm[c * 256 + 255, :].partition_broadcast(128), writes=[('alb', s)])
        dt_c = sm_c[s][:, 0, :, :]
        acs_c = sm_c[s][:, 1, :, :]
        S.add('act', lambda e, acs_c=acs_c: e.activation(eacs[:], acs_c, AF.Exp), reads=[('acsc', s)], writes=['eacs'])
        S.add('dve', lambda e, acs_c=acs_c, s=s: e.tensor_tensor(w2[:], bc(alb[s][:], [128, 2, 64], 1), acs_c, ALU.subtract), reads=[('alb', s), ('acsc', s)], writes=['w2'])
        S.add('act', lambda e: e.activation(w2[:], w2[:], AF.Exp), reads=['w2'], writes=['w2'])
        S.add('dve', lambda e, dt_c=dt_c: e.tensor_tensor(w2[:], w2[:], dt_c, ALU.mult), reads=['w2', ('dtc', s)], writes=['w2'])
        S.add('act', lambda e, s=s: e.activation(eal[:], alb[s][:], AF.Exp), reads=[('alb', s)], writes=['eal'])
        stA = {}

        def stageA(g, c=c, s=s, dt_c=dt_c, acs_c=acs_c):
            nonlocal gi, li_
            sg = gi % 2
            k = gi % 2
            gi += 1
            S.dma(ar[sg][:], acsC[c, g * 2048:(g + 1) * 2048].partition_broadcast(128), writes=[('ar', sg)])
            gc = slice(g * 512, (g + 1) * 512)
            hs_ = slice(g * 8, (g + 1) * 8)
            xs4 = xs_c[s][:, :, gc].rearrange("p t (h q) -> p t h q", h=8)
            S.add('pool', lambda e, k=k, xs4=xs4, dt_c=dt_c, hs_=hs_: e.tensor_tensor(xdt[k][:].rearrange("p t (h q) -> p t h q", h=8), xs4, bc(dt_c[:, :, hs_], [128, 2, 8, 64], 3), ALU.mult),
                  reads=[('xs', 0), ('dtc', s)], writes=[('xdt', k)])
            S.add('pool', lambda e, k=k, xs4=xs4, hs_=hs_: e.tensor_tensor(xde[k][:].rearrange("p t (h q) -> p t h q", h=8), xs4, bc(w2[:, :, hs_], [128, 2, 8, 64], 3), ALU.mult),
                  reads=[('xs', 0), 'w2'], writes=[('xde', k)])
            S.add('pe', lambda e, s=s, g=g: e.matmul(ps[0][:, 0:256], BT_c[s][:, g, 0:128], CT_c[s][:, g, 0:256], start=True, stop=True), reads=[('BT', s), ('CT', s)], writes=[('ps', 0)])
            S.add('pe', lambda e, s=s, g=g: e.matmul(ps[0][:, 256:384], BT_c[s][:, g, 128:256], CT_c[s][:, g, 128:256], start=True, stop=True), reads=[('BT', s), ('CT', s)], writes=[('ps', 0)])
            S.add('dve', lambda e, k=k: e.tensor_tensor(Gsb[k][:], ps[0][:, 0:384], C.gmask[:], ALU.mult), reads=[('ps', 0)], writes=[('G', k)])
            kk = li_ % 2
            li_ += 1
            ar3 = ar[sg][:].rearrange("p (h i) -> p h i", h=8)
            acs0 = bc(acs_c[:, 0, hs_], [128, 8, 256], 2)
            acs1 = bc(acs_c[:, 1, hs_], [128, 8, 128], 2)
            S.add('dve', lambda e, kk=kk, ar3=ar3, acs1=acs1: e.tensor_tensor(tLb[kk][:], ar3[:, :, 128:256], acs1, ALU.subtract), reads=[('ar', sg), ('acsc', s)], writes=[('tLb', kk)])
            S.add('act', lambda e, kk=kk: e.activation(tLb[kk][:], tLb[kk][:], AF.Exp), reads=[('tLb', kk)], writes=[('tLb', kk)])
            S.add('dve', lambda e, kk=kk, k=k: e.scalar_tensor_tensor(M1[kk][:], tLb[kk][:], 1.0, bc(Gsb[k][:, 256:384], [128, 8, 128], 1), ALU.min, ALU.mult), reads=[('tLb', kk), ('G', k)], writes=[('M1', kk)])
            S.add('dve', lambda e, ar3=ar3, acs0=acs0: e.tensor_tensor(ar3, ar3, acs0, ALU.subtract), reads=[('ar', sg), ('acsc', s)], writes=[('ar', sg)])
            S.add('act', lambda e, sg=sg: e.activation(ar[sg][:], ar[sg][:], AF.Exp), reads=[('ar', sg)], writes=[('ar', sg)])
            S.add('dve', lambda e, kk=kk, k=k, ar3=ar3: e.scalar_tensor_tensor(M0[kk][:], ar3, 1.0, bc(Gsb[k][:, 0:256], [128, 8, 256], 1), ALU.min, ALU.mult), reads=[('ar', sg), ('G', k)], writes=[('M0', kk)])
            stA[g] = (k, kk, gc, hs_)

        stageA(0)
        for g in range(8):
            if g + 1 < 8:
                stageA(g + 1)
            k, kk, gc, hs_ = stA[g]
            if c > 0:
                for it in range(2):
                    S.add('pe', lambda e, it=it, s=s, g=g: e.matmul(ps[2 + it][:, :512], CT_c[s][:, g, it * 128:(it + 1) * 128], stb[:, g, :], start=True, stop=True),
                          reads=[('CT', s), ('stb', g)], writes=[('ps', 2 + it)])
            for hh in range(8):
                hq = slice(hh * 64, (hh + 1) * 64)
                S.add('pe', lambda e, kk=kk, k=k, hq=hq, hh=hh: e.matmul(ps[4][:, hq], M0[kk][:, hh, 0:128], xdt[k][:, 0, hq], start=True, stop=True), reads=[('M0', kk), ('xdt', k)], writes=[('ps', 4)])
                S.add('pe', lambda e, kk=kk, k=k, hq=hq, hh=hh: e.matmul(ps[5][:, hq], M0[kk][:, hh, 128:256], xdt[k][:, 0, hq], start=True, stop=False), reads=[('M0', kk), ('xdt', k)], writes=[('ps', 5)])
                S.add('pe', lambda e, kk=kk, k=k, hq=hq, hh=hh: e.matmul(ps[5][:, hq], M1[kk][:, hh, :], xdt[k][:, 1, hq], start=False, stop=True), reads=[('M1', kk), ('xdt', k)], writes=[('ps', 5)])
            for it in range(2):
                y2 = yi % 2
                yi += 1
                if c > 0:
                    S.add('dve', lambda e, it=it, y2=y2, hs_=hs_: e.tensor_tensor(yt[y2][:].rearrange("p (h q) -> p h q", h=8), ps[2 + it][:, :512].rearrange("p (h q) -> p h q", h=8), bc(eacs[:, it, hs_], [128, 8, 64], 2), ALU.mult),
                          reads=[('ps', 2 + it), 'eacs'], writes=[('yt', y2)])
                    S.add('dve', lambda e, it=it, y2=y2: e.tensor_tensor(yt[y2][:], ps[4 + it][:, :512], yt[y2][:], ALU.add), reads=[('ps', 4 + it), ('yt', y2)], writes=[('yt', y2)])
                else:
                    S.add('act', lambda e, it=it, y2=y2: e.copy(yt[y2][:], ps[4 + it][:, :512]), reads=[('ps', 4 + it)], writes=[('yt', y2)])
                S.add('pool', lambda e, it=it, y2=y2, s=s, gc=gc, hs_=hs_: e.tensor_tensor(t2[y2][:].rearrange("p (h q) -> p h q", h=8), xs_c[s][:, it, gc].rearrange("p (h q) -> p h q", h=8), bc(dsk[:, hs_], [128, 8, 64], 2), ALU.mult),
                      reads=[('xs', 0), 'cdsk'], writes=[('t2', y2)])
                S.add('pool', lambda e, y2=y2: e.tensor_tensor(yt[y2][:], yt[y2][:], t2[y2][:], ALU.add), reads=[('yt', y2), ('t2', y2)], writes=[('yt', y2)])
                S.add('dve', lambda e, it=it, y2=y2, gc=gc: e.tensor_tensor(yt[y2][:], yt[y2][:], sz_c[:, it, gc], ALU.mult), reads=[('yt', y2), 'sz'], writes=[('yt', y2)])
                ssa = smn[:, y2:y2 + 1]
                rsa = smn[:, 2 + y2:3 + y2]
                S.add('act', lambda e, y2=y2, ssa=ssa: e.activation(junk[:], yt[y2][:], AF.Square, accum_out=ssa), reads=[('yt', y2)], writes=[('ssc', y2)])
                rstd_ops(C, ssa, rsa, 512, [('ssc', y2)], ('c', y2))
                S.add('dve', lambda e, y2=y2, rsa=rsa, gc=gc: e.scalar_tensor_tensor(ob[y2][:], yt[y2][:], rsa, nw[:, gc], ALU.mult, ALU.mult), reads=[('yt', y2), ('rs', ('c', y2)), 'cnw'], writes=[('ob', y2)])
                tbk = 6 + y2
                for cc in range(4):
                    S.add('pe', lambda e, tbk=tbk, cc=cc, y2=y2: e.transpose(psb[tbk][:, cc * 128:(cc + 1) * 128], ob[y2][:, cc * 128:(cc + 1) * 128], C.ident[:]), reads=[('ob', y2)], writes=[('ps', tbk)])
                dst = ost[:, g * 4:(g + 1) * 4, it * 128:(it + 1) * 128]
                srcv = psb[tbk][:, 0:512].rearrange("p (c t) -> p c t", c=4)
                S.add('act', lambda e, dst=dst, srcv=srcv: e.copy(dst, srcv), reads=[('ps', tbk)], writes=[('ost', g, it)])
            if c < 15:
                for jt in range(2):
                    S.add('pe', lambda e, jt=jt, s=s, g=g, k=k: e.matmul(ps[1][:, :512], Bt_c[s][:, jt, g * 128:(g + 1) * 128], xde[k][:, jt, :], start=(jt == 0), stop=(jt == 1)),
                          reads=[('Bt', s), ('xde', k)], writes=[('ps', 1)])
                if c == 0:
                    S.add('act', lambda e, g=g: e.copy(stf[:, g, :], ps[1][:, :512]), reads=[('ps', 1)], writes=[('stf', g)])
                else:
                    S.add('pool', lambda e, g=g, hs_=hs_: e.tensor_tensor(stf[:, g, :].rearrange("p (h q) -> p h q", h=8), stf[:, g, :].rearrange("p (h q) -> p h q", h=8), bc(eal[:, hs_], [128, 8, 64], 2), ALU.mult),
                          reads=[('stf', g), 'eal'], writes=[('stf', g)])
                    S.add('dve', lambda e, g=g: e.tensor_tensor(stf[:, g, :], ps[1][:, :512], stf[:, g, :], ALU.add), reads=[('ps', 1), ('stf', g)], writes=[('stf', g)])
                S.add('act', lambda e, g=g: e.copy(stb[:, g, :], stf[:, g, :]), reads=[('stf', g)], writes=[('stb', g)])
        S.dma(oTv[:, :, rows], ost[:], reads=[('ost', g, it) for g in range(8) for it in range(2)])
    A.pop()
    S.barrier()
    phase_outproj(C, oT, C.Wb['c_w_out'], 4096, x_src, x_dst)


def kernel(**inputs):
    nc, _ = build((0, 1, 2, 3), True)
    in_maps = [make_in_map(inputs, b) for b in range(8)]
    res = run_bass_kernel_spmd(nc, in_maps, core_ids=list(range(8)))
    return np.stack([np.asarray(r['y'], dtype=np.float32) for r in res.results], 0)
```

```python
import numpy as np
from contextlib import ExitStack
import concourse.bass as bass
import concourse.mybir as mybir

F32 = mybir.dt.float32
BF16 = mybir.dt.bfloat16
AF = mybir.ActivationFunctionType
ALU = mybir.AluOpType
AX = mybir.AxisListType

SEM_CAP = 30000
N_DMA_SEMS = 8
SAME_ENG_SYNC = True


class Sched:
    def __init__(self, nc):
        self.nc = nc
        self.ops = []
        self.last_w = {}
        self.readers = {}
        self.since_barrier = []

    def add(self, eng, fn, reads=(), writes=(), dma=False):
        idx = len(self.ops)
        deps = set()
        for r in reads:
            w = self.last_w.get(r)
            if w is not None:
                deps.add(w)
        for r in writes:
            w = self.last_w.get(r)
            if w is not None:
                deps.add(w)
            deps.update(self.readers.get(r, ()))
        for r in reads:
            self.readers.setdefault(r, []).append(idx)
        for r in writes:
            self.last_w[r] = idx
            self.readers[r] = []
        deps.discard(idx)
        self.ops.append(dict(eng=eng, fn=fn, deps=deps, dma=dma, load=(dma and len(writes) > 0)))
        self.since_barrier.append(idx)
        return idx

    def dma(self, out, in_, reads=(), writes=(), q='sp', **kw):
        return self.add(q, lambda e: e.dma_start(out=out, in_=in_, **kw), reads, writes, dma=True)

    def barrier(self):
        lastc = {}
        prev = []
        for i in self.since_barrier:
            o = self.ops[i]
            if o['dma']:
                prev.append(i)
            elif o['fn'] is not None:
                lastc[o['eng']] = i
        prev += list(lastc.values())
        self.since_barrier = []
        self.last_w = {}
        self.readers = {}
        for eng in ['pe', 'act', 'dve', 'pool', 'sp']:
            idx = len(self.ops)
            self.ops.append(dict(eng=eng, fn=None, deps=set(prev), dma=False))
            self.since_barrier.append(idx)

    def emit(self):
        nc = self.nc
        ops = self.ops
        needed = [False] * len(ops)
        for i, op in enumerate(ops):
            for d in op['deps']:
                od = ops[d]
                if od['dma']:
                    needed[d] = True
                elif od['eng'] == op['eng'] and (od['eng'] == 'pe' or not SAME_ENG_SYNC):
                    continue
                else:
                    needed[d] = True
        order = {e: [] for e in ['pe', 'act', 'dve', 'pool', 'sp']}
        for i, op in enumerate(ops):
            order[op['eng']].append(i)
        for e in ('sp', 'pool'):
            out = []
            for i in order[e]:
                op = ops[i]
                pos = len(out)
                if op['dma'] and op.get('load'):
                    md = max(op['deps']) if op['deps'] else -1
                    while pos > 0:
                        p = out[pos - 1]
                        po = ops[p]
                        if po['dma'] and not po.get('load') and p > md:
                            pos -= 1
                        else:
                            break
                out.insert(pos, i)
            order[e] = out
        dma_n = {}
        dma_rr = {}
        events = [None] * len(ops)
        pre_wait = [None] * len(ops)
        for e, lst in order.items():
            cnt = 0
            for i in lst:
                op = ops[i]
                if op['dma']:
                    k = dma_rr.get(e, 0)
                    dma_rr[e] = (k + 1) % N_DMA_SEMS
                    key = ('dma', e, k)
                    n = dma_n.get(key, 0)
                    if n > 0:
                        pre_wait[i] = (key, 16 * n)
                    n += 1
                    if 16 * n > SEM_CAP:
                        raise RuntimeError('dma sem cap exceeded; raise N_DMA_SEMS')
                    dma_n[key] = n
                    events[i] = (key, 16 * n)
                elif needed[i]:
                    cnt += 1
                    epoch, val = divmod(cnt - 1, SEM_CAP)
                    events[i] = (('eng', e, epoch), val + 1)
        keys = []
        for ev in events:
            if ev is not None and ev[0] not in keys:
                keys.append(ev[0])
        self.stack = ExitStack()
        sems = {}
        for k in keys:
            sems[k] = self.stack.enter_context(nc.semaphore('_'.join(str(x) for x in k)))
        self.n_sems = len(keys)
        streams = {e: [] for e in order}
        n_wait = 0
        for e, lst in order.items():
            seen = {}
            for i in lst:
                op = ops[i]
                waits = {}
                if pre_wait[i] is not None:
                    k, v = pre_wait[i]
                    waits[k] = max(waits.get(k, 0), v)
                for d in op['deps']:
                    od = ops[d]
                    if not od['dma'] and od['eng'] == e and (e == 'pe' or not SAME_ENG_SYNC):
                        continue
                    ev = events[d]
                    assert ev is not None, (i, d)
                    k, v = ev
                    waits[k] = max(waits.get(k, 0), v)
                wl = []
                for k, v in waits.items():
                    if seen.get(k, 0) >= v:
                        continue
                    seen[k] = v
                    wl.append((sems[k], v))
                n_wait += len(wl)
                inc = None
                if events[i] is not None:
                    inc = (sems[events[i][0]], 16 if op['dma'] else 1)
                streams[e].append((wl, op['fn'], inc))
        self.n_wait = n_wait

        def run(engobj, lst):
            for wl, fn, inc in lst:
                for s, v in wl:
                    engobj.wait_ge(s, v)
                if fn is None:
                    continue
                ins = fn(engobj)
                if inc is not None:
                    ins.then_inc(inc[0], inc[1])

        with nc.Block() as block:
            @block.tensor
            def _(t):
                run(t, streams['pe'])

            @block.scalar
            def _(t):
                run(t, streams['act'])

            @block.vector
            def _(t):
                run(t, streams['dve'])

            @block.gpsimd
            def _(t):
                run(t, streams['pool'])

            @block.sync
            def _(t):
                run(t, streams['sp'])
        self.stack.close()


class SbufAlloc:
    def __init__(self, nc, base=16640, limit=229376 - 64):
        self.nc = nc
        self.base = base
        self.off = base
        self.limit = limit
        self.uid = 0
        self.marks = []

    def push(self):
        self.marks.append(self.off)

    def pop(self):
        self.off = self.marks.pop()

    def tile(self, shape, dtype, name='t'):
        esz = 4 if dtype == F32 else 2
        n = 1
        for s in shape[1:]:
            n *= s
        nbytes = (n * esz + 63) // 64 * 64
        self.uid += 1
        t = self.nc.alloc_sbuf_tensor_at(f'{name}_{self.uid}', list(shape), dtype, offset=self.off)
        self.off += nbytes
        assert self.off <= self.limit, (name, self.off)
        return t


import math
import numpy as np
from concourse.bass_utils import run_bass_kernel_spmd

SL = 4096
D = 2048
NT = 32
EPS = 1e-6
TB = 1024


class Ctx:
    pass


def bc(ap, shape, axis):
    return ap.unsqueeze(axis).to_broadcast(list(shape))


def setup_consts(C):
    nc, S, A = C.nc, C.S, C.A
    C.ps = [nc.alloc_psum_tensor(f"ps{i}", [128, 512], F32) for i in range(8)]
    C.psb = [p[:].bitcast(BF16) for p in C.ps]
    C.identf = A.tile([128, 128], F32, 'identf')
    C.ident = A.tile([128, 128], BF16, 'ident')
    C.mask4 = A.tile([128, 4, 128], BF16, 'mask4')
    C.mle4 = A.tile([128, 4, 128], BF16, 'mle4')
    C.mle = A.tile([128, 128], BF16, 'mle')
    C.gmask = A.tile([128, 384], F32, 'gmask')
    C.pad_ = A.tile([128, 128], F32, 'cpad')
    A.push()
    C.mle_f = A.tile([128, 128], F32, 'mlef')
    C.mge_f = A.tile([128, 128], F32, 'mgef')
    for t, pat, cm, op in [(C.identf, [[-1, 128]], 1, ALU.is_equal),
                           (C.mle_f, [[1, 128]], -1, ALU.is_ge),
                           (C.mge_f, [[-1, 128]], 1, ALU.is_ge)]:
        S.add('pool', lambda e, t=t: e.memset(t[:], 1.0), writes=[t.name])
        S.add('pool', lambda e, t=t, pat=pat, cm=cm, op=op: e.affine_select(t[:], t[:], pat, op, 0.0, base=0, channel_multiplier=cm),
              reads=[t.name], writes=[t.name])
    S.add('dve', lambda e: e.tensor_copy(C.ident[:], C.identf[:]), reads=[C.identf.name], writes=['c1'])
    S.add('dve', lambda e: e.tensor_copy(C.mle[:], C.mle_f[:]), reads=[C.mle_f.name], writes=['c2'])
    for k in range(4):
        src = C.mge_f if k % 2 == 0 else C.mle_f
        S.add('dve', lambda e, k=k, src=src: e.tensor_copy(C.mask4[:, k, :], src[:]), reads=[src.name], writes=[('c3', k)])
        S.add('dve', lambda e, k=k: e.tensor_copy(C.mle4[:, k, :], C.mle_f[:]), reads=[C.mle_f.name], writes=[('c4', k)])
    S.add('pool', lambda e: e.memset(C.gmask[:, 128:256], 1.0), writes=['c8'])
    S.add('dve', lambda e: e.tensor_copy(C.gmask[:, 0:128], C.mle_f[:]), reads=[C.mle_f.name], writes=['c9'])
    S.add('dve', lambda e: e.tensor_copy(C.gmask[:, 256:384], C.mle_f[:]), reads=[C.mle_f.name], writes=['c10'])
    S.barrier()
    A.pop()


class BgConv:
    def __init__(self, C):
        self.C = C
        self.jobs = []
        self.done = set()
        self.i = 0
        A = C.A
        self.f = [A.tile([128, 2048], F32, 'bgf') for _ in range(2)]
        self.b = [A.tile([128, 2048], BF16, 'bgb') for _ in range(2)]

    def add(self, name, src, dst):
        sv, dv = flat128(src), flat128(dst)
        F_ = sv.shape[1]
        steps = []
        for f0 in range(0, F_, 2048):
            steps.append((sv, dv, f0, min(2048, F_ - f0)))
        self.jobs.append([name, steps])

    def _one(self, eng='pool'):
        S = self.C.S
        name, steps = self.jobs[0]
        sv, dv, f0, fsz = steps.pop(0)
        s = self.i % 2
        self.i += 1
        f, b = self.f[s], self.b[s]
        S.dma(f[:, :fsz], sv[:, f0:f0 + fsz], writes=[('bgf', s)])
        if eng == 'act':
            S.add('act', lambda e: e.copy(b[:, :fsz], f[:, :fsz]), reads=[('bgf', s)], writes=[('bgb', s)])
        else:
            S.add(eng, lambda e: e.tensor_copy(b[:, :fsz], f[:, :fsz]), reads=[('bgf', s)], writes=[('bgb', s)])
        S.dma(dv[:, f0:f0 + fsz], b[:, :fsz], reads=[('bgb', s)])
        if not steps:
            self.done.add(name)
            self.jobs.pop(0)

    def step(self, n):
        for _ in range(n):
            if not self.jobs:
                return
            self._one()

    def require(self, name):
        if name in self.done:
            return
        k = 0
        while name not in self.done:
            self._one(['pool', 'act', 'dve'][k % 3])
            k += 1
        self.C.S.barrier()


def flat128(ap2d):
    return ap2d.rearrange("k n -> (k n)").rearrange("(p f) -> p f", p=128)


def phase_convert(C, pairs):
    S, A = C.S, C.A
    A.push()
    CH = 4096
    fs = [A.tile([128, CH], F32, 'cvf') for _ in range(3)]
    bs = [A.tile([128, CH], BF16, 'cvb') for _ in range(3)]
    i = 0
    for src, dst in pairs:
        sv, dv = flat128(src), flat128(dst)
        F_ = sv.shape[1]
        for f0 in range(0, F_, CH):
            fsz = min(CH, F_ - f0)
            s = i % 3
            eng = ['act', 'dve', 'pool'][i % 3]
            S.dma(fs[s][:, :fsz], sv[:, f0:f0 + fsz], writes=[('cf', s)])
            if eng == 'act':
                S.add('act', lambda e, s=s, fsz=fsz: e.copy(bs[s][:, :fsz], fs[s][:, :fsz]), reads=[('cf', s)], writes=[('cb', s)])
            else:
                S.add(eng, lambda e, s=s, fsz=fsz: e.tensor_copy(bs[s][:, :fsz], fs[s][:, :fsz]), reads=[('cf', s)], writes=[('cb', s)])
            S.dma(dv[:, f0:f0 + fsz], bs[s][:, :fsz], reads=[('cb', s)])
            i += 1
    A.pop()
    S.barrier()


def rstd_ops(C, ss, rs, n, reads, tag):
    S = C.S
    S.add('act', lambda e: e.activation(rs, ss, AF.Sqrt, bias=EPS, scale=1.0 / n), reads=reads, writes=[('rs', tag)])
    S.add('dve', lambda e: e.reciprocal(rs, rs), reads=[('rs', tag)], writes=[('rs', tag)])


def phase_norm(C, x_src, w_ap, hT):
    S, A, ps, psb = C.S, C.A, C.ps, C.psb
    A.push()
    wt = A.tile([128, D], F32, 'nw')
    S.dma(wt[:], w_ap.partition_broadcast(128), writes=['nw'])
    xs = [A.tile([128, D], F32, 'nx') for _ in range(2)]
    sq = A.tile([128, D], BF16, 'nsq')
    hs = [A.tile([128, D], BF16, 'nh') for _ in range(2)]
    st = [A.tile([128, 16, 512], BF16, 'nst') for _ in range(2)]
    sm = A.tile([128, 4], F32, 'nsm')
    hTv = hT.rearrange("(c p) t -> p c t", p=128)
    for g in range(8):
        gs = g % 2
        toks = []
        for j in range(4):
            t = g * 4 + j
            s = t % 2
            S.dma(xs[s][:], x_src[t * 128:(t + 1) * 128, :], writes=[('x', s)])
            S.add('act', lambda e, s=s: e.activation(sq[:], xs[s][:], AF.Square, accum_out=sm[:, s:s + 1]), reads=[('x', s)], writes=[('ss', s)])
            rstd_ops(C, sm[:, s:s + 1], sm[:, 2 + s:3 + s], D, [('ss', s)], ('n', s))
            S.add('dve', lambda e, s=s: e.scalar_tensor_tensor(hs[s][:], xs[s][:], sm[:, 2 + s:3 + s], wt[:], ALU.mult, ALU.mult),
                  reads=[('x', s), ('rs', ('n', s)), 'nw'], writes=[('h', s)])
            for half in range(2):
                bank = 6 + half
                for c in range(8):
                    cc = half * 8 + c
                    S.add('pe', lambda e, s=s, c=c, cc=cc, bank=bank: e.transpose(psb[bank][:, c * 128:(c + 1) * 128], hs[s][:, cc * 128:(cc + 1) * 128], C.ident[:]),
                          reads=[('h', s)], writes=[('ps', bank)])
                dst = st[gs][:, half * 8:(half + 1) * 8, j * 128:(j + 1) * 128]
                srcv = psb[bank][:, 0:1024].rearrange("p (c t) -> p c t", c=8)
                tk = ('st', gs, j, half)
                toks.append(tk)
                if half == 0:
                    S.add('act', lambda e, dst=dst, srcv=srcv: e.copy(dst, srcv), reads=[('ps', bank)], writes=[tk])
                else:
                    S.add('dve', lambda e, dst=dst, srcv=srcv: e.tensor_copy(dst, srcv), reads=[('ps', bank)], writes=[tk])
        S.dma(hTv[:, :, g * 512:(g + 1) * 512], st[gs][:], reads=toks)
    A.pop()
    S.barrier()


def phase_linear(C, aT, Wb, K, nblocks, epi, on_tb=None, on_block=None, end_block=None, fm=False, lin_banks=(0, 1, 2, 3), finish=None, bg_steps=2):
    S, A, ps = C.S, C.A, C.ps
    KC = K // 128
    C.bg.require(C.Wn[id(Wb)])
    A.push()
    NA = 2 if KC == 16 else 1
    ats = [A.tile([128, KC, TB], BF16, 'lat') for _ in range(NA)]
    wt = [A.tile([128, KC, 512], BF16, 'lwt') for _ in range(2)]
    aTv = aT.rearrange("(c p) t -> p c t", p=128)
    Wv = Wb.rearrange("(c p) n -> p c n", p=128)
    C.lin_rr = 0
    wi = 0
    def load_at(tb):
        sa = tb % NA
        for kc in range(KC):
            S.dma(ats[sa][:, kc, :], aTv[:, kc, tb * TB:(tb + 1) * TB], writes=[('at', sa, kc)])

    load_at(0)
    for tb in range(SL // TB):
        sa = tb % NA
        at = ats[sa]
        if NA == 1 and tb > 0:
            load_at(tb)
        if NA == 2 and tb + 1 < SL // TB:
            load_at(tb + 1)
        if on_tb:
            on_tb(tb)
        for bi, (n0, nsz) in enumerate(nblocks):
            sw = wi % 2
            wi += 1
            S.dma(wt[sw][:, :, :nsz], Wv[:, :, n0:n0 + nsz], writes=[('wt', sw)])
            if on_block:
                on_block(tb, bi)
            if not fm:
                for ti in range(TB // 128):
                    bank = lin_banks[C.lin_rr % len(lin_banks)]
                    C.lin_rr += 1
                    for kc in range(KC):
                        S.add('pe', lambda e, bank=bank, kc=kc, ti=ti, sw=sw, nsz=nsz, at=at: e.matmul(ps[bank][:, :nsz], at[:, kc, ti * 128:(ti + 1) * 128], wt[sw][:, kc, :nsz], start=(kc == 0), stop=(kc == KC - 1)),
                              reads=[('at', sa, kc), ('wt', sw)], writes=[('ps', bank)])
                    epi(tb, bi, ti, bank, nsz)
            else:
                for j in range((nsz + 127) // 128):
                    msz = min(128, nsz - j * 128)
                    for th in range(TB // 512):
                        bank = lin_banks[C.lin_rr % len(lin_banks)]
                        C.lin_rr += 1
                        for kc in range(KC):
                            S.add('pe', lambda e, bank=bank, kc=kc, th=th, sw=sw, j=j, msz=msz, at=at: e.matmul(ps[bank][:msz, :512], wt[sw][:, kc, j * 128:j * 128 + msz], at[:, kc, th * 512:(th + 1) * 512], start=(kc == 0), stop=(kc == KC - 1)),
                                  reads=[('at', sa, kc), ('wt', sw)], writes=[('ps', bank)])
                        epi(tb, bi, j, th, bank, msz)
            C.bg.step(bg_steps)
            if end_block:
                end_block(tb, bi)
    if finish:
        finish()
    A.pop()
    S.barrier()


def phase_outproj(C, oT, Wb, K, x_src, x_dst):
    S, A, ps = C.S, C.A, C.ps
    A.push()
    xp = [A.tile([128, 8, 512], F32, 'xp') for _ in range(2)]
    st = {}

    def on_block(tb, bi):
        s = (tb * 4 + bi) % 2
        st['s'] = s
        src = x_src[tb * TB:(tb + 1) * TB, bi * 512:(bi + 1) * 512].rearrange("(t p) n -> p t n", p=128)
        S.dma(xp[s][:], src, writes=[('xp', s, ti) for ti in range(8)])

    def epi(tb, bi, ti, bank, nsz):
        s = st['s']
        S.add('dve', lambda e: e.tensor_tensor(xp[s][:, ti, :], ps[bank][:, :512], xp[s][:, ti, :], ALU.add),
              reads=[('ps', bank), ('xp', s, ti)], writes=[('xp', s, ti)])

    def end_block(tb, bi):
        s = st['s']
        dst = x_dst[tb * TB:(tb + 1) * TB, bi * 512:(bi + 1) * 512].rearrange("(t p) n -> p t n", p=128)
        S.dma(dst, xp[s][:], reads=[('xp', s, ti) for ti in range(8)])

    phase_linear(C, oT, Wb, K, [(i * 512, 512) for i in range(4)], epi, on_block=on_block, end_block=end_block, bg_steps=0)
    A.pop()


def phase_mlp(C, hT, W1b, W2b, x_io):
    S, A, ps = C.S, C.A, C.ps
    C.bg.require(C.Wn[id(W1b)])
    C.bg.require(C.Wn[id(W2b)])
    A.push()
    at = A.tile([128, 16, TB], BF16, 'mat')
    acc = A.tile([128, 8, D], F32, 'macc')
    w1 = [A.tile([128, 16, 512], BF16, 'mw1') for _ in range(2)]
    w2 = [A.tile([128, 4, D], BF16, 'mw2') for _ in range(2)]
    uT = [A.tile([128, 4, TB], BF16, 'muT') for _ in range(2)]
    rl = [A.tile([128, 512], F32, 'mrl')]
    hTv = hT.rearrange("(c p) t -> p c t", p=128)
    W1v = W1b.rearrange("(c p) n -> p c n", p=128)
    W2v = W2b.rearrange("(c p) n -> p c n", p=128)
    rr = 0
    ri = 0
    for tb in range(SL // TB):
        for kc in range(16):
            S.dma(at[:, kc, :], hTv[:, kc, tb * TB:(tb + 1) * TB], writes=[('at', kc)])
        for ti in range(8):
            r0 = tb * TB + ti * 128
            S.dma(acc[:, ti, :], x_io[r0:r0 + 128, :], writes=[('acc', ti, dc) for dc in range(4)])
        for fg in range(16):
            s = fg % 2
            S.dma(w1[s][:], W1v[:, :, fg * 512:(fg + 1) * 512], writes=[('w1', s)])
            S.dma(w2[s][:], W2v[:, fg * 4:(fg + 1) * 4, :], writes=[('w2', s)])
            C.bg.step(3)
            for fc in range(4):
                for th in range(2):
                    bank = rr % 4
                    rr += 1
                    for kc in range(16):
                        S.add('pe', lambda e, bank=bank, kc=kc, s=s, fc=fc, th=th: e.matmul(ps[bank][:, :512], w1[s][:, kc, fc * 128:(fc + 1) * 128], at[:, kc, th * 512:(th + 1) * 512], start=(kc == 0), stop=(kc == 15)),
                              reads=[('at', kc), ('w1', s)], writes=[('ps', bank)])
                    k = 0
                    ri += 1
                    S.add('act', lambda e, bank=bank, k=k: e.activation(rl[k][:], ps[bank][:, :512], AF.Relu), reads=[('ps', bank)], writes=[('rl', k)])
                    S.add('dve', lambda e, bank=bank, k=k, s=s, fc=fc, th=th: e.tensor_tensor(uT[s][:, fc, th * 512:(th + 1) * 512], ps[bank][:, :512], rl[k][:], ALU.mult),
                          reads=[('ps', bank), ('rl', k)], writes=[('uT', s, fc, th)])
            for ti in range(8):
                for dc in range(4):
                    bank = rr % 4
                    rr += 1
                    for fc in range(4):
                        S.add('pe', lambda e, bank=bank, s=s, fc=fc, ti=ti, dc=dc: e.matmul(ps[bank][:, :512], uT[s][:, fc, ti * 128:(ti + 1) * 128], w2[s][:, fc, dc * 512:(dc + 1) * 512], start=(fc == 0), stop=(fc == 3)),
                              reads=[('uT', s, fc, ti // 4), ('w2', s)], writes=[('ps', bank)])
                    S.add('dve', lambda e, bank=bank, ti=ti, dc=dc: e.tensor_tensor(acc[:, ti, dc * 512:(dc + 1) * 512], ps[bank][:, :512], acc[:, ti, dc * 512:(dc + 1) * 512], ALU.add),
                          reads=[('ps', bank), ('acc', ti, dc)], writes=[('acc', ti, dc)])
        for ti in range(8):
            r0 = tb * TB + ti * 128
            S.dma(x_io[r0:r0 + 128, :], acc[:, ti, :], reads=[('acc', ti, dc) for dc in range(4)])
    A.pop()
    S.barrier()


LAG = 2


class Deferred:
    def __init__(self):
        self.q = []
        self.u = 0

    def push(self, fn):
        self.q.append((self.u, fn))

    def tick(self):
        self.u += 1
        while self.q and self.u - self.q[0][0] >= LAG:
            self.q.pop(0)[1]()

    def flush(self):
        while self.q:
            self.q.pop(0)[1]()


def make_qkv_epi(C, specs, qkT, vdst, wq, wk):
    S, A, ps, psb = C.S, C.A, C.ps, C.psb
    sq = [A.tile([128, 512], F32, 'esq') for _ in range(3)]
    tn = [A.tile([128, 512], F32, 'etn') for _ in range(3)]
    qn = [A.tile([128, 512], BF16, 'eqn') for _ in range(4)]
    sm = A.tile([128, 24], F32, 'esm')
    qst = [A.tile([128, 4, TB], BF16, 'eqst') for _ in range(2)]
    vst = [A.tile([128, 8, 512], BF16, 'evst') for _ in range(2)]
    st = {'k': 0, 'blk': 0}
    dq = Deferred()
    qkTv = qkT.rearrange("m d t -> d m t")

    def on_block(tb, bi):
        st['slot'] = st['blk'] % 2
        st['blk'] += 1

    def epi(tb, bi, ti, bank, nsz):
        kind, a0 = specs[bi]
        slot = st['slot']
        if kind == 'v':
            S.add('act', lambda e: e.copy(vst[slot][:, ti, :], ps[bank][:, :512]), reads=[('ps', bank)], writes=[('vst', slot, ti)])
            dq.tick()
            return
        wn = wq if kind == 'q' else wk
        cnt = st['k']
        st['k'] += 1
        k = cnt % 3
        kq = cnt % 4
        S.add('act', lambda e: e.activation(sq[k][:], ps[bank][:, :512], AF.Square), reads=[('ps', bank)], writes=[('sq', k)])
        ssa = sm[:, k * 8:k * 8 + 4]
        rsa = sm[:, k * 8 + 4:k * 8 + 8]
        S.add('dve', lambda e: e.tensor_reduce(ssa, sq[k][:].rearrange("p (g d) -> p g d", g=4), AX.X, ALU.add), reads=[('sq', k)], writes=[('ss', k)])
        rstd_ops(C, ssa, rsa, 128, [('ss', k)], ('e', k))
        S.add('dve', lambda e: e.tensor_tensor(tn[k][:].rearrange("p (g d) -> p g d", g=4), ps[bank][:, :512].rearrange("p (g d) -> p g d", g=4), bc(rsa, [128, 4, 128], 2), ALU.mult),
              reads=[('ps', bank), ('rs', ('e', k))], writes=[('tn', k)])
        S.add('pool', lambda e: e.tensor_tensor(qn[kq][:].rearrange("p (g d) -> p g d", g=4), tn[k][:].rearrange("p (g d) -> p g d", g=4), bc(wn[:], [128, 4, 128], 1), ALU.mult),
              reads=[('tn', k)], writes=[('qn', kq)])

        def tail():
            tbk = 6 + cnt % 2
            for c in range(4):
                S.add('pe', lambda e, c=c: e.transpose(psb[tbk][:, c * 128:(c + 1) * 128], qn[kq][:, c * 128:(c + 1) * 128], C.ident[:]), reads=[('qn', kq)], writes=[('ps', tbk)])
            dst = qst[slot][:, :, ti * 128:(ti + 1) * 128]
            srcv = psb[tbk][:, 0:512].rearrange("p (c t) -> p c t", c=4)
            if cnt % 2 == 0:
                S.add('act', lambda e: e.copy(dst, srcv), reads=[('ps', tbk)], writes=[('qst', slot, ti)])
            else:
                S.add('dve', lambda e: e.tensor_copy(dst, srcv), reads=[('ps', tbk)], writes=[('qst', slot, ti)])

        dq.push(tail)
        dq.tick()

    def end_block(tb, bi):
        kind, a0 = specs[bi]
        slot = st['slot']
        if kind == 'v':
            dst = vdst[tb * TB:(tb + 1) * TB, a0:a0 + 512].rearrange("(t p) n -> p t n", p=128)
            S.dma(dst, vst[slot][:], reads=[('vst', slot, ti) for ti in range(8)])
        else:
            dq.push(lambda: S.dma(qkTv[:, a0:a0 + 4, tb * TB:(tb + 1) * TB], qst[slot][:], reads=[('qst', slot, ti) for ti in range(8)]))

    return epi, on_block, end_block, dq.flush


def load_bcast(C, ap1d, n, name):
    t = C.A.tile([128, n], F32, name)
    C.S.dma(t[:], ap1d.partition_broadcast(128), writes=[name])
    return t


def layer_diff(C, I, hT, x_src, x_dst, layer_idx):
    nc, S, A, ps, psb = C.nc, C.S, C.A, C.ps, C.psb
    lam_init = 0.8 - 0.6 * math.exp(-0.3 * layer_idx)
    qkT = nc.dram_tensor("a_qkT", [32, 128, SL], BF16).ap()
    vv = nc.dram_tensor("a_v", [SL, D], BF16).ap()
    oT = nc.dram_tensor("a_oT", [D, SL], BF16).ap()
    A.push()
    wq = load_bcast(C, I['a_q_norm_w'], 128, 'awq')
    wk = load_bcast(C, I['a_k_norm_w'], 128, 'awk')
    specs = [('q', 4 * i) for i in range(4)] + [('k', 16 + 4 * i) for i in range(4)] + [('v', 512 * i) for i in range(4)]
    epi, on_block, end_block, fin = make_qkv_epi(C, specs, qkT, vv, wq, wk)
    phase_linear(C, hT, C.Wb['a_w_in'], D, [(i * 512, 512) for i in range(12)], epi, on_block=on_block, end_block=end_block, finish=fin)
    A.pop()
    A.push()
    lt = A.tile([128, 4, 128], F32, 'alt')
    for i, nm in enumerate(['a_lambda_q1', 'a_lambda_k1', 'a_lambda_q2', 'a_lambda_k2']):
        S.dma(lt[:, i, :], I[nm].partition_broadcast(128), writes=[('lt', i)])
    lsm = A.tile([128, 8], F32, 'alsm')
    ljunk = A.tile([128, 128], F32, 'alj')
    for i in range(2):
        S.add('dve', lambda e, i=i: e.tensor_tensor(ljunk[:], lt[:, 2 * i, :], lt[:, 2 * i + 1, :], ALU.mult), reads=[('lt', 2 * i), ('lt', 2 * i + 1)], writes=['lj'])
        S.add('dve', lambda e, i=i: e.tensor_reduce(lsm[:, i:i + 1], ljunk[:], AX.X, ALU.add), reads=['lj'], writes=[('ld', i)])
        S.add('act', lambda e, i=i: e.activation(lsm[:, 2 + i:3 + i], lsm[:, i:i + 1], AF.Exp), reads=[('ld', i)], writes=[('le', i)])
    S.add('dve', lambda e: e.tensor_tensor(lsm[:, 4:5], lsm[:, 3:4], lsm[:, 2:3], ALU.subtract), reads=[('le', 0), ('le', 1)], writes=['l4'])
    S.add('dve', lambda e: e.tensor_scalar(lsm[:, 5:6], lsm[:, 4:5], -lam_init, None, ALU.add), reads=['l4'], writes=['neglam'])
    neglam = lsm[:, 5:6]
    wsub = load_bcast(C, I['a_subln_w'], 256, 'awsub')
    S.add('dve', lambda e: e.tensor_scalar(wsub[:], wsub[:], 1.0 - lam_init, None, ALU.mult), reads=['awsub'], writes=['awsub'])
    qT = [A.tile([128, 2, SL], BF16, 'aqT') for _ in range(2)]
    kT = [A.tile([128, 2, SL], BF16, 'akT') for _ in range(2)]
    Va = [A.tile([128, 32, 257], BF16, 'aVa') for _ in range(2)]
    E = [A.tile([128, 512], BF16, 'aE') for _ in range(5)]
    on0 = [A.tile([128, 256], F32, 'aon0') for _ in range(4)]
    dif = [A.tile([128, 256], F32, 'adif') for _ in range(2)]
    ob = [A.tile([128, 256], BF16, 'aob') for _ in range(2)]
    junk = A.tile([128, 256], BF16, 'ajunk')
    sm = A.tile([128, 16], F32, 'asm')
    ost = [A.tile([128, 2, 512], BF16, 'aost') for _ in range(2)]
    qkTv = qkT.rearrange("m d t -> d m t")
    oTv = oT.rearrange("(c p) t -> p c t", p=128)
    scale = 128 ** -0.5
    for s in range(2):
        S.add('pool', lambda e, s=s: e.memset(Va[s][:, :, 256:257], 1.0), writes=[('Vone', s)])
    ei = 0
    sti = 0
    di = 0
    bi_ = 0
    for hp in range(8):
        s = hp % 2
        S.dma(qT[s][:], qkTv[:, 2 * hp:2 * hp + 2, :], writes=[('qT', s)])
        S.dma(kT[s][:], qkTv[:, 16 + 2 * hp:16 + 2 * hp + 2, :], writes=[('kT', s)])
        S.dma(Va[s][:, :, 0:256], vv[:, hp * 256:(hp + 1) * 256].rearrange("(t p) n -> p t n", p=128), writes=[('V', s)])
        for qb in range(8):
            oslot = bi_ % 2
            bi_ += 1
            for m in range(2):
                nkt = 4 * (qb + 1)
                pend = {}

                def emit_st(kt, s=s, m=m, qb=qb):
                    nonlocal sti, ei
                    sb = 4 + (sti % 3)
                    sti += 1
                    S.add('pe', lambda e, sb=sb, kt=kt, s=s, m=m, qb=qb: e.matmul(ps[sb][:, :512], kT[s][:, m, kt * 128:(kt + 1) * 128], qT[s][:, m, qb * 512:(qb + 1) * 512], start=True, stop=True),
                          reads=[('kT', s), ('qT', s)], writes=[('ps', sb)])
                    pend[kt] = sb

                emit_st(0)
                if nkt > 1:
                    emit_st(1)
                for kt in range(nkt):
                    if kt + 2 < nkt:
                        emit_st(kt + 2)
                    sb = pend.pop(kt)
                    k = ei % 5
                    ei += 1
                    S.add('act', lambda e, sb=sb, k=k: e.activation(E[k][:], ps[sb][:, :512], AF.Exp, scale=scale), reads=[('ps', sb)], writes=[('E', k)])
                    jd = kt - 4 * qb
                    if jd >= 0:
                        S.add('pool', lambda e, k=k, jd=jd: e.tensor_tensor(E[k][:, jd * 128:(jd + 1) * 128], E[k][:, jd * 128:(jd + 1) * 128], C.mle[:], ALU.mult), reads=[('E', k)], writes=[('E', k)])
                    for j in range(4):
                        if jd > j:
                            continue
                        last_kt = 4 * qb + j
                        S.add('pe', lambda e, j=j, k=k, s=s, kt=kt, last_kt=last_kt: e.matmul(ps[j][:, :257], E[k][:, j * 128:(j + 1) * 128], Va[s][:, kt, :], start=(kt == 0), stop=(kt == last_kt)),
                              reads=[('E', k), ('V', s), ('Vone', s)], writes=[('ps', j)])
                for j in range(4):
                    rl_ = sm[:, j:j + 1]
                    S.add('dve', lambda e, j=j, rl_=rl_: e.reciprocal(rl_, ps[j][:, 256:257]), reads=[('ps', j)], writes=[('rl', j)])
                    if m == 0:
                        S.add('dve', lambda e, j=j, rl_=rl_: e.tensor_scalar(on0[j][:], ps[j][:, 0:256], rl_, None, ALU.mult), reads=[('ps', j), ('rl', j)], writes=[('on0', j)])
                    else:
                        d = di % 2
                        di += 1
                        rl2 = sm[:, 4 + j:5 + j]
                        S.add('dve', lambda e, rl2=rl2, rl_=rl_: e.tensor_tensor(rl2, rl_, neglam, ALU.mult), reads=[('rl', j), 'neglam'], writes=[('rl2', j)])
                        S.add('dve', lambda e, j=j, d=d, rl2=rl2: e.scalar_tensor_tensor(dif[d][:], ps[j][:, 0:256], rl2, on0[j][:], ALU.mult, ALU.add),
                              reads=[('ps', j), ('rl2', j), ('on0', j)], writes=[('dif', d)])
                        ssd = sm[:, 8 + d:9 + d]
                        rsd = sm[:, 10 + d:11 + d]
                        S.add('act', lambda e, d=d, ssd=ssd: e.activation(junk[:], dif[d][:], AF.Square, accum_out=ssd), reads=[('dif', d)], writes=[('ssd', d)])
                        rstd_ops(C, ssd, rsd, 256, [('ssd', d)], ('a', d))
                        S.add('dve', lambda e, d=d, rsd=rsd: e.scalar_tensor_tensor(ob[d][:], dif[d][:], rsd, wsub[:], ALU.mult, ALU.mult), reads=[('dif', d), ('rs', ('a', d)), 'awsub'], writes=[('ob', d)])
                        for c in range(2):
                            S.add('pe', lambda e, d=d, c=c: e.transpose(psb[7][:, c * 128:(c + 1) * 128], ob[d][:, c * 128:(c + 1) * 128], C.ident[:]), reads=[('ob', d)], writes=[('ps', 7)])
                        S.add('act', lambda e, oslot=oslot, j=j: e.copy(ost[oslot][:, :, j * 128:(j + 1) * 128], psb[7][:, 0:256].rearrange("p (c t) -> p c t", c=2)),
                              reads=[('ps', 7)], writes=[('ost', oslot, j)])
            S.dma(oTv[:, 2 * hp:2 * hp + 2, qb * 512:(qb + 1) * 512], ost[oslot][:], reads=[('ost', oslot, j) for j in range(4)])
    A.pop()
    S.barrier()
    phase_outproj(C, oT, C.Wb['a_w_out'], D, x_src, x_dst)


WSHAPES = {
    'a_w_in': (D, 3 * D), 'a_w_out': (D, D),
    'b_w_in': (D, 12288), 'b_w_out': (4096, D),
    'c_w_in': (D, 10304), 'c_w_out': (4096, D),
    'd_w_in': (D, 18432), 'd_w_out': (D, D),
}
SMALL = {
    'a_q_norm_w': [128], 'a_k_norm_w': [128], 'a_lambda_q1': [128], 'a_lambda_k1': [128], 'a_lambda_q2': [128],
    'a_lambda_k2': [128], 'a_subln_w': [256], 'b_gn_w': [4096], 'c_conv_wT': [6144, 4], 'c_conv_bT': [128, 48],
    'c_dt_bias': [64], 'c_a_log': [64], 'c_d_skip': [64], 'c_norm_w': [4096], 'd_q_norm_w': [3, 128], 'd_k_norm_w': [3, 128],
    'r_cos': [SL, 256], 'r_ssin': [SL, 256], 'r_qs': [128, 8], 'r_ks': [128, 8],
}
LAYER_W = {0: ['a_w_in', 'a_w_out'], 1: ['b_w_in', 'b_w_out'], 2: ['c_w_in', 'c_w_out'], 3: ['d_w_in', 'd_w_out']}
LAYER_S = {0: ['a_q_norm_w', 'a_k_norm_w', 'a_lambda_q1', 'a_lambda_k1', 'a_lambda_q2', 'a_lambda_k2', 'a_subln_w'],
           1: ['b_gn_w', 'r_cos', 'r_ssin', 'r_qs', 'r_ks'],
           2: ['c_conv_wT', 'c_conv_bT', 'c_dt_bias', 'c_a_log', 'c_d_skip', 'c_norm_w'],
           3: ['d_q_norm_w', 'd_k_norm_w']}


def build(layers=(0, 1, 2, 3), mlp=True, dbg=False):
    nc = bass.Bass("TRN2", target_bir_lowering=False)
    C = Ctx()
    C.dbg = dbg
    C.nc = nc
    C.S = Sched(nc)
    C.A = SbufAlloc(nc)
    I = {}

    def inp(name, shape):
        I[name] = nc.dram_tensor(name, list(shape), F32, kind="ExternalInput").ap()

    inp('x', [SL, D])
    inp('norm1_w', [4, D])
    inp('norm2_w', [4, D])
    y = nc.dram_tensor("y", [SL, D], F32, kind="ExternalOutput").ap()
    C.Wb = {}
    C.Wn = {}
    pairs = []
    for li in layers:
        for nm in LAYER_W[li]:
            inp(nm, WSHAPES[nm])
            dst = nc.dram_tensor(nm + '_bf', list(WSHAPES[nm]), BF16).ap()
            C.Wb[nm] = dst
            C.Wn[id(dst)] = nm
            pairs.append((I[nm], dst))
        for nm in LAYER_S[li]:
            inp(nm, SMALL[nm])
        if mlp:
            inp(f'mlp_w1_{li}', [D, 4 * D])
            inp(f'mlp_w2_{li}', [4 * D, D])
            for nm, shp in [(f'mlp_w1_{li}', [D, 4 * D]), (f'mlp_w2_{li}', [4 * D, D])]:
                dst = nc.dram_tensor(nm + '_bf', shp, BF16).ap()
                C.Wb[nm] = dst
                C.Wn[id(dst)] = nm
                pairs.append((I[nm], dst))
    setup_consts(C)
    C.bg = BgConv(C)
    for src, dst in pairs:
        C.bg.add(C.Wn[id(dst)], src, dst)
    hT = nc.dram_tensor("hT", [D, SL], BF16).ap()
    x_src = I['x']
    for li in layers:
        phase_norm(C, x_src, I['norm1_w'][li], hT)
        [layer_diff, layer_ret, layer_ssd, layer_dil][li](C, I, hT, x_src, y, li)
        x_src = y
        if mlp:
            phase_norm(C, y, I['norm2_w'][li], hT)
            phase_mlp(C, hT, C.Wb[f'mlp_w1_{li}'], C.Wb[f'mlp_w2_{li}'], y)
    C.S.emit()
    C.I = I
    return nc, C


def host_consts():
    out = {}
    theta = np.repeat(1.0 / (10000.0 ** np.linspace(0.0, 1.0, 128, dtype=np.float32)), 2).astype(np.float32)
    ang = (np.arange(SL, dtype=np.float32)[:, None] * theta[None, :]).astype(np.float32)
    cos = np.cos(ang.astype(np.float64)).astype(np.float32)
    sin = np.sin(ang.astype(np.float64)).astype(np.float32)
    ssin = sin.copy()
    ssin[:, 0::2] *= -1.0
    out['r_cos'] = cos
    out['r_ssin'] = ssin
    gam = 1.0 - 2.0 ** (-5.0 - np.arange(8, dtype=np.float64))
    i1 = np.arange(128, dtype=np.float64)[:, None] + 1.0
    out['r_qs'] = (gam[None, :] ** i1).astype(np.float32)
    out['r_ks'] = ((gam[None, :] ** (-i1)) * (256 ** -0.5)).astype(np.float32)
    return out


def make_in_map(inputs, b, layers=(0, 1, 2, 3), mlp=True):
    hc = host_consts()
    m = {'x': np.ascontiguousarray(inputs['x'][b]), 'norm1_w': np.asarray(inputs['norm1_w']), 'norm2_w': np.asarray(inputs['norm2_w'])}
    for li in layers:
        for nm in LAYER_W[li]:
            m[nm] = np.asarray(inputs[nm])
        for nm in LAYER_S[li]:
            if nm in hc:
                m[nm] = hc[nm]
            elif nm == 'c_conv_wT':
                m[nm] = np.ascontiguousarray(np.asarray(inputs['c_conv_w']).T)
            elif nm == 'c_conv_bT':
                m[nm] = np.ascontiguousarray(np.asarray(inputs['c_conv_b']).reshape(48, 128).T)
            else:
                m[nm] = np.asarray(inputs[nm])
        if mlp:
            m[f'mlp_w1_{li}'] = np.ascontiguousarray(inputs['mlp_w1'][li])
            m[f'mlp_w2_{li}'] = np.ascontiguousarray(inputs['mlp_w2'][li])
    return m


def layer_ret(C, I, hT, x_src, x_dst, layer_idx):
    nc, S, A, ps, psb = C.nc, C.S, C.A, C.ps, C.psb
    qT = nc.dram_tensor("b_qT", [D, SL], BF16).ap()
    kT = nc.dram_tensor("b_kT", [D, SL], BF16).ap()
    ktm = nc.dram_tensor("b_ktm", [SL, D], BF16).ap()
    vtm = nc.dram_tensor("b_vtm", [SL, 4096], BF16).ap()
    sgt = nc.dram_tensor("b_sgt", [SL, 4096], BF16).ap()
    oT = nc.dram_tensor("b_oT", [4096, SL], BF16).ap()
    qTv = qT.rearrange("(c p) t -> p c t", p=128)
    kTv = kT.rearrange("(c p) t -> p c t", p=128)
    oTv = oT.rearrange("(c p) t -> p c t", p=128)
    A.push()
    qs = A.tile([128, 8], F32, 'bqs')
    ks = A.tile([128, 8], F32, 'bks')
    S.dma(qs[:], I['r_qs'], writes=['bqs'])
    S.dma(ks[:], I['r_ks'], writes=['bks'])
    cst = A.tile([128, 8, 256], F32, 'bcos')
    sst = A.tile([128, 8, 256], F32, 'bsin')
    t1 = [A.tile([128, 512], F32, 'bt1') for _ in range(3)]
    t2 = [A.tile([128, 512], F32, 'bt2') for _ in range(3)]
    qn = [A.tile([128, 512], BF16, 'bqn') for _ in range(4)]
    dq = Deferred()
    qst = [A.tile([128, 4, TB], BF16, 'bqst') for _ in range(2)]
    vst = [A.tile([128, 8, 512], BF16, 'bvst') for _ in range(2)]
    st = {'k': 0, 'blk': 0}

    def on_tb(tb):
        S.dma(cst[:], I['r_cos'][tb * TB:(tb + 1) * TB, :].rearrange("(t p) n -> p t n", p=128), writes=['bcos'])
        S.dma(sst[:], I['r_ssin'][tb * TB:(tb + 1) * TB, :].rearrange("(t p) n -> p t n", p=128), writes=['bsin'])

    def on_block(tb, bi):
        st['slot'] = st['blk'] % 2
        st['blk'] += 1

    def epi(tb, bi, ti, bank, nsz):
        slot = st['slot']
        if bi >= 8:
            if bi < 16:
                S.add('act', lambda e: e.copy(vst[slot][:, ti, :], ps[bank][:, :512]), reads=[('ps', bank)], writes=[('vst', slot, ti)])
            else:
                S.add('act', lambda e: e.activation(vst[slot][:, ti, :], ps[bank][:, :512], AF.Silu), reads=[('ps', bank)], writes=[('vst', slot, ti)])
            dq.tick()
            return
        isq = bi < 4
        sc = qs if isq else ks
        scn = 'bqs' if isq else 'bks'
        cnt = st['k']
        st['k'] += 1
        k = cnt % 3
        kq = cnt % 4
        pv3 = ps[bank][:, :512].rearrange("p (h d) -> p h d", h=2)
        pv4 = ps[bank][:, :512].rearrange("p (h i two) -> p h i two", h=2, two=2)
        t14 = t2[k][:].rearrange("p (h i two) -> p h i two", h=2, two=2)
        cosb = bc(cst[:, ti, :], [128, 2, 256], 1)
        sn3 = sst[:, ti, :].rearrange("p (i two) -> p i two", two=2)
        S.add('dve', lambda e: e.tensor_tensor(t1[k][:].rearrange("p (h d) -> p h d", h=2), pv3, cosb, ALU.mult), reads=[('ps', bank), 'bcos'], writes=[('t1', k)])
        S.add('dve', lambda e: e.tensor_tensor(t14[:, :, :, 0], pv4[:, :, :, 1], bc(sn3[:, :, 0], [128, 2, 128], 1), ALU.mult), reads=[('ps', bank), 'bsin'], writes=[('t2a', k)])
        S.add('dve', lambda e: e.tensor_tensor(t14[:, :, :, 1], pv4[:, :, :, 0], bc(sn3[:, :, 1], [128, 2, 128], 1), ALU.mult), reads=[('ps', bank), 'bsin'], writes=[('t2b', k)])
        S.add('pool', lambda e: e.tensor_tensor(t1[k][:], t1[k][:], t2[k][:], ALU.add), reads=[('t1', k), ('t2a', k), ('t2b', k)], writes=[('t1', k)])
        for hh in range(2):
            h = (bi % 4) * 2 + hh
            S.add('act', lambda e, hh=hh, h=h: e.mul(qn[kq][:, hh * 256:(hh + 1) * 256], t1[k][:, hh * 256:(hh + 1) * 256], sc[:, h:h + 1]), reads=[('t1', k), scn], writes=[('qn', kq, hh)])
        if not isq:
            S.add('pool', lambda e: e.tensor_copy(vst[slot][:, ti, :], qn[kq][:]), reads=[('qn', kq, 0), ('qn', kq, 1)], writes=[('vst', slot, ti)])

        def tail():
            tbk = 6 + cnt % 2
            for c in range(4):
                S.add('pe', lambda e, c=c: e.transpose(psb[tbk][:, c * 128:(c + 1) * 128], qn[kq][:, c * 128:(c + 1) * 128], C.ident[:]), reads=[('qn', kq, c // 2)], writes=[('ps', tbk)])
            dst = qst[slot][:, :, ti * 128:(ti + 1) * 128]
            srcv = psb[tbk][:, 0:512].rearrange("p (c t) -> p c t", c=4)
            S.add('dve', lambda e: e.tensor_copy(dst, srcv), reads=[('ps', tbk)], writes=[('qst', slot, ti)])

        dq.push(tail)
        dq.tick()

    def end_block(tb, bi):
        slot = st['slot']
        rows = slice(tb * TB, (tb + 1) * TB)
        if bi < 8:
            dstT = (qTv if bi < 4 else kTv)[:, (bi % 4) * 4:(bi % 4) * 4 + 4, rows]
            dq.push(lambda: S.dma(dstT, qst[slot][:], reads=[('qst', slot, ti) for ti in range(8)]))
        if bi >= 4:
            if bi < 8:
                dst = ktm[rows, (bi - 4) * 512:(bi - 3) * 512]
            elif bi < 16:
                dst = vtm[rows, (bi - 8) * 512:(bi - 7) * 512]
            else:
                dst = sgt[rows, (bi - 16) * 512:(bi - 15) * 512]
            S.dma(dst.rearrange("(t p) n -> p t n", p=128), vst[slot][:], reads=[('vst', slot, ti) for ti in range(8)])

    phase_linear(C, hT, C.Wb['b_w_in'], D, [(i * 512, 512) for i in range(24)], epi, on_tb=on_tb, on_block=on_block, end_block=end_block, finish=dq.flush)
    A.pop()
    A.push()
    gnw = load_bcast(C, I['b_gn_w'], 4096, 'bgnw')
    qTc = [A.tile([128, 16, 128], BF16, 'bqTc') for _ in range(2)]
    kTc = [A.tile([128, 16, 128], BF16, 'bkTc') for _ in range(2)]
    kc_ = [A.tile([128, D], BF16, 'bkc') for _ in range(2)]
    vc_ = [A.tile([128, 4096], BF16, 'bvc') for _ in range(2)]
    gc_ = [A.tile([128, 4096], BF16, 'bgc') for _ in range(2)]
    R = A.tile([128, 16, 512], F32, 'bR')
    Rb = A.tile([128, 16, 512], BF16, 'bRb')
    SM = [A.tile([128, 512], BF16, 'bSM') for _ in range(2)]
    tmp = [A.tile([128, 512], F32, 'btmp') for _ in range(2)]
    obf = [A.tile([128, 512], BF16, 'bobf') for _ in range(4)]
    rdq = Deferred()
    junk = A.tile([128, 512], BF16, 'bjunk')
    sm = A.tile([128, 8], F32, 'bsm')
    ost = [A.tile([128, 32, 256], BF16, 'bost') for _ in range(2)]
    ki = 0
    k2i = 0
    def loads(c):
        s = c % 2
        cols = slice(c * 128, (c + 1) * 128)
        S.dma(qTc[s][:], qTv[:, :, cols], writes=[('qTc', s)])
        S.dma(kTc[s][:], kTv[:, :, cols], writes=[('kTc', s)])
        S.dma(kc_[s][:], ktm[cols, :], writes=[('kc', s)])
        S.dma(vc_[s][:], vtm[cols, :], writes=[('vc', s)])
        S.dma(gc_[s][:], sgt[cols, :], writes=[('gc', s)])

    def emit_ST(c, hg):
        s = c % 2
        b = hg % 2
        for hh in range(4):
            h = hg * 4 + hh
            for dc in range(2):
                S.add('pe', lambda e, b=b, hh=hh, h=h, dc=dc, s=s: e.matmul(ps[b][:, hh * 128:(hh + 1) * 128], kTc[s][:, h * 2 + dc, :], qTc[s][:, h * 2 + dc, :], start=(dc == 0), stop=(dc == 1)),
                      reads=[('kTc', s), ('qTc', s)], writes=[('ps', b)])
        return b

    loads(0)
    pend = emit_ST(0, 0)
    for c in range(32):
        s = c % 2
        oslot = (c // 2) % 2
        for hg in range(2):
            b = pend
            if hg == 0:
                pend = emit_ST(c, 1)
            elif c + 1 < 32:
                loads(c + 1)
                pend = emit_ST(c + 1, 0)
            k = ki % 2
            ki += 1
            S.add('dve', lambda e, b=b, k=k: e.tensor_tensor(SM[k][:], ps[b][:, :512], C.mle4[:].rearrange("p a b -> p (a b)"), ALU.mult), reads=[('ps', b)], writes=[('SM', k)])
            for hh in range(4):
                h = hg * 4 + hh
                gC = (1.0 - 2.0 ** (-5.0 - h)) ** 128
                ob_ = 2 + (h % 2)
                S.add('pe', lambda e, ob_=ob_, k=k, hh=hh, h=h, s=s, c=c: e.matmul(ps[ob_][:, :512], SM[k][:, hh * 128:(hh + 1) * 128], vc_[s][:, h * 512:(h + 1) * 512], start=True, stop=(c == 0)),
                      reads=[('SM', k), ('vc', s)], writes=[('ps', ob_)])
                if c > 0:
                    for dc in range(2):
                        S.add('pe', lambda e, ob_=ob_, h=h, dc=dc, s=s: e.matmul(ps[ob_][:, :512], qTc[s][:, h * 2 + dc, :], Rb[:, h * 2 + dc, :], start=False, stop=(dc == 1)),
                              reads=[('qTc', s), ('Rb', h, dc)], writes=[('ps', ob_)])
                if c < 31:
                    for dc in range(2):
                        kvb = 4 + dc
                        S.add('pe', lambda e, kvb=kvb, h=h, dc=dc, s=s: e.matmul(ps[kvb][:, :512], kc_[s][:, h * 256 + dc * 128:h * 256 + (dc + 1) * 128], vc_[s][:, h * 512:(h + 1) * 512], start=True, stop=True),
                              reads=[('kc', s), ('vc', s)], writes=[('ps', kvb)])
                        Rs = R[:, h * 2 + dc, :]
                        if c == 0:
                            S.add('dve', lambda e, kvb=kvb, Rs=Rs, gC=gC: e.tensor_scalar(Rs, ps[kvb][:, :512], gC, None, ALU.mult), reads=[('ps', kvb)], writes=[('R', h, dc)])
                        else:
                            S.add('act', lambda e, Rs=Rs, gC=gC: e.mul(Rs, Rs, gC), reads=[('R', h, dc)], writes=[('R', h, dc)])
                            S.add('dve', lambda e, kvb=kvb, Rs=Rs, gC=gC: e.scalar_tensor_tensor(Rs, ps[kvb][:, :512], gC, Rs, ALU.mult, ALU.add), reads=[('ps', kvb), ('R', h, dc)], writes=[('R', h, dc)])
                        S.add('act', lambda e, Rs=Rs, h=h, dc=dc: e.copy(Rb[:, h * 2 + dc, :], Rs), reads=[('R', h, dc)], writes=[('Rb', h, dc)])
                k2 = k2i % 2
                k2i += 1
                ssa = sm[:, k2:k2 + 1]
                rsa = sm[:, 2 + k2:3 + k2]
                S.add('act', lambda e, ob_=ob_, ssa=ssa: e.activation(junk[:], ps[ob_][:, :512], AF.Square, accum_out=ssa), reads=[('ps', ob_)], writes=[('ssb', k2)])
                rstd_ops(C, ssa, rsa, 512, [('ssb', k2)], ('b', k2))
                S.add('dve', lambda e, ob_=ob_, k2=k2, rsa=rsa, h=h: e.scalar_tensor_tensor(tmp[k2][:], ps[ob_][:, :512], rsa, gnw[:, h * 512:(h + 1) * 512], ALU.mult, ALU.mult),
                      reads=[('ps', ob_), ('rs', ('b', k2)), 'bgnw'], writes=[('tmp', k2)])
                ko = (k2i - 1) % 4
                S.add('dve', lambda e, k2=k2, ko=ko, h=h, s=s: e.tensor_tensor(obf[ko][:], tmp[k2][:], gc_[s][:, h * 512:(h + 1) * 512], ALU.mult), reads=[('tmp', k2), ('gc', s)], writes=[('obf', ko)])

                def tail(ko=ko, h=h, c=c, oslot=oslot, par=(k2i - 1) % 2):
                    tbk = 6 + par
                    for cc in range(4):
                        S.add('pe', lambda e, tbk=tbk, cc=cc, ko=ko: e.transpose(psb[tbk][:, cc * 128:(cc + 1) * 128], obf[ko][:, cc * 128:(cc + 1) * 128], C.ident[:]), reads=[('obf', ko)], writes=[('ps', tbk)])
                    dst = ost[oslot][:, h * 4:(h + 1) * 4, (c % 2) * 128:(c % 2 + 1) * 128]
                    srcv = psb[tbk][:, 0:512].rearrange("p (c t) -> p c t", c=4)
                    if par == 0:
                        S.add('act', lambda e, dst=dst, srcv=srcv: e.copy(dst, srcv), reads=[('ps', tbk)], writes=[('ost', oslot, c % 2, h)])
                    else:
                        S.add('dve', lambda e, dst=dst, srcv=srcv: e.tensor_copy(dst, srcv), reads=[('ps', tbk)], writes=[('ost', oslot, c % 2, h)])

                rdq.push(tail)
                rdq.tick()
        if c % 2 == 1:
            rdq.push(lambda c=c, oslot=oslot: S.dma(oTv[:, :, (c - 1) * 128:(c + 1) * 128], ost[oslot][:], reads=[('ost', oslot, cc, h) for cc in range(2) for h in range(8)]))
    rdq.flush()
    A.pop()
    S.barrier()
    phase_outproj(C, oT, C.Wb['b_w_out'], 4096, x_src, x_dst)


def emit_T16(C, hsrc, rtoks, stt, j, toks, tag):
    S, psb = C.S, C.psb
    for half in range(2):
        bank = 6 + half
        for c in range(8):
            cc = half * 8 + c
            S.add('pe', lambda e, c=c, cc=cc, bank=bank: e.transpose(psb[bank][:, c * 128:(c + 1) * 128], hsrc[:, cc * 128:(cc + 1) * 128], C.ident[:]),
                  reads=rtoks, writes=[('ps', bank)])
        dst = stt[:, half * 8:(half + 1) * 8, j * 128:(j + 1) * 128]
        srcv = psb[bank][:, 0:1024].rearrange("p (c t) -> p c t", c=8)
        tk = (tag, j, half)
        toks.append(tk)
        if half == 0:
            S.add('act', lambda e, dst=dst, srcv=srcv: e.copy(dst, srcv), reads=[('ps', bank)], writes=[tk])
        else:
            S.add('dve', lambda e, dst=dst, srcv=srcv: e.tensor_copy(dst, srcv), reads=[('ps', bank)], writes=[tk])


def layer_dil(C, I, hT, x_src, x_dst, layer_idx):
    nc, S, A, ps, psb = C.nc, C.S, C.A, C.ps, C.psb
    qkT = nc.dram_tensor("d_qkT", [3, 32, 128, SL], BF16).ap()
    vv = nc.dram_tensor("d_v", [3, SL, D], BF16).ap()
    Og = nc.dram_tensor("d_Og", [3, SL, 16, 129], F32).ap()
    oT = nc.dram_tensor("d_oT", [D, SL], BF16).ap()
    specs = [('q', 4 * i) for i in range(4)] + [('k', 16 + 4 * i) for i in range(4)] + [('v', 512 * i) for i in range(4)]
    for g3 in range(3):
        A.push()
        wq = load_bcast(C, I['d_q_norm_w'][g3], 128, 'dwq')
        wk = load_bcast(C, I['d_k_norm_w'][g3], 128, 'dwk')
        epi, on_block, end_block, fin = make_qkv_epi(C, specs, qkT[g3], vv[g3], wq, wk)
        phase_linear(C, hT, C.Wb['d_w_in'], D, [(g3 * 6144 + i * 512, 512) for i in range(12)], epi, on_block=on_block, end_block=end_block, finish=fin)
        A.pop()
    A.push()
    qTh = [A.tile([128, SL], BF16, 'dqT') for _ in range(2)]
    kTh = [A.tile([128, SL], BF16, 'dkT') for _ in range(2)]
    Vh = [A.tile([128, 32, 129], BF16, 'dVh') for _ in range(2)]
    E = [A.tile([128, 512], BF16, 'dE') for _ in range(3)]
    Ost = [A.tile([128, 32, 129], F32, 'dOst') for _ in range(2)]
    for s in range(2):
        S.add('pool', lambda e, s=s: e.memset(Vh[s][:, :, 128:129], 1.0), writes=[('Vone', s)])
    scale = 128 ** -0.5
    it = 0
    sti = 0
    ei = 0
    oi = 0
    for g3, dil in enumerate([1, 4, 16]):
        nblk = SL // dil // 128
        for h in range(16):
            s = it % 2
            it += 1
            S.dma(qTh[s][:], qkT[g3, h], writes=[('qT', s)])
            S.dma(kTh[s][:], qkT[g3, 16 + h], writes=[('kT', s)])
            vsrc = vv[g3][:, h * 128:(h + 1) * 128].rearrange("(nb a r) d -> a r nb d", a=128, r=dil)
            for r in range(dil):
                S.dma(Vh[s][:, r * nblk:(r + 1) * nblk, 0:128], vsrc[:, r, :, :], writes=[('V', s, r)])
            vtoks = [('V', s, r) for r in range(dil)] + [('Vone', s)]
            qv = qTh[s][:].rearrange("d (nb a r) -> d r nb a", a=128, r=dil)
            kv = kTh[s][:].rearrange("d (nb a r) -> d r nb a", a=128, r=dil)
            units = [(r, nb0) for r in range(dil) for nb0 in range(0, nblk, 2)]

            def emit_st(u, s=s, kv=kv, qv=qv):
                nonlocal sti
                r, nb0 = units[u]
                sb = sti % 3
                sti += 1
                for uu in range(2):
                    nb = nb0 + uu
                    if nb > 0:
                        S.add('pe', lambda e, sb=sb, uu=uu, r=r, nb=nb: e.matmul(ps[sb][:, (2 * uu) * 128:(2 * uu + 1) * 128], kv[:, r, nb - 1, :], qv[:, r, nb, :], start=True, stop=True),
                              reads=[('kT', s), ('qT', s)], writes=[('ps', sb)])
                    S.add('pe', lambda e, sb=sb, uu=uu, r=r, nb=nb: e.matmul(ps[sb][:, (2 * uu + 1) * 128:(2 * uu + 2) * 128], kv[:, r, nb, :], qv[:, r, nb, :], start=True, stop=True),
                          reads=[('kT', s), ('qT', s)], writes=[('ps', sb)])
                return sb

            pend = emit_st(0)
            for ui, (r, nb0) in enumerate(units):
                sb = pend
                if ui + 1 < len(units):
                    pend = emit_st(ui + 1)
                k = ei % 3
                ei += 1
                S.add('act', lambda e, sb=sb, k=k: e.activation(E[k][:], ps[sb][:, :512], AF.Exp, scale=scale), reads=[('ps', sb)], writes=[('E', k)])
                S.add('dve', lambda e, k=k: e.tensor_tensor(E[k][:], E[k][:], C.mask4[:].rearrange("p a b -> p (a b)"), ALU.mult), reads=[('E', k)], writes=[('E', k)])
                ob = 4 + (oi % 2)
                oi += 1
                for u in range(2):
                    nb = nb0 + u
                    tau = r * nblk + nb
                    if nb > 0:
                        S.add('pe', lambda e, ob=ob, u=u, k=k, tau=tau, s=s: e.matmul(ps[ob][:, u * 129:(u + 1) * 129], E[k][:, (2 * u) * 128:(2 * u + 1) * 128], Vh[s][:, tau - 1, :], start=True, stop=False),
                              reads=[('E', k)] + vtoks, writes=[('ps', ob)])
                    S.add('pe', lambda e, ob=ob, u=u, k=k, tau=tau, nb=nb, s=s: e.matmul(ps[ob][:, u * 129:(u + 1) * 129], E[k][:, (2 * u + 1) * 128:(2 * u + 2) * 128], Vh[s][:, tau, :], start=(nb == 0), stop=True),
                          reads=[('E', k)] + vtoks, writes=[('ps', ob)])
                tau0 = r * nblk + nb0
                dst = Ost[s][:, tau0:tau0 + 2, :].rearrange("p a b -> p (a b)")
                if oi % 2 == 0:
                    S.add('act', lambda e, ob=ob, dst=dst: e.copy(dst, ps[ob][:, 0:258]), reads=[('ps', ob)], writes=[('Ost', s, r)])
                else:
                    S.add('dve', lambda e, ob=ob, dst=dst: e.tensor_copy(dst, ps[ob][:, 0:258]), reads=[('ps', ob)], writes=[('Ost', s, r)])
            odst = Og[g3][:, h, :].rearrange("(nb b r) c -> b r nb c", b=128, r=dil)
            for r in range(dil):
                S.dma(odst[:, r, :, :], Ost[s][:, r * nblk:(r + 1) * nblk, :], reads=[('Ost', s, r)])
    A.pop()
    S.barrier()
    A.push()
    Ot = [[A.tile([128, 16, 129], F32, 'dOt') for _ in range(3)] for _ in range(2)]
    rl = A.tile([128, 2, 16], F32, 'drl')
    ob_ = [A.tile([128, D], BF16, 'dob') for _ in range(2)]
    stt = [A.tile([128, 16, 512], BF16, 'dst') for _ in range(2)]
    oTv = oT.rearrange("(c p) t -> p c t", p=128)
    for g in range(8):
        gs = g % 2
        toks = []
        for j in range(4):
            t = g * 4 + j
            s = t % 2
            for g3 in range(3):
                S.dma(Ot[s][g3][:], Og[g3][t * 128:(t + 1) * 128], writes=[('Ot', s, g3)])
            a = Ot[s][0]
            S.add('pool', lambda e, s=s, a=a: e.tensor_tensor(a[:], a[:], Ot[s][1][:], ALU.add), reads=[('Ot', s, 0), ('Ot', s, 1)], writes=[('Ot', s, 0)])
            S.add('dve', lambda e, s=s, a=a: e.tensor_tensor(a[:], a[:], Ot[s][2][:], ALU.add), reads=[('Ot', s, 0), ('Ot', s, 2)], writes=[('Ot', s, 0)])
            S.add('dve', lambda e, s=s, a=a: e.reciprocal(rl[:, s, :], a[:, :, 128]), reads=[('Ot', s, 0)], writes=[('rl', s)])
            S.add('dve', lambda e, s=s, a=a: e.tensor_tensor(ob_[s][:].rearrange("p (h d) -> p h d", h=16), a[:, :, 0:128], bc(rl[:, s, :], [128, 16, 128], 2), ALU.mult),
                  reads=[('Ot', s, 0), ('rl', s)], writes=[('ob', s)])
            emit_T16(C, ob_[s], [('ob', s)], stt[gs], j, toks, ('dst', gs))
        S.dma(oTv[:, :, g * 512:(g + 1) * 512], stt[gs][:], reads=toks)
    A.pop()
    S.barrier()
    phase_outproj(C, oT, C.Wb['d_w_out'], D, x_src, x_dst)


def layer_ssd(C, I, hT, x_src, x_dst, layer_idx):
    nc, S, A, ps, psb = C.nc, C.S, C.A, C.ps, C.psb
    kd = "ExternalOutput" if getattr(C, 'dbg', False) else "Internal"
    szd = nc.dram_tensor("c_sz", [SL, 4096], BF16, kind=kd).ap()
    xstm = nc.dram_tensor("c_xstm", [SL, 4096], BF16, kind=kd).ap()
    Btm = nc.dram_tensor("c_Btm", [SL, 1024], BF16, kind=kd).ap()
    bcT = nc.dram_tensor("c_bcT", [2048, SL], BF16, kind=kd).ap()
    acsC = nc.dram_tensor("c_acsC", [16, 64 * 256], F32, kind=kd).ap()
    dttm = nc.dram_tensor("c_dttm", [SL, 64], F32, kind=kd).ap()
    acstm = nc.dram_tensor("c_acstm", [SL, 64], F32, kind=kd).ap()
    oT = nc.dram_tensor("c_oT", [4096, SL], BF16, kind=kd).ap()
    W = C.Wb['c_w_in']
    A.push()
    vst = [A.tile([128, 8, 512], BF16, 'cvst') for _ in range(2)]
    st = {'blk': 0}

    def on_block(tb, bi):
        st['slot'] = st['blk'] % 2
        st['blk'] += 1

    def epi(tb, bi, ti, bank, nsz):
        slot = st['slot']
        S.add('act', lambda e: e.activation(vst[slot][:, ti, :], ps[bank][:, :512], AF.Silu), reads=[('ps', bank)], writes=[('vst', slot, ti)])

    def end_block(tb, bi):
        slot = st['slot']
        dst = szd[tb * TB:(tb + 1) * TB, bi * 512:(bi + 1) * 512].rearrange("(t p) n -> p t n", p=128)
        S.dma(dst, vst[slot][:], reads=[('vst', slot, ti) for ti in range(8)])

    phase_linear(C, hT, W, D, [(i * 512, 512) for i in range(8)], epi, on_block=on_block, end_block=end_block)
    A.pop()
    A.push()
    dtT = A.tile([64, SL], F32, 'cdtT')
    cw = A.tile([128, 48, 4], F32, 'ccw')
    cb = A.tile([128, 48], F32, 'ccb')
    dtb = A.tile([64, 1], F32, 'cdtb')
    halo = A.tile([128, 48, 3], F32, 'chalo')
    S.dma(cw[:], I['c_conv_wT'].rearrange("(c p) k -> p c k", p=128), writes=['ccw'])
    S.dma(cb[:], I['c_conv_bT'], writes=['ccb'])
    S.dma(dtb[:], I['c_dt_bias'].rearrange("(p o) -> p o", o=1), writes=['cdtb'])
    S.add('pool', lambda e: e.memset(halo[:], 0.0), writes=[('halo', i) for i in range(48)])
    xr = [A.tile([128, 515], F32, 'cxr') for _ in range(2)]
    acc = [A.tile([128, 512], F32, 'cacc') for _ in range(2)]
    so = [A.tile([128, 512], BF16, 'cso') for _ in range(4)]
    dq = Deferred()
    tms = [A.tile([128, 8, 128], BF16, 'ctms') for _ in range(2)]
    e1 = A.tile([64, 512], F32, 'ce1')
    st2 = {'k': 0, 'u': 0}
    bcTv = bcT

    def epi_fm(tb, bi, j, th, bank, msz):
        if bi == 12:
            cols = slice(tb * TB + th * 512, tb * TB + (th + 1) * 512)
            S.add('act', lambda e: e.activation(e1[:], ps[bank][:64, :512], AF.Exp, bias=dtb[:, 0:1]), reads=[('ps', bank), 'cdtb'], writes=['ce1'])
            S.add('act', lambda e: e.activation(dtT[:, cols], e1[:], AF.Ln, bias=1.0), reads=['ce1'], writes=[('dtT', tb, th)])
            return
        cidx = bi * 4 + j
        cnt = st2['k']
        st2['k'] += 1
        k = cnt % 2
        ko = cnt % 4
        S.add('pool', lambda e: e.tensor_copy(xr[k][:, 0:3], halo[:, cidx, :]), reads=[('halo', cidx)], writes=[('xrh', k)])
        S.add('act', lambda e: e.copy(xr[k][:, 3:515], ps[bank][:, :512]), reads=[('ps', bank)], writes=[('xrb', k)])
        S.add('pool', lambda e: e.tensor_copy(halo[:, cidx, :], xr[k][:, 512:515]), reads=[('xrb', k), ('xrh', k)], writes=[('halo', cidx)])
        S.add('act', lambda e: e.mul(acc[k][:], xr[k][:, 0:512], cw[:, cidx, 0:1]), reads=[('xrh', k), ('xrb', k), 'ccw'], writes=[('acc', k)])
        for q in range(1, 4):
            S.add('dve', lambda e, q=q: e.scalar_tensor_tensor(acc[k][:], xr[k][:, q:q + 512], cw[:, cidx, q:q + 1], acc[k][:], ALU.mult, ALU.add),
                  reads=[('xrh', k), ('xrb', k), ('acc', k)], writes=[('acc', k)])
        S.add('act', lambda e: e.activation(so[ko][:], acc[k][:], AF.Silu, bias=cb[:, cidx:cidx + 1]), reads=[('acc', k), 'ccb'], writes=[('so', ko)])
        if cidx >= 32:
            r0 = (cidx - 32) * 128
            S.dma(bcTv[r0:r0 + 128, tb * TB + th * 512:tb * TB + (th + 1) * 512], so[ko][:], reads=[('so', ko)])
        if cidx < 40:
            if th == 0:
                st2['slot'] = st2['u'] % 2
                st2['u'] += 1
            slot = st2['slot']

            def tail():
                tbk = 6 + cnt % 2
                for c in range(4):
                    S.add('pe', lambda e, c=c: e.transpose(psb[tbk][:, c * 128:(c + 1) * 128], so[ko][:, c * 128:(c + 1) * 128], C.ident[:]), reads=[('so', ko)], writes=[('ps', tbk)])
                S.add('dve', lambda e: e.tensor_copy(tms[slot][:, th * 4:(th + 1) * 4, :], psb[tbk][:, 0:512].rearrange("p (c t) -> p c t", c=4)), reads=[('ps', tbk)], writes=[('tms', slot, th)])
                if th == 1:
                    rows = slice(tb * TB, (tb + 1) * TB)
                    if cidx < 32:
                        dst = xstm[rows, cidx * 128:(cidx + 1) * 128]
                    else:
                        dst = Btm[rows, (cidx - 32) * 128:(cidx - 31) * 128]
                    S.dma(dst.rearrange("(t p) n -> p t n", p=128), tms[slot][:], reads=[('tms', slot, 0), ('tms', slot, 1)])

            dq.push(tail)
        dq.tick()

    nblocks = [(4096 + i * 512, 512) for i in range(12)] + [(10240, 64)]
    phase_linear(C, hT, W, D, nblocks, epi_fm, fm=True, finish=dq.flush)
    alog = A.tile([64, 2], F32, 'calog')
    S.dma(alog[:, 0:1], I['c_a_log'].rearrange("(p o) -> p o", o=1), writes=['alog'])
    S.add('act', lambda e: e.activation(alog[:, 1:2], alog[:, 0:1], AF.Exp), reads=['alog'], writes=['aexp'])
    S.add('dve', lambda e: e.tensor_scalar(alog[:, 1:2], alog[:, 1:2], -1.0, None, ALU.mult), reads=['aexp'], writes=['aneg'])
    aT = A.tile([64, SL], F32, 'caT')
    acsT = A.tile([64, SL], F32, 'cacsT')
    onesf = A.tile([64, 256], F32, 'cones')
    S.add('pool', lambda e: e.memset(onesf[:], 1.0), writes=['cones'])
    S.add('dve', lambda e: e.tensor_scalar(aT[:], dtT[:], alog[:, 1:2], None, ALU.mult), reads=['aneg'], writes=['aT'])
    for c in range(16):
        cols = slice(c * 256, (c + 1) * 256)
        S.add('dve', lambda e, cols=cols: e.tensor_tensor_scan(acsT[:, cols], onesf[:], aT[:, cols], 0.0, ALU.mult, ALU.add), reads=['aT', 'cones'], writes=[('acsT', c)])
        S.dma(acsC[c].rearrange("(h i) -> h i", h=64), acsT[:, cols], reads=[('acsT', c)])
    tst = A.tile([128, 2, 32, 64], F32, 'ctst')
    for wi_, srcT in enumerate([dtT, acsT]):
        for t8 in range(4):
            bank = (wi_ * 4 + t8) % 4
            for q in range(8):
                t = t8 * 8 + q
                S.add('pe', lambda e, bank=bank, q=q, t=t, srcT=srcT: e.transpose(ps[bank][:, q * 64:(q + 1) * 64], srcT[:, t * 128:(t + 1) * 128], C.identf[:64, :64]),
                      reads=[('acsT', t // 2)], writes=[('ps', bank)])
            S.add('act', lambda e, bank=bank, wi_=wi_, t8=t8: e.copy(tst[:, wi_, t8 * 8:(t8 + 1) * 8, :], ps[bank][:, :512].rearrange("p (q h) -> p q h", q=8)),
                  reads=[('ps', bank)], writes=[('tst', wi_, t8)])
        dstd = dttm if wi_ == 0 else acstm
        S.dma(dstd.rearrange("(t p) h -> p t h", p=128), tst[:, wi_, :, :], reads=[('tst', wi_, t8) for t8 in range(4)])
    A.pop()
    S.barrier()
    A.push()
    nw = load_bcast(C, I['c_norm_w'], 4096, 'cnw')
    dsk = load_bcast(C, I['c_d_skip'], 64, 'cdsk')
    xs_c = [A.tile([128, 2, 4096], BF16, 'cxs')] * 2
    sz_c = A.tile([128, 2, 4096], BF16, 'csz')
    Bt_c = [A.tile([128, 2, 1024], BF16, 'cBt') for _ in range(2)]
    BT_c = [A.tile([128, 8, 256], BF16, 'cBT') for _ in range(2)]
    CT_c = [A.tile([128, 8, 256], BF16, 'cCT') for _ in range(2)]
    sm_c = [A.tile([128, 2, 2, 64], F32, 'csm') for _ in range(2)]
    alb = [A.tile([128, 64], F32, 'calb') for _ in range(2)]
    ar = [A.tile([128, 2048], F32, 'car') for _ in range(2)]
    eacs = A.tile([128, 2, 64], F32, 'ceacs')
    w2 = A.tile([128, 2, 64], F32, 'cw2')
    eal = A.tile([128, 64], F32, 'ceal')
    stf = A.tile([128, 8, 512], F32, 'cstf')
    stb = A.tile([128, 8, 512], BF16, 'cstb')
    xdt = [A.tile([128, 2, 512], BF16, 'cxdt') for _ in range(2)]
    xde = [A.tile([128, 2, 512], BF16, 'cxde') for _ in range(2)]
    Gsb = [A.tile([128, 384], F32, 'cG') for _ in range(2)]
    tLb = [A.tile([128, 8, 128], F32, 'ctLb') for _ in range(2)]
    M0 = [A.tile([128, 8, 256], BF16, 'cM0') for _ in range(2)]
    M1 = [A.tile([128, 8, 128], BF16, 'cM1') for _ in range(2)]
    yt = [A.tile([128, 512], F32, 'cyt') for _ in range(2)]
    t2 = [A.tile([128, 512], F32, 'ct2') for _ in range(2)]
    ob = [A.tile([128, 512], BF16, 'cob') for _ in range(2)]
    junk = A.tile([128, 512], BF16, 'cjunk')
    smn = A.tile([128, 4], F32, 'csmn')
    ost = A.tile([128, 32, 256], BF16, 'cost')
    BTv = bcT[0:1024, :].rearrange("(g p) t -> p g t", p=128)
    CTv = bcT[1024:2048, :].rearrange("(g p) t -> p g t", p=128)
    oTv = oT.rearrange("(c p) t -> p c t", p=128)
    gi = 0
    li_ = 0
    yi = 0
    for c in range(16):
        s = c % 2
        rows = slice(c * 256, (c + 1) * 256)
        S.dma(xs_c[s][:], xstm[rows, :].rearrange("(t p) n -> p t n", p=128), writes=[('xs', 0)])
        S.dma(sz_c[:], szd[rows, :].rearrange("(t p) n -> p t n", p=128), writes=['sz'])
        S.dma(Bt_c[s][:], Btm[rows, :].rearrange("(t p) n -> p t n", p=128), writes=[('Bt', s)])
        S.dma(BT_c[s][:], BTv[:, :, rows], writes=[('BT', s)])
        S.dma(CT_c[s][:], CTv[:, :, rows], writes=[('CT', s)])
        S.dma(sm_c[s][:, 0, :, :], dttm[rows, :].rearrange("(t p) h -> p t h", p=128), writes=[('dtc', s)])
        S.dma(sm_c[s][:, 1, :, :], acstm[rows, :].rearrange("(t p) h -> p t h", p=128), writes=[('acsc', s)])
        S.dma(alb[s][:], acstm[c * 256 + 255, :].partition_broadcast(128), writes=[('alb', s)])
        dt_c = sm_c[s][:, 0, :, :]
        acs_c = sm_c[s][:, 1, :, :]
        S.add('act', lambda e, acs_c=acs_c: e.activation(eacs[:], acs_c, AF.Exp), reads=[('acsc', s)], writes=['eacs'])
        S.add('dve', lambda e, acs_c=acs_c, s=s: e.tensor_tensor(w2[:], bc(alb[s][:], [128, 2, 64], 1), acs_c, ALU.subtract), reads=[('alb', s), ('acsc', s)], writes=['w2'])
        S.add('act', lambda e: e.activation(w2[:], w2[:], AF.Exp), reads=['w2'], writes=['w2'])
        S.add('dve', lambda e, dt_c=dt_c: e.tensor_tensor(w2[:], w2[:], dt_c, ALU.mult), reads=['w2', ('dtc', s)], writes=['w2'])
        S.add('act', lambda e, s=s: e.activation(eal[:], alb[s][:], AF.Exp), reads=[('alb', s)], writes=['eal'])
        stA = {}

        def stageA(g, c=c, s=s, dt_c=dt_c, acs_c=acs_c):
            nonlocal gi, li_
            sg = gi % 2
            k = gi % 2
            gi += 1
            S.dma(ar[sg][:], acsC[c, g * 2048:(g + 1) * 2048].partition_broadcast(128), writes=[('ar', sg)])
            gc = slice(g * 512, (g + 1) * 512)
            hs_ = slice(g * 8, (g + 1) * 8)
            xs4 = xs_c[s][:, :, gc].rearrange("p t (h q) -> p t h q", h=8)
            S.add('pool', lambda e, k=k, xs4=xs4, dt_c=dt_c, hs_=hs_: e.tensor_tensor(xdt[k][:].rearrange("p t (h q) -> p t h q", h=8), xs4, bc(dt_c[:, :, hs_], [128, 2, 8, 64], 3), ALU.mult),
                  reads=[('xs', 0), ('dtc', s)], writes=[('xdt', k)])
            S.add('pool', lambda e, k=k, xs4=xs4, hs_=hs_: e.tensor_tensor(xde[k][:].rearrange("p t (h q) -> p t h q", h=8), xs4, bc(w2[:, :, hs_], [128, 2, 8, 64], 3), ALU.mult),
                  reads=[('xs', 0), 'w2'], writes=[('xde', k)])
            S.add('pe', lambda e, s=s, g=g: e.matmul(ps[0][:, 0:256], BT_c[s][:, g, 0:128], CT_c[s][:, g, 0:256], start=True, stop=True), reads=[('BT', s), ('CT', s)], writes=[('ps', 0)])
            S.add('pe', lambda e, s=s, g=g: e.matmul(ps[0][:, 256:384], BT_c[s][:, g, 128:256], CT_c[s][:, g, 128:256], start=True, stop=True), reads=[('BT', s), ('CT', s)], writes=[('ps', 0)])
            S.add('dve', lambda e, k=k: e.tensor_tensor(Gsb[k][:], ps[0][:, 0:384], C.gmask[:], ALU.mult), reads=[('ps', 0)], writes=[('G', k)])
            kk = li_ % 2
            li_ += 1
            ar3 = ar[sg][:].rearrange("p (h i) -> p h i", h=8)
            acs0 = bc(acs_c[:, 0, hs_], [128, 8, 256], 2)
            acs1 = bc(acs_c[:, 1, hs_], [128, 8, 128], 2)
            S.add('dve', lambda e, kk=kk, ar3=ar3, acs1=acs1: e.tensor_tensor(tLb[kk][:], ar3[:, :, 128:256], acs1, ALU.subtract), reads=[('ar', sg), ('acsc', s)], writes=[('tLb', kk)])
            S.add('act', lambda e, kk=kk: e.activation(tLb[kk][:], tLb[kk][:], AF.Exp), reads=[('tLb', kk)], writes=[('tLb', kk)])
            S.add('dve', lambda e, kk=kk, k=k: e.scalar_tensor_tensor(M1[kk][:], tLb[kk][:], 1.0, bc(Gsb[k][:, 256:384], [128, 8, 128], 1), ALU.min, ALU.mult), reads=[('tLb', kk), ('G', k)], writes=[('M1', kk)])
            S.add('dve', lambda e, ar3=ar3, acs0=acs0: e.tensor_tensor(ar3, ar3, acs0, ALU.subtract), reads=[('ar', sg), ('acsc', s)], writes=[('ar', sg)])
            S.add('act', lambda e, sg=sg: e.activation(ar[sg][:], ar[sg][:], AF.Exp), reads=[('ar', sg)], writes=[('ar', sg)])
            S.add('dve', lambda e, kk=kk, k=k, ar3=ar3: e.scalar_tensor_tensor(M0[kk][:], ar3, 1.0, bc(Gsb[k][:, 0:256], [128, 8, 256], 1), ALU.min, ALU.mult), reads=[('ar', sg), ('G', k)], writes=[('M0', kk)])
            stA[g] = (k, kk, gc, hs_)

        stageA(0)
        for g in range(8):
            if g + 1 < 8:
                stageA(g + 1)
            k, kk, gc, hs_ = stA[g]
            if c > 0:
                for it in range(2):
                    S.add('pe', lambda e, it=it, s=s, g=g: e.matmul(ps[2 + it][:, :512], CT_c[s][:, g, it * 128:(it + 1) * 128], stb[:, g, :], start=True, stop=True),
                          reads=[('CT', s), ('stb', g)], writes=[('ps', 2 + it)])
            for hh in range(8):
                hq = slice(hh * 64, (hh + 1) * 64)
                S.add('pe', lambda e, kk=kk, k=k, hq=hq, hh=hh: e.matmul(ps[4][:, hq], M0[kk][:, hh, 0:128], xdt[k][:, 0, hq], start=True, stop=True), reads=[('M0', kk), ('xdt', k)], writes=[('ps', 4)])
                S.add('pe', lambda e, kk=kk, k=k, hq=hq, hh=hh: e.matmul(ps[5][:, hq], M0[kk][:, hh, 128:256], xdt[k][:, 0, hq], start=True, stop=False), reads=[('M0', kk), ('xdt', k)], writes=[('ps', 5)])
                S.add('pe', lambda e, kk=kk, k=k, hq=hq, hh=hh: e.matmul(ps[5][:, hq], M1[kk][:, hh, :], xdt[k][:, 1, hq], start=False, stop=True), reads=[('M1', kk), ('xdt', k)], writes=[('ps', 5)])
            for it in range(2):
                y2 = yi % 2
                yi += 1
                if c > 0:
                    S.add('dve', lambda e, it=it, y2=y2, hs_=hs_: e.tensor_tensor(yt[y2][:].rearrange("p (h q) -> p h q", h=8), ps[2 + it][:, :512].rearrange("p (h q) -> p h q", h=8), bc(eacs[:, it, hs_], [128, 8, 64], 2), ALU.mult),
                          reads=[('ps', 2 + it), 'eacs'], writes=[('yt', y2)])
                    S.add('dve', lambda e, it=it, y2=y2: e.tensor_tensor(yt[y2][:], ps[4 + it][:, :512], yt[y2][:], ALU.add), reads=[('ps', 4 + it), ('yt', y2)], writes=[('yt', y2)])
                else:
                    S.add('act', lambda e, it=it, y2=y2: e.copy(yt[y2][:], ps[4 + it][:, :512]), reads=[('ps', 4 + it)], writes=[('yt', y2)])
                S.add('pool', lambda e, it=it, y2=y2, s=s, gc=gc, hs_=hs_: e.tensor_tensor(t2[y2][:].rearrange("p (h q) -> p h q", h=8), xs_c[s][:, it, gc].rearrange("p (h q) -> p h q", h=8), bc(dsk[:, hs_], [128, 8, 64], 2), ALU.mult),
                      reads=[('xs', 0), 'cdsk'], writes=[('t2', y2)])
                S.add('pool', lambda e, y2=y2: e.tensor_tensor(yt[y2][:], yt[y2][:], t2[y2][:], ALU.add), reads=[('yt', y2), ('t2', y2)], writes=[('yt', y2)])
                S.add('dve', lambda e, it=it, y2=y2, gc=gc: e.tensor_tensor(yt[y2][:], yt[y2][:], sz_c[:, it, gc], ALU.mult), reads=[('yt', y2), 'sz'], writes=[('yt', y2)])
                ssa = smn[:, y2:y2 + 1]
                rsa = smn[:, 2 + y2:3 + y2]
                S.add('act', lambda e, y2=y2, ssa=ssa: e.activation(junk[:], yt[y2][:], AF.Square, accum_out=ssa), reads=[('yt', y2)], writes=[('ssc', y2)])
                rstd_ops(C, ssa, rsa, 512, [('ssc', y2)], ('c', y2))
                S.add('dve', lambda e, y2=y2, rsa=rsa, gc=gc: e.scalar_tensor_tensor(ob[y2][:], yt[y2][:], rsa, nw[:, gc], ALU.mult, ALU.mult), reads=[('yt', y2), ('rs', ('c', y2)), 'cnw'], writes=[('ob', y2)])
                tbk = 6 + y2
                for cc in range(4):
                    S.add('pe', lambda e, tbk=tbk, cc=cc, y2=y2: e.transpose(psb[tbk][:, cc * 128:(cc + 1) * 128], ob[y2][:, cc * 128:(cc + 1) * 128], C.ident[:]), reads=[('ob', y2)], writes=[('ps', tbk)])
                dst = ost[:, g * 4:(g + 1) * 4, it * 128:(it + 1) * 128]
                srcv = psb[tbk][:, 0:512].rearrange("p (c t) -> p c t", c=4)
                S.add('act', lambda e, dst=dst, srcv=srcv: e.copy(dst, srcv), reads=[('ps', tbk)], writes=[('ost', g, it)])
            if c < 15:
                for jt in range(2):
                    S.add('pe', lambda e, jt=jt, s=s, g=g, k=k: e.matmul(ps[1][:, :512], Bt_c[s][:, jt, g * 128:(g + 1) * 128], xde[k][:, jt, :], start=(jt == 0), stop=(jt == 1)),
                          reads=[('Bt', s), ('xde', k)], writes=[('ps', 1)])
                if c == 0:
                    S.add('act', lambda e, g=g: e.copy(stf[:, g, :], ps[1][:, :512]), reads=[('ps', 1)], writes=[('stf', g)])
                else:
                    S.add('pool', lambda e, g=g, hs_=hs_: e.tensor_tensor(stf[:, g, :].rearrange("p (h q) -> p h q", h=8), stf[:, g, :].rearrange("p (h q) -> p h q", h=8), bc(eal[:, hs_], [128, 8, 64], 2), ALU.mult),
                          reads=[('stf', g), 'eal'], writes=[('stf', g)])
                    S.add('dve', lambda e, g=g: e.tensor_tensor(stf[:, g, :], ps[1][:, :512], stf[:, g, :], ALU.add), reads=[('ps', 1), ('stf', g)], writes=[('stf', g)])
                S.add('act', lambda e, g=g: e.copy(stb[:, g, :], stf[:, g, :]), reads=[('stf', g)], writes=[('stb', g)])
        S.dma(oTv[:, :, rows], ost[:], reads=[('ost', g, it) for g in range(8) for it in range(2)])
    A.pop()
    S.barrier()
    phase_outproj(C, oT, C.Wb['c_w_out'], 4096, x_src, x_dst)


def kernel(**inputs):
    nc, _ = build((0, 1, 2, 3), True)
    in_maps = [make_in_map(inputs, b) for b in range(8)]
    res = run_bass_kernel_spmd(nc, in_maps, core_ids=list(range(8)))
    return np.stack([np.asarray(r['y'], dtype=np.float32) for r in res.results], 0)
```

```python
import numpy as np
from contextlib import ExitStack
import concourse.bass as bass
import concourse.mybir as mybir

F32 = mybir.dt.float32
BF16 = mybir.dt.bfloat16
AF = mybir.ActivationFunctionType
ALU = mybir.AluOpType
AX = mybir.AxisListType

SEM_CAP = 30000
N_DMA_SEMS = 8
SAME_ENG_SYNC = True


class Sched:
    def __init__(self, nc):
        self.nc = nc
        self.ops = []
        self.last_w = {}
        self.readers = {}
        self.since_barrier = []

    def add(self, eng, fn, reads=(), writes=(), dma=False):
        idx = len(self.ops)
        deps = set()
        for r in reads:
            w = self.last_w.get(r)
            if w is not None:
                deps.add(w)
        for r in writes:
            w = self.last_w.get(r)
            if w is not None:
                deps.add(w)
            deps.update(self.readers.get(r, ()))
        for r in reads:
            self.readers.setdefault(r, []).append(idx)
        for r in writes:
            self.last_w[r] = idx
            self.readers[r] = []
        deps.discard(idx)
        self.ops.append(dict(eng=eng, fn=fn, deps=deps, dma=dma, load=(dma and len(writes) > 0)))
        self.since_barrier.append(idx)
        return idx

    def dma(self, out, in_, reads=(), writes=(), q='sp', **kw):
        return self.add(q, lambda e: e.dma_start(out=out, in_=in_, **kw), reads, writes, dma=True)

    def barrier(self):
        lastc = {}
        prev = []
        for i in self.since_barrier:
            o = self.ops[i]
            if o['dma']:
                prev.append(i)
            elif o['fn'] is not None:
                lastc[o['eng']] = i
        prev += list(lastc.values())
        self.since_barrier = []
        self.last_w = {}
        self.readers = {}
        for eng in ['pe', 'act', 'dve', 'pool', 'sp']:
            idx = len(self.ops)
            self.ops.append(dict(eng=eng, fn=None, deps=set(prev), dma=False))
            self.since_barrier.append(idx)

    def emit(self):
        nc = self.nc
        ops = self.ops
        needed = [False] * len(ops)
        for i, op in enumerate(ops):
            for d in op['deps']:
                od = ops[d]
                if od['dma']:
                    needed[d] = True
                elif od['eng'] == op['eng'] and (od['eng'] == 'pe' or not SAME_ENG_SYNC):
                    continue
                else:
                    needed[d] = True
        order = {e: [] for e in ['pe', 'act', 'dve', 'pool', 'sp']}
        for i, op in enumerate(ops):
            order[op['eng']].append(i)
        for e in ('sp', 'pool'):
            out = []
            for i in order[e]:
                op = ops[i]
                pos = len(out)
                if op['dma'] and op.get('load'):
                    md = max(op['deps']) if op['deps'] else -1
                    while pos > 0:
                        p = out[pos - 1]
                        po = ops[p]
                        if po['dma'] and not po.get('load') and p > md:
                            pos -= 1
                        else:
                            break
                out.insert(pos, i)
            order[e] = out
        dma_n = {}
        dma_rr = {}
        events = [None] * len(ops)
        pre_wait = [None] * len(ops)
        for e, lst in order.items():
            cnt = 0
            for i in lst:
                op = ops[i]
                if op['dma']:
                    k = dma_rr.get(e, 0)
                    dma_rr[e] = (k + 1) % N_DMA_SEMS
                    key = ('dma', e, k)
                    n = dma_n.get(key, 0)
                    if n > 0:
                        pre_wait[i] = (key, 16 * n)
                    n += 1
                    if 16 * n > SEM_CAP:
                        raise RuntimeError('dma sem cap exceeded; raise N_DMA_SEMS')
                    dma_n[key] = n
                    events[i] = (key, 16 * n)
                elif needed[i]:
                    cnt += 1
                    epoch, val = divmod(cnt - 1, SEM_CAP)
                    events[i] = (('eng', e, epoch), val + 1)
        keys = []
        for ev in events:
            if ev is not None and ev[0] not in keys:
                keys.append(ev[0])
        self.stack = ExitStack()
        sems = {}
        for k in keys:
            sems[k] = self.stack.enter_context(nc.semaphore('_'.join(str(x) for x in k)))
        self.n_sems = len(keys)
        streams = {e: [] for e in order}
        n_wait = 0
        for e, lst in order.items():
            seen = {}
            for i in lst:
                op = ops[i]
                waits = {}
                if pre_wait[i] is not None:
                    k, v = pre_wait[i]
                    waits[k] = max(waits.get(k, 0), v)
                for d in op['deps']:
                    od = ops[d]
                    if not od['dma'] and od['eng'] == e and (e == 'pe' or not SAME_ENG_SYNC):
                        continue
                    ev = events[d]
                    assert ev is not None, (i, d)
                    k, v = ev
                    waits[k] = max(waits.get(k, 0), v)
                wl = []
                for k, v in waits.items():
                    if seen.get(k, 0) >= v:
                        continue
                    seen[k] = v
                    wl.append((sems[k], v))
                n_wait += len(wl)
                inc = None
                if events[i] is not None:
                    inc = (sems[events[i][0]], 16 if op['dma'] else 1)
                streams[e].append((wl, op['fn'], inc))
        self.n_wait = n_wait

        def run(engobj, lst):
            for wl, fn, inc in lst:
                for s, v in wl:
                    engobj.wait_ge(s, v)
                if fn is None:
                    continue
                ins = fn(engobj)
                if inc is not None:
                    ins.then_inc(inc[0], inc[1])

        with nc.Block() as block:
            @block.tensor
            def _(t):
                run(t, streams['pe'])

            @block.scalar
            def _(t):
                run(t, streams['act'])

            @block.vector
            def _(t):
                run(t, streams['dve'])

            @block.gpsimd
            def _(t):
                run(t, streams['pool'])

            @block.sync
            def _(t):
                run(t, streams['sp'])
        self.stack.close()


class SbufAlloc:
    def __init__(self, nc, base=16640, limit=229376 - 64):
        self.nc = nc
        self.base = base
        self.off = base
        self.limit = limit
        self.uid = 0
        self.marks = []

    def push(self):
        self.marks.append(self.off)

    def pop(self):
        self.off = self.marks.pop()

    def tile(self, shape, dtype, name='t'):
        esz = 4 if dtype == F32 else 2
        n = 1
        for s in shape[1:]:
            n *= s
        nbytes = (n * esz + 63) // 64 * 64
        self.uid += 1
        t = self.nc.alloc_sbuf_tensor_at(f'{name}_{self.uid}', list(shape), dtype, offset=self.off)
        self.off += nbytes
        assert self.off <= self.limit, (name, self.off)
        return t


import math
import numpy as np
from concourse.bass_utils import run_bass_kernel_spmd

SL = 4096
D = 2048
NT = 32
EPS = 1e-6
TB = 1024


class Ctx:
    pass


def bc(ap, shape, axis):
    return ap.unsqueeze(axis).to_broadcast(list(shape))


def setup_consts(C):
    nc, S, A = C.nc, C.S, C.A
    C.ps = [nc.alloc_psum_tensor(f"ps{i}", [128, 512], F32) for i in range(8)]
    C.psb = [p[:].bitcast(BF16) for p in C.ps]
    C.identf = A.tile([128, 128], F32, 'identf')
    C.ident = A.tile([128, 128], BF16, 'ident')
    C.mask4 = A.tile([128, 4, 128], BF16, 'mask4')
    C.mle4 = A.tile([128, 4, 128], BF16, 'mle4')
    C.mle = A.tile([128, 128], BF16, 'mle')
    C.gmask = A.tile([128, 384], F32, 'gmask')
    C.pad_ = A.tile([128, 128], F32, 'cpad')
    A.push()
    C.mle_f = A.tile([128, 128], F32, 'mlef')
    C.mge_f = A.tile([128, 128], F32, 'mgef')
    for t, pat, cm, op in [(C.identf, [[-1, 128]], 1, ALU.is_equal),
                           (C.mle_f, [[1, 128]], -1, ALU.is_ge),
                           (C.mge_f, [[-1, 128]], 1, ALU.is_ge)]:
        S.add('pool', lambda e, t=t: e.memset(t[:], 1.0), writes=[t.name])
        S.add('pool', lambda e, t=t, pat=pat, cm=cm, op=op: e.affine_select(t[:], t[:], pat, op, 0.0, base=0, channel_multiplier=cm),
              reads=[t.name], writes=[t.name])
    S.add('dve', lambda e: e.tensor_copy(C.ident[:], C.identf[:]), reads=[C.identf.name], writes=['c1'])
    S.add('dve', lambda e: e.tensor_copy(C.mle[:], C.mle_f[:]), reads=[C.mle_f.name], writes=['c2'])
    for k in range(4):
        src = C.mge_f if k % 2 == 0 else C.mle_f
        S.add('dve', lambda e, k=k, src=src: e.tensor_copy(C.mask4[:, k, :], src[:]), reads=[src.name], writes=[('c3', k)])
        S.add('dve', lambda e, k=k: e.tensor_copy(C.mle4[:, k, :], C.mle_f[:]), reads=[C.mle_f.name], writes=[('c4', k)])
    S.add('pool', lambda e: e.memset(C.gmask[:, 128:256], 1.0), writes=['c8'])
    S.add('dve', lambda e: e.tensor_copy(C.gmask[:, 0:128], C.mle_f[:]), reads=[C.mle_f.name], writes=['c9'])
    S.add('dve', lambda e: e.tensor_copy(C.gmask[:, 256:384], C.mle_f[:]), reads=[C.mle_f.name], writes=['c10'])
    S.barrier()
    A.pop()


class BgConv:
    def __init__(self, C):
        self.C = C
        self.jobs = []
        self.done = set()
        self.i = 0
        A = C.A
        self.f = [A.tile([128, 2048], F32, 'bgf') for _ in range(2)]
        self.b = [A.tile([128, 2048], BF16, 'bgb') for _ in range(2)]

    def add(self, name, src, dst):
        sv, dv = flat128(src), flat128(dst)
        F_ = sv.shape[1]
        steps = []
        for f0 in range(0, F_, 2048):
            steps.append((sv, dv, f0, min(2048, F_ - f0)))
        self.jobs.append([name, steps])

    def _one(self, eng='pool'):
        S = self.C.S
        name, steps = self.jobs[0]
        sv, dv, f0, fsz = steps.pop(0)
        s = self.i % 2
        self.i += 1
        f, b = self.f[s], self.b[s]
        S.dma(f[:, :fsz], sv[:, f0:f0 + fsz], writes=[('bgf', s)])
        if eng == 'act':
            S.add('act', lambda e: e.copy(b[:, :fsz], f[:, :fsz]), reads=[('bgf', s)], writes=[('bgb', s)])
        else:
            S.add(eng, lambda e: e.tensor_copy(b[:, :fsz], f[:, :fsz]), reads=[('bgf', s)], writes=[('bgb', s)])
        S.dma(dv[:, f0:f0 + fsz], b[:, :fsz], reads=[('bgb', s)])
        if not steps:
            self.done.add(name)
            self.jobs.pop(0)

    def step(self, n):
        for _ in range(n):
            if not self.jobs:
                return
            self._one()

    def require(self, name):
        if name in self.done:
            return
        k = 0
        while name not in self.done:
            self._one(['pool', 'act', 'dve'][k % 3])
            k += 1
        self.C.S.barrier()


def flat128(ap2d):
    return ap2d.rearrange("k n -> (k n)").rearrange("(p f) -> p f", p=128)


def phase_convert(C, pairs):
    S, A = C.S, C.A
    A.push()
    CH = 4096
    fs = [A.tile([128, CH], F32, 'cvf') for _ in range(3)]
    bs = [A.tile([128, CH], BF16, 'cvb') for _ in range(3)]
    i = 0
    for src, dst in pairs:
        sv, dv = flat128(src), flat128(dst)
        F_ = sv.shape[1]
        for f0 in range(0, F_, CH):
            fsz = min(CH, F_ - f0)
            s = i % 3
            eng = ['act', 'dve', 'pool'][i % 3]
            S.dma(fs[s][:, :fsz], sv[:, f0:f0 + fsz], writes=[('cf', s)])
            if eng == 'act':
                S.add('act', lambda e, s=s, fsz=fsz: e.copy(bs[s][:, :fsz], fs[s][:, :fsz]), reads=[('cf', s)], writes=[('cb', s)])
            else:
                S.add(eng, lambda e, s=s, fsz=fsz: e.tensor_copy(bs[s][:, :fsz], fs[s][:, :fsz]), reads=[('cf', s)], writes=[('cb', s)])
            S.dma(dv[:, f0:f0 + fsz], bs[s][:, :fsz], reads=[('cb', s)])
            i += 1
    A.pop()
    S.barrier()


def rstd_ops(C, ss, rs, n, reads, tag):
    S = C.S
    S.add('act', lambda e: e.activation(rs, ss, AF.Sqrt, bias=EPS, scale=1.0 / n), reads=reads, writes=[('rs', tag)])
    S.add('dve', lambda e: e.reciprocal(rs, rs), reads=[('rs', tag)], writes=[('rs', tag)])


def phase_norm(C, x_src, w_ap, hT):
    S, A, ps, psb = C.S, C.A, C.ps, C.psb
    A.push()
    wt = A.tile([128, D], F32, 'nw')
    S.dma(wt[:], w_ap.partition_broadcast(128), writes=['nw'])
    xs = [A.tile([128, D], F32, 'nx') for _ in range(3)]
    sq = A.tile([128, D], BF16, 'nsq')
    hs = [A.tile([128, D], BF16, 'nh') for _ in range(3)]
    st = [A.tile([128, 16, 512], BF16, 'nst') for _ in range(2)]
    sm = A.tile([128, 6], F32, 'nsm')
    hTv = hT.rearrange("(c p) t -> p c t", p=128)

    def stage_a(t):
        s = t % 3
        S.dma(xs[s][:], x_src[t * 128:(t + 1) * 128, :], writes=[('x', s)])
        S.add('act', lambda e, s=s: e.activation(sq[:], xs[s][:], AF.Square, accum_out=sm[:, s:s + 1]), reads=[('x', s)], writes=[('ss', s)])
        rstd_ops(C, sm[:, s:s + 1], sm[:, 3 + s:4 + s], D, [('ss', s)], ('n', s))
        S.add('dve', lambda e, s=s: e.scalar_tensor_tensor(hs[s][:], xs[s][:], sm[:, 3 + s:4 + s], wt[:], ALU.mult, ALU.mult),
              reads=[('x', s), ('rs', ('n', s)), 'nw'], writes=[('h', s)])

    toks = []
    stage_a(0)
    for t in range(NT):
        if t + 1 < NT:
            stage_a(t + 1)
        g, j = divmod(t, 4)
        gs = g % 2
        emit_T16(C, hs[t % 3], [('h', t % 3)], st[gs], j, toks, ('nst', gs))
        if j == 3:
            S.dma(hTv[:, :, g * 512:(g + 1) * 512], st[gs][:], reads=toks)
            toks = []
    A.pop()
    S.barrier()


def phase_linear(C, aT, Wb, K, nblocks, epi, on_tb=None, on_block=None, end_block=None, fm=False, lin_banks=(0, 1, 2, 3), finish=None, bg_steps=2):
    S, A, ps = C.S, C.A, C.ps
    KC = K // 128
    C.bg.require(C.Wn[id(Wb)])
    A.push()
    NA = 2 if KC == 16 else 1
    ats = [A.tile([128, KC, TB], BF16, 'lat') for _ in range(NA)]
    wt = [A.tile([128, KC, 512], BF16, 'lwt') for _ in range(2)]
    aTv = aT.rearrange("(c p) t -> p c t", p=128)
    Wv = Wb.rearrange("(c p) n -> p c n", p=128)
    C.lin_rr = 0
    wi = 0
    def load_at(tb):
        sa = tb % NA
        for kc in range(KC):
            S.dma(ats[sa][:, kc, :], aTv[:, kc, tb * TB:(tb + 1) * TB], writes=[('at', sa, kc)])

    load_at(0)
    for tb in range(SL // TB):
        sa = tb % NA
        at = ats[sa]
        if NA == 1 and tb > 0:
            load_at(tb)
        if NA == 2 and tb + 1 < SL // TB:
            load_at(tb + 1)
        if on_tb:
            on_tb(tb)
        for bi, (n0, nsz) in enumerate(nblocks):
            sw = wi % 2
            wi += 1
            S.dma(wt[sw][:, :, :nsz], Wv[:, :, n0:n0 + nsz], writes=[('wt', sw)])
            if on_block:
                on_block(tb, bi)
            if not fm:
                for ti in range(TB // 128):
                    bank = lin_banks[C.lin_rr % len(lin_banks)]
                    C.lin_rr += 1
                    for kc in range(KC):
                        S.add('pe', lambda e, bank=bank, kc=kc, ti=ti, sw=sw, nsz=nsz, at=at: e.matmul(ps[bank][:, :nsz], at[:, kc, ti * 128:(ti + 1) * 128], wt[sw][:, kc, :nsz], start=(kc == 0), stop=(kc == KC - 1)),
                              reads=[('at', sa, kc), ('wt', sw)], writes=[('ps', bank)])
                    epi(tb, bi, ti, bank, nsz)
            else:
                for j in range((nsz + 127) // 128):
                    msz = min(128, nsz - j * 128)
                    for th in range(TB // 512):
                        bank = lin_banks[C.lin_rr % len(lin_banks)]
                        C.lin_rr += 1
                        for kc in range(KC):
                            S.add('pe', lambda e, bank=bank, kc=kc, th=th, sw=sw, j=j, msz=msz, at=at: e.matmul(ps[bank][:msz, :512], wt[sw][:, kc, j * 128:j * 128 + msz], at[:, kc, th * 512:(th + 1) * 512], start=(kc == 0), stop=(kc == KC - 1)),
                                  reads=[('at', sa, kc), ('wt', sw)], writes=[('ps', bank)])
                        epi(tb, bi, j, th, bank, msz)
            C.bg.step(bg_steps)
            if end_block:
                end_block(tb, bi)
    if finish:
        finish()
    A.pop()
    S.barrier()


def phase_outproj(C, oT, Wb, K, x_src, x_dst):
    S, A, ps = C.S, C.A, C.ps
    A.push()
    xp = [A.tile([128, 8, 512], F32, 'xp') for _ in range(2)]
    st = {}

    def on_block(tb, bi):
        s = (tb * 4 + bi) % 2
        st['s'] = s
        src = x_src[tb * TB:(tb + 1) * TB, bi * 512:(bi + 1) * 512].rearrange("(t p) n -> p t n", p=128)
        S.dma(xp[s][:], src, writes=[('xp', s, ti) for ti in range(8)])

    def epi(tb, bi, ti, bank, nsz):
        s = st['s']
        S.add('dve', lambda e: e.tensor_tensor(xp[s][:, ti, :], ps[bank][:, :512], xp[s][:, ti, :], ALU.add),
              reads=[('ps', bank), ('xp', s, ti)], writes=[('xp', s, ti)])

    def end_block(tb, bi):
        s = st['s']
        dst = x_dst[tb * TB:(tb + 1) * TB, bi * 512:(bi + 1) * 512].rearrange("(t p) n -> p t n", p=128)
        S.dma(dst, xp[s][:], reads=[('xp', s, ti) for ti in range(8)])

    phase_linear(C, oT, Wb, K, [(i * 512, 512) for i in range(4)], epi, on_block=on_block, end_block=end_block, bg_steps=0)
    A.pop()


def phase_mlp(C, hT, W1b, W2b, x_io):
    S, A, ps = C.S, C.A, C.ps
    C.bg.require(C.Wn[id(W1b)])
    C.bg.require(C.Wn[id(W2b)])
    A.push()
    at = A.tile([128, 16, TB], BF16, 'mat')
    acc = A.tile([128, 8, D], F32, 'macc')
    w1 = [A.tile([128, 16, 512], BF16, 'mw1') for _ in range(2)]
    w2 = [A.tile([128, 4, D], BF16, 'mw2') for _ in range(2)]
    uT = [A.tile([128, 4, TB], BF16, 'muT') for _ in range(2)]
    rl = [A.tile([128, 512], F32, 'mrl')]
    hTv = hT.rearrange("(c p) t -> p c t", p=128)
    W1v = W1b.rearrange("(c p) n -> p c n", p=128)
    W2v = W2b.rearrange("(c p) n -> p c n", p=128)
    rr = 0
    ri = 0
    for tb in range(SL // TB):
        for kc in range(16):
            S.dma(at[:, kc, :], hTv[:, kc, tb * TB:(tb + 1) * TB], writes=[('at', kc)])
        for ti in range(8):
            r0 = tb * TB + ti * 128
            S.dma(acc[:, ti, :], x_io[r0:r0 + 128, :], writes=[('acc', ti, dc) for dc in range(4)])
        for fg in range(16):
            s = fg % 2
            S.dma(w1[s][:], W1v[:, :, fg * 512:(fg + 1) * 512], writes=[('w1', s)])
            S.dma(w2[s][:], W2v[:, fg * 4:(fg + 1) * 4, :], writes=[('w2', s)])
            C.bg.step(3)
            for fc in range(4):
                for th in range(2):
                    bank = rr % 4
                    rr += 1
                    for kc in range(16):
                        S.add('pe', lambda e, bank=bank, kc=kc, s=s, fc=fc, th=th: e.matmul(ps[bank][:, :512], w1[s][:, kc, fc * 128:(fc + 1) * 128], at[:, kc, th * 512:(th + 1) * 512], start=(kc == 0), stop=(kc == 15)),
                              reads=[('at', kc), ('w1', s)], writes=[('ps', bank)])
                    k = 0
                    ri += 1
                    S.add('act', lambda e, bank=bank, k=k: e.activation(rl[k][:], ps[bank][:, :512], AF.Relu), reads=[('ps', bank)], writes=[('rl', k)])
                    S.add('dve', lambda e, bank=bank, k=k, s=s, fc=fc, th=th: e.tensor_tensor(uT[s][:, fc, th * 512:(th + 1) * 512], ps[bank][:, :512], rl[k][:], ALU.mult),
                          reads=[('ps', bank), ('rl', k)], writes=[('uT', s, fc, th)])
            for ti in range(8):
                for dc in range(4):
                    bank = rr % 4
                    rr += 1
                    for fc in range(4):
                        S.add('pe', lambda e, bank=bank, s=s, fc=fc, ti=ti, dc=dc: e.matmul(ps[bank][:, :512], uT[s][:, fc, ti * 128:(ti + 1) * 128], w2[s][:, fc, dc * 512:(dc + 1) * 512], start=(fc == 0), stop=(fc == 3)),
                              reads=[('uT', s, fc, ti // 4), ('w2', s)], writes=[('ps', bank)])
                    S.add('dve', lambda e, bank=bank, ti=ti, dc=dc: e.tensor_tensor(acc[:, ti, dc * 512:(dc + 1) * 512], ps[bank][:, :512], acc[:, ti, dc * 512:(dc + 1) * 512], ALU.add),
                          reads=[('ps', bank), ('acc', ti, dc)], writes=[('acc', ti, dc)])
        for ti in range(8):
            r0 = tb * TB + ti * 128
            S.dma(x_io[r0:r0 + 128, :], acc[:, ti, :], reads=[('acc', ti, dc) for dc in range(4)])
    A.pop()
    S.barrier()


LAG = 2


class Deferred:
    def __init__(self):
        self.q = []
        self.u = 0

    def push(self, fn):
        self.q.append((self.u, fn))

    def tick(self):
        self.u += 1
        while self.q and self.u - self.q[0][0] >= LAG:
            self.q.pop(0)[1]()

    def flush(self):
        while self.q:
            self.q.pop(0)[1]()


def make_qkv_epi(C, specs, qkT, vdst, wq, wk):
    S, A, ps, psb = C.S, C.A, C.ps, C.psb
    sq = [A.tile([128, 512], F32, 'esq') for _ in range(3)]
    tn = [A.tile([128, 512], F32, 'etn') for _ in range(3)]
    qn = [A.tile([128, 512], BF16, 'eqn') for _ in range(4)]
    sm = A.tile([128, 24], F32, 'esm')
    qst = [A.tile([128, 4, TB], BF16, 'eqst') for _ in range(2)]
    vst = [A.tile([128, 8, 512], BF16, 'evst') for _ in range(2)]
    st = {'k': 0, 'blk': 0}
    dq = Deferred()
    qkTv = qkT.rearrange("m d t -> d m t")

    def on_block(tb, bi):
        st['slot'] = st['blk'] % 2
        st['blk'] += 1

    def epi(tb, bi, ti, bank, nsz):
        kind, a0 = specs[bi]
        slot = st['slot']
        if kind == 'v':
            S.add('act', lambda e: e.copy(vst[slot][:, ti, :], ps[bank][:, :512]), reads=[('ps', bank)], writes=[('vst', slot, ti)])
            dq.tick()
            return
        wn = wq if kind == 'q' else wk
        cnt = st['k']
        st['k'] += 1
        k = cnt % 3
        kq = cnt % 4
        S.add('act', lambda e: e.activation(sq[k][:], ps[bank][:, :512], AF.Square), reads=[('ps', bank)], writes=[('sq', k)])
        ssa = sm[:, k * 8:k * 8 + 4]
        rsa = sm[:, k * 8 + 4:k * 8 + 8]
        S.add('dve', lambda e: e.tensor_reduce(ssa, sq[k][:].rearrange("p (g d) -> p g d", g=4), AX.X, ALU.add), reads=[('sq', k)], writes=[('ss', k)])
        rstd_ops(C, ssa, rsa, 128, [('ss', k)], ('e', k))
        S.add('dve', lambda e: e.tensor_tensor(tn[k][:].rearrange("p (g d) -> p g d", g=4), ps[bank][:, :512].rearrange("p (g d) -> p g d", g=4), bc(rsa, [128, 4, 128], 2), ALU.mult),
              reads=[('ps', bank), ('rs', ('e', k))], writes=[('tn', k)])
        S.add('pool', lambda e: e.tensor_tensor(qn[kq][:].rearrange("p (g d) -> p g d", g=4), tn[k][:].rearrange("p (g d) -> p g d", g=4), bc(wn[:], [128, 4, 128], 1), ALU.mult),
              reads=[('tn', k)], writes=[('qn', kq)])

        def tail():
            tbk = 6 + cnt % 2
            for c in range(4):
                S.add('pe', lambda e, c=c: e.transpose(psb[tbk][:, c * 128:(c + 1) * 128], qn[kq][:, c * 128:(c + 1) * 128], C.ident[:]), reads=[('qn', kq)], writes=[('ps', tbk)])
            dst = qst[slot][:, :, ti * 128:(ti + 1) * 128]
            srcv = psb[tbk][:, 0:512].rearrange("p (c t) -> p c t", c=4)
            if cnt % 2 == 0:
                S.add('act', lambda e: e.copy(dst, srcv), reads=[('ps', tbk)], writes=[('qst', slot, ti)])
            else:
                S.add('dve', lambda e: e.tensor_copy(dst, srcv), reads=[('ps', tbk)], writes=[('qst', slot, ti)])

        dq.push(tail)
        dq.tick()

    def end_block(tb, bi):
        kind, a0 = specs[bi]
        slot = st['slot']
        if kind == 'v':
            dst = vdst[tb * TB:(tb + 1) * TB, a0:a0 + 512].rearrange("(t p) n -> p t n", p=128)
            S.dma(dst, vst[slot][:], reads=[('vst', slot, ti) for ti in range(8)])
        else:
            dq.push(lambda: S.dma(qkTv[:, a0:a0 + 4, tb * TB:(tb + 1) * TB], qst[slot][:], reads=[('qst', slot, ti) for ti in range(8)]))

    return epi, on_block, end_block, dq.flush


def load_bcast(C, ap1d, n, name):
    t = C.A.tile([128, n], F32, name)
    C.S.dma(t[:], ap1d.partition_broadcast(128), writes=[name])
    return t


def layer_diff(C, I, hT, x_src, x_dst, layer_idx):
    nc, S, A, ps, psb = C.nc, C.S, C.A, C.ps, C.psb
    lam_init = 0.8 - 0.6 * math.exp(-0.3 * layer_idx)
    qkT = nc.dram_tensor("a_qkT", [32, 128, SL], BF16).ap()
    vv = nc.dram_tensor("a_v", [SL, D], BF16).ap()
    oT = nc.dram_tensor("a_oT", [D, SL], BF16).ap()
    A.push()
    wq = load_bcast(C, I['a_q_norm_w'], 128, 'awq')
    wk = load_bcast(C, I['a_k_norm_w'], 128, 'awk')
    specs = [('q', 4 * i) for i in range(4)] + [('k', 16 + 4 * i) for i in range(4)] + [('v', 512 * i) for i in range(4)]
    epi, on_block, end_block, fin = make_qkv_epi(C, specs, qkT, vv, wq, wk)
    phase_linear(C, hT, C.Wb['a_w_in'], D, [(i * 512, 512) for i in range(12)], epi, on_block=on_block, end_block=end_block, finish=fin)
    A.pop()
    A.push()
    lt = A.tile([128, 4, 128], F32, 'alt')
    for i, nm in enumerate(['a_lambda_q1', 'a_lambda_k1', 'a_lambda_q2', 'a_lambda_k2']):
        S.dma(lt[:, i, :], I[nm].partition_broadcast(128), writes=[('lt', i)])
    lsm = A.tile([128, 8], F32, 'alsm')
    ljunk = A.tile([128, 128], F32, 'alj')
    for i in range(2):
        S.add('dve', lambda e, i=i: e.tensor_tensor(ljunk[:], lt[:, 2 * i, :], lt[:, 2 * i + 1, :], ALU.mult), reads=[('lt', 2 * i), ('lt', 2 * i + 1)], writes=['lj'])
        S.add('dve', lambda e, i=i: e.tensor_reduce(lsm[:, i:i + 1], ljunk[:], AX.X, ALU.add), reads=['lj'], writes=[('ld', i)])
        S.add('act', lambda e, i=i: e.activation(lsm[:, 2 + i:3 + i], lsm[:, i:i + 1], AF.Exp), reads=[('ld', i)], writes=[('le', i)])
    S.add('dve', lambda e: e.tensor_tensor(lsm[:, 4:5], lsm[:, 3:4], lsm[:, 2:3], ALU.subtract), reads=[('le', 0), ('le', 1)], writes=['l4'])
    S.add('dve', lambda e: e.tensor_scalar(lsm[:, 5:6], lsm[:, 4:5], -lam_init, None, ALU.add), reads=['l4'], writes=['neglam'])
    neglam = lsm[:, 5:6]
    wsub = load_bcast(C, I['a_subln_w'], 256, 'awsub')
    S.add('dve', lambda e: e.tensor_scalar(wsub[:], wsub[:], 1.0 - lam_init, None, ALU.mult), reads=['awsub'], writes=['awsub'])
    qT = [A.tile([128, 2, SL], BF16, 'aqT') for _ in range(2)]
    kT = [A.tile([128, 2, SL], BF16, 'akT') for _ in range(2)]
    Va = [A.tile([128, 32, 257], BF16, 'aVa') for _ in range(2)]
    E = [A.tile([128, 512], BF16, 'aE') for _ in range(5)]
    on0 = [A.tile([128, 256], F32, 'aon0') for _ in range(4)]
    dif = [A.tile([128, 256], F32, 'adif') for _ in range(2)]
    ob = [A.tile([128, 256], BF16, 'aob') for _ in range(2)]
    junk = A.tile([128, 256], BF16, 'ajunk')
    sm = A.tile([128, 16], F32, 'asm')
    ost = [A.tile([128, 2, 512], BF16, 'aost') for _ in range(2)]
    qkTv = qkT.rearrange("m d t -> d m t")
    oTv = oT.rearrange("(c p) t -> p c t", p=128)
    scale = 128 ** -0.5
    for s in range(2):
        S.add('pool', lambda e, s=s: e.memset(Va[s][:, :, 256:257], 1.0), writes=[('Vone', s)])
    ei = 0
    sti = 0
    di = 0
    bi_ = 0
    for hp in range(8):
        s = hp % 2
        S.dma(qT[s][:], qkTv[:, 2 * hp:2 * hp + 2, :], writes=[('qT', s)])
        S.dma(kT[s][:], qkTv[:, 16 + 2 * hp:16 + 2 * hp + 2, :], writes=[('kT', s)])
        S.dma(Va[s][:, :, 0:256], vv[:, hp * 256:(hp + 1) * 256].rearrange("(t p) n -> p t n", p=128), writes=[('V', s)])
        for qb in range(8):
            oslot = bi_ % 2
            bi_ += 1
            for m in range(2):
                nkt = 4 * (qb + 1)
                pend = {}

                def emit_st(kt, s=s, m=m, qb=qb):
                    nonlocal sti, ei
                    sb = 4 + (sti % 3)
                    sti += 1
                    S.add('pe', lambda e, sb=sb, kt=kt, s=s, m=m, qb=qb: e.matmul(ps[sb][:, :512], kT[s][:, m, kt * 128:(kt + 1) * 128], qT[s][:, m, qb * 512:(qb + 1) * 512], start=True, stop=True),
                          reads=[('kT', s), ('qT', s)], writes=[('ps', sb)])
                    pend[kt] = sb

                emit_st(0)
                if nkt > 1:
                    emit_st(1)
                for kt in range(nkt):
                    if kt + 2 < nkt:
                        emit_st(kt + 2)
                    sb = pend.pop(kt)
                    k = ei % 5
                    ei += 1
                    S.add('act', lambda e, sb=sb, k=k: e.activation(E[k][:], ps[sb][:, :512], AF.Exp, scale=scale), reads=[('ps', sb)], writes=[('E', k)])
                    jd = kt - 4 * qb
                    if jd >= 0:
                        S.add('pool', lambda e, k=k, jd=jd: e.tensor_tensor(E[k][:, jd * 128:(jd + 1) * 128], E[k][:, jd * 128:(jd + 1) * 128], C.mle[:], ALU.mult), reads=[('E', k)], writes=[('E', k)])
                    for j in range(4):
                        if jd > j:
                            continue
                        last_kt = 4 * qb + j
                        S.add('pe', lambda e, j=j, k=k, s=s, kt=kt, last_kt=last_kt: e.matmul(ps[j][:, :257], E[k][:, j * 128:(j + 1) * 128], Va[s][:, kt, :], start=(kt == 0), stop=(kt == last_kt)),
                              reads=[('E', k), ('V', s), ('Vone', s)], writes=[('ps', j)])
                for j in range(4):
                    rl_ = sm[:, j:j + 1]
                    S.add('dve', lambda e, j=j, rl_=rl_: e.reciprocal(rl_, ps[j][:, 256:257]), reads=[('ps', j)], writes=[('rl', j)])
                    if m == 0:
                        S.add('dve', lambda e, j=j, rl_=rl_: e.tensor_scalar(on0[j][:], ps[j][:, 0:256], rl_, None, ALU.mult), reads=[('ps', j), ('rl', j)], writes=[('on0', j)])
                    else:
                        d = di % 2
                        di += 1
                        rl2 = sm[:, 4 + j:5 + j]
                        S.add('dve', lambda e, rl2=rl2, rl_=rl_: e.tensor_tensor(rl2, rl_, neglam, ALU.mult), reads=[('rl', j), 'neglam'], writes=[('rl2', j)])
                        S.add('dve', lambda e, j=j, d=d, rl2=rl2: e.scalar_tensor_tensor(dif[d][:], ps[j][:, 0:256], rl2, on0[j][:], ALU.mult, ALU.add),
                              reads=[('ps', j), ('rl2', j), ('on0', j)], writes=[('dif', d)])
                        ssd = sm[:, 8 + d:9 + d]
                        rsd = sm[:, 10 + d:11 + d]
                        S.add('act', lambda e, d=d, ssd=ssd: e.activation(junk[:], dif[d][:], AF.Square, accum_out=ssd), reads=[('dif', d)], writes=[('ssd', d)])
                        rstd_ops(C, ssd, rsd, 256, [('ssd', d)], ('a', d))
                        S.add('dve', lambda e, d=d, rsd=rsd: e.scalar_tensor_tensor(ob[d][:], dif[d][:], rsd, wsub[:], ALU.mult, ALU.mult), reads=[('dif', d), ('rs', ('a', d)), 'awsub'], writes=[('ob', d)])
                        for c in range(2):
                            S.add('pe', lambda e, d=d, c=c: e.transpose(psb[7][:, c * 128:(c + 1) * 128], ob[d][:, c * 128:(c + 1) * 128], C.ident[:]), reads=[('ob', d)], writes=[('ps', 7)])
                        S.add('act', lambda e, oslot=oslot, j=j: e.copy(ost[oslot][:, :, j * 128:(j + 1) * 128], psb[7][:, 0:256].rearrange("p (c t) -> p c t", c=2)),
                              reads=[('ps', 7)], writes=[('ost', oslot, j)])
            S.dma(oTv[:, 2 * hp:2 * hp + 2, qb * 512:(qb + 1) * 512], ost[oslot][:], reads=[('ost', oslot, j) for j in range(4)])
    A.pop()
    S.barrier()
    phase_outproj(C, oT, C.Wb['a_w_out'], D, x_src, x_dst)


WSHAPES = {
    'a_w_in': (D, 3 * D), 'a_w_out': (D, D),
    'b_w_in': (D, 12288), 'b_w_out': (4096, D),
    'c_w_in': (D, 10304), 'c_w_out': (4096, D),
    'd_w_in': (D, 18432), 'd_w_out': (D, D),
}
SMALL = {
    'a_q_norm_w': [128], 'a_k_norm_w': [128], 'a_lambda_q1': [128], 'a_lambda_k1': [128], 'a_lambda_q2': [128],
    'a_lambda_k2': [128], 'a_subln_w': [256], 'b_gn_w': [4096], 'c_conv_wT': [6144, 4], 'c_conv_bT': [128, 48],
    'c_dt_bias': [64], 'c_a_log': [64], 'c_d_skip': [64], 'c_norm_w': [4096], 'd_q_norm_w': [3, 128], 'd_k_norm_w': [3, 128],
    'r_cos': [SL, 256], 'r_ssin': [SL, 256], 'r_qs': [128, 8], 'r_ks': [128, 8],
}
LAYER_W = {0: ['a_w_in', 'a_w_out'], 1: ['b_w_in', 'b_w_out'], 2: ['c_w_in', 'c_w_out'], 3: ['d_w_in', 'd_w_out']}
LAYER_S = {0: ['a_q_norm_w', 'a_k_norm_w', 'a_lambda_q1', 'a_lambda_k1', 'a_lambda_q2', 'a_lambda_k2', 'a_subln_w'],
           1: ['b_gn_w', 'r_cos', 'r_ssin', 'r_qs', 'r_ks'],
           2: ['c_conv_wT', 'c_conv_bT', 'c_dt_bias', 'c_a_log', 'c_d_skip', 'c_norm_w'],
           3: ['d_q_norm_w', 'd_k_norm_w']}


def build(layers=(0, 1, 2, 3), mlp=True, dbg=False):
    nc = bass.Bass("TRN2", target_bir_lowering=False)
    C = Ctx()
    C.dbg = dbg
    C.nc = nc
    C.S = Sched(nc)
    C.A = SbufAlloc(nc)
    I = {}

    def inp(name, shape):
        I[name] = nc.dram_tensor(name, list(shape), F32, kind="ExternalInput").ap()

    inp('x', [SL, D])
    inp('norm1_w', [4, D])
    inp('norm2_w', [4, D])
    y = nc.dram_tensor("y", [SL, D], F32, kind="ExternalOutput").ap()
    C.Wb = {}
    C.Wn = {}
    pairs = []
    for li in layers:
        for nm in LAYER_W[li]:
            inp(nm, WSHAPES[nm])
            dst = nc.dram_tensor(nm + '_bf', list(WSHAPES[nm]), BF16).ap()
            C.Wb[nm] = dst
            C.Wn[id(dst)] = nm
            pairs.append((I[nm], dst))
        for nm in LAYER_S[li]:
            inp(nm, SMALL[nm])
        if mlp:
            inp(f'mlp_w1_{li}', [D, 4 * D])
            inp(f'mlp_w2_{li}', [4 * D, D])
            for nm, shp in [(f'mlp_w1_{li}', [D, 4 * D]), (f'mlp_w2_{li}', [4 * D, D])]:
                dst = nc.dram_tensor(nm + '_bf', shp, BF16).ap()
                C.Wb[nm] = dst
                C.Wn[id(dst)] = nm
                pairs.append((I[nm], dst))
    setup_consts(C)
    C.bg = BgConv(C)
    for src, dst in pairs:
        C.bg.add(C.Wn[id(dst)], src, dst)
    hT = nc.dram_tensor("hT", [D, SL], BF16).ap()
    x_src = I['x']
    for li in layers:
        phase_norm(C, x_src, I['norm1_w'][li], hT)
        [layer_diff, layer_ret, layer_ssd, layer_dil][li](C, I, hT, x_src, y, li)
        x_src = y
        if mlp:
            phase_norm(C, y, I['norm2_w'][li], hT)
            phase_mlp(C, hT, C.Wb[f'mlp_w1_{li}'], C.Wb[f'mlp_w2_{li}'], y)
    C.S.emit()
    C.I = I
    return nc, C


def host_consts():
    out = {}
    theta = np.repeat(1.0 / (10000.0 ** np.linspace(0.0, 1.0, 128, dtype=np.float32)), 2).astype(np.float32)
    ang = (np.arange(SL, dtype=np.float32)[:, None] * theta[None, :]).astype(np.float32)
    cos = np.cos(ang.astype(np.float64)).astype(np.float32)
    sin = np.sin(ang.astype(np.float64)).astype(np.float32)
    ssin = sin.copy()
    ssin[:, 0::2] *= -1.0
    out['r_cos'] = cos
    out['r_ssin'] = ssin
    gam = 1.0 - 2.0 ** (-5.0 - np.arange(8, dtype=np.float64))
    i1 = np.arange(128, dtype=np.float64)[:, None] + 1.0
    out['r_qs'] = (gam[None, :] ** i1).astype(np.float32)
    out['r_ks'] = ((gam[None, :] ** (-i1)) * (256 ** -0.5)).astype(np.float32)
    return out


def make_in_map(inputs, b, layers=(0, 1, 2, 3), mlp=True):
    hc = host_consts()
    m = {'x': np.ascontiguousarray(inputs['x'][b]), 'norm1_w': np.asarray(inputs['norm1_w']), 'norm2_w': np.asarray(inputs['norm2_w'])}
    for li in layers:
        for nm in LAYER_W[li]:
            m[nm] = np.asarray(inputs[nm])
        for nm in LAYER_S[li]:
            if nm in hc:
                m[nm] = hc[nm]
            elif nm == 'c_conv_wT':
                m[nm] = np.ascontiguousarray(np.asarray(inputs['c_conv_w']).T)
            elif nm == 'c_conv_bT':
                m[nm] = np.ascontiguousarray(np.asarray(inputs['c_conv_b']).reshape(48, 128).T)
            else:
                m[nm] = np.asarray(inputs[nm])
        if mlp:
            m[f'mlp_w1_{li}'] = np.ascontiguousarray(inputs['mlp_w1'][li])
            m[f'mlp_w2_{li}'] = np.ascontiguousarray(inputs['mlp_w2'][li])
    return m


def layer_ret(C, I, hT, x_src, x_dst, layer_idx):
    nc, S, A, ps, psb = C.nc, C.S, C.A, C.ps, C.psb
    qT = nc.dram_tensor("b_qT", [D, SL], BF16).ap()
    kT = nc.dram_tensor("b_kT", [D, SL], BF16).ap()
    ktm = nc.dram_tensor("b_ktm", [SL, D], BF16).ap()
    vtm = nc.dram_tensor("b_vtm", [SL, 4096], BF16).ap()
    sgt = nc.dram_tensor("b_sgt", [SL, 4096], BF16).ap()
    oT = nc.dram_tensor("b_oT", [4096, SL], BF16).ap()
    qTv = qT.rearrange("(c p) t -> p c t", p=128)
    kTv = kT.rearrange("(c p) t -> p c t", p=128)
    oTv = oT.rearrange("(c p) t -> p c t", p=128)
    A.push()
    qs = A.tile([128, 8], F32, 'bqs')
    ks = A.tile([128, 8], F32, 'bks')
    S.dma(qs[:], I['r_qs'], writes=['bqs'])
    S.dma(ks[:], I['r_ks'], writes=['bks'])
    cst = A.tile([128, 8, 256], F32, 'bcos')
    sst = A.tile([128, 8, 256], F32, 'bsin')
    t1 = [A.tile([128, 512], F32, 'bt1') for _ in range(3)]
    t2 = [A.tile([128, 512], F32, 'bt2') for _ in range(3)]
    qn = [A.tile([128, 512], BF16, 'bqn') for _ in range(4)]
    dq = Deferred()
    qst = [A.tile([128, 4, TB], BF16, 'bqst') for _ in range(2)]
    vst = [A.tile([128, 8, 512], BF16, 'bvst') for _ in range(2)]
    st = {'k': 0, 'blk': 0}

    def on_tb(tb):
        S.dma(cst[:], I['r_cos'][tb * TB:(tb + 1) * TB, :].rearrange("(t p) n -> p t n", p=128), writes=['bcos'])
        S.dma(sst[:], I['r_ssin'][tb * TB:(tb + 1) * TB, :].rearrange("(t p) n -> p t n", p=128), writes=['bsin'])

    def on_block(tb, bi):
        st['slot'] = st['blk'] % 2
        st['blk'] += 1

    def epi(tb, bi, ti, bank, nsz):
        slot = st['slot']
        if bi >= 8:
            if bi < 16:
                S.add('act', lambda e: e.copy(vst[slot][:, ti, :], ps[bank][:, :512]), reads=[('ps', bank)], writes=[('vst', slot, ti)])
            else:
                S.add('act', lambda e: e.activation(vst[slot][:, ti, :], ps[bank][:, :512], AF.Silu), reads=[('ps', bank)], writes=[('vst', slot, ti)])
            dq.tick()
            return
        isq = bi < 4
        sc = qs if isq else ks
        scn = 'bqs' if isq else 'bks'
        cnt = st['k']
        st['k'] += 1
        k = cnt % 3
        kq = cnt % 4
        pv3 = ps[bank][:, :512].rearrange("p (h d) -> p h d", h=2)
        pv4 = ps[bank][:, :512].rearrange("p (h i two) -> p h i two", h=2, two=2)
        t14 = t2[k][:].rearrange("p (h i two) -> p h i two", h=2, two=2)
        cosb = bc(cst[:, ti, :], [128, 2, 256], 1)
        sn3 = sst[:, ti, :].rearrange("p (i two) -> p i two", two=2)
        S.add('dve', lambda e: e.tensor_tensor(t1[k][:].rearrange("p (h d) -> p h d", h=2), pv3, cosb, ALU.mult), reads=[('ps', bank), 'bcos'], writes=[('t1', k)])
        S.add('dve', lambda e: e.tensor_tensor(t14[:, :, :, 0], pv4[:, :, :, 1], bc(sn3[:, :, 0], [128, 2, 128], 1), ALU.mult), reads=[('ps', bank), 'bsin'], writes=[('t2a', k)])
        S.add('dve', lambda e: e.tensor_tensor(t14[:, :, :, 1], pv4[:, :, :, 0], bc(sn3[:, :, 1], [128, 2, 128], 1), ALU.mult), reads=[('ps', bank), 'bsin'], writes=[('t2b', k)])
        S.add('pool', lambda e: e.tensor_tensor(t1[k][:], t1[k][:], t2[k][:], ALU.add), reads=[('t1', k), ('t2a', k), ('t2b', k)], writes=[('t1', k)])
        for hh in range(2):
            h = (bi % 4) * 2 + hh
            S.add('act', lambda e, hh=hh, h=h: e.mul(qn[kq][:, hh * 256:(hh + 1) * 256], t1[k][:, hh * 256:(hh + 1) * 256], sc[:, h:h + 1]), reads=[('t1', k), scn], writes=[('qn', kq, hh)])
        if not isq:
            S.add('pool', lambda e: e.tensor_copy(vst[slot][:, ti, :], qn[kq][:]), reads=[('qn', kq, 0), ('qn', kq, 1)], writes=[('vst', slot, ti)])

        def tail():
            tbk = 6 + cnt % 2
            for c in range(4):
                S.add('pe', lambda e, c=c: e.transpose(psb[tbk][:, c * 128:(c + 1) * 128], qn[kq][:, c * 128:(c + 1) * 128], C.ident[:]), reads=[('qn', kq, c // 2)], writes=[('ps', tbk)])
            dst = qst[slot][:, :, ti * 128:(ti + 1) * 128]
            srcv = psb[tbk][:, 0:512].rearrange("p (c t) -> p c t", c=4)
            S.add('dve', lambda e: e.tensor_copy(dst, srcv), reads=[('ps', tbk)], writes=[('qst', slot, ti)])

        dq.push(tail)
        dq.tick()

    def end_block(tb, bi):
        slot = st['slot']
        rows = slice(tb * TB, (tb + 1) * TB)
        if bi < 8:
            dstT = (qTv if bi < 4 else kTv)[:, (bi % 4) * 4:(bi % 4) * 4 + 4, rows]
            dq.push(lambda: S.dma(dstT, qst[slot][:], reads=[('qst', slot, ti) for ti in range(8)]))
        if bi >= 4:
            if bi < 8:
                dst = ktm[rows, (bi - 4) * 512:(bi - 3) * 512]
            elif bi < 16:
                dst = vtm[rows, (bi - 8) * 512:(bi - 7) * 512]
            else:
                dst = sgt[rows, (bi - 16) * 512:(bi - 15) * 512]
            S.dma(dst.rearrange("(t p) n -> p t n", p=128), vst[slot][:], reads=[('vst', slot, ti) for ti in range(8)])

    phase_linear(C, hT, C.Wb['b_w_in'], D, [(i * 512, 512) for i in range(24)], epi, on_tb=on_tb, on_block=on_block, end_block=end_block, finish=dq.flush)
    A.pop()
    A.push()
    gnw = load_bcast(C, I['b_gn_w'], 4096, 'bgnw')
    qTc = [A.tile([128, 16, 128], BF16, 'bqTc') for _ in range(2)]
    kTc = [A.tile([128, 16, 128], BF16, 'bkTc') for _ in range(2)]
    kc_ = [A.tile([128, D], BF16, 'bkc') for _ in range(2)]
    vc_ = [A.tile([128, 4096], BF16, 'bvc') for _ in range(2)]
    gc_ = [A.tile([128, 4096], BF16, 'bgc') for _ in range(2)]
    R = A.tile([128, 16, 512], F32, 'bR')
    Rb = A.tile([128, 16, 512], BF16, 'bRb')
    SM = [A.tile([128, 512], BF16, 'bSM') for _ in range(2)]
    tmp = [A.tile([128, 512], F32, 'btmp') for _ in range(2)]
    obf = [A.tile([128, 512], BF16, 'bobf') for _ in range(4)]
    rdq = Deferred()
    junk = A.tile([128, 512], BF16, 'bjunk')
    sm = A.tile([128, 8], F32, 'bsm')
    ost = [A.tile([128, 32, 256], BF16, 'bost') for _ in range(2)]
    ki = 0
    k2i = 0
    def loads(c):
        s = c % 2
        cols = slice(c * 128, (c + 1) * 128)
        S.dma(qTc[s][:], qTv[:, :, cols], writes=[('qTc', s)])
        S.dma(kTc[s][:], kTv[:, :, cols], writes=[('kTc', s)])
        S.dma(kc_[s][:], ktm[cols, :], writes=[('kc', s)])
        S.dma(vc_[s][:], vtm[cols, :], writes=[('vc', s)])
        S.dma(gc_[s][:], sgt[cols, :], writes=[('gc', s)])

    def emit_ST(c, hg):
        s = c % 2
        b = hg % 2
        for hh in range(4):
            h = hg * 4 + hh
            for dc in range(2):
                S.add('pe', lambda e, b=b, hh=hh, h=h, dc=dc, s=s: e.matmul(ps[b][:, hh * 128:(hh + 1) * 128], kTc[s][:, h * 2 + dc, :], qTc[s][:, h * 2 + dc, :], start=(dc == 0), stop=(dc == 1)),
                      reads=[('kTc', s), ('qTc', s)], writes=[('ps', b)])
        return b

    loads(0)
    pend = emit_ST(0, 0)
    for c in range(32):
        s = c % 2
        oslot = (c // 2) % 2
        for hg in range(2):
            b = pend
            if hg == 0:
                pend = emit_ST(c, 1)
            elif c + 1 < 32:
                loads(c + 1)
                pend = emit_ST(c + 1, 0)
            k = ki % 2
            ki += 1
            S.add('dve', lambda e, b=b, k=k: e.tensor_tensor(SM[k][:], ps[b][:, :512], C.mle4[:].rearrange("p a b -> p (a b)"), ALU.mult), reads=[('ps', b)], writes=[('SM', k)])
            for hh in range(4):
                h = hg * 4 + hh
                gC = (1.0 - 2.0 ** (-5.0 - h)) ** 128
                ob_ = 2 + (h % 2)
                S.add('pe', lambda e, ob_=ob_, k=k, hh=hh, h=h, s=s, c=c: e.matmul(ps[ob_][:, :512], SM[k][:, hh * 128:(hh + 1) * 128], vc_[s][:, h * 512:(h + 1) * 512], start=True, stop=(c == 0)),
                      reads=[('SM', k), ('vc', s)], writes=[('ps', ob_)])
                if c > 0:
                    for dc in range(2):
                        S.add('pe', lambda e, ob_=ob_, h=h, dc=dc, s=s: e.matmul(ps[ob_][:, :512], qTc[s][:, h * 2 + dc, :], Rb[:, h * 2 + dc, :], start=False, stop=(dc == 1)),
                              reads=[('qTc', s), ('Rb', h, dc)], writes=[('ps', ob_)])
                if c < 31:
                    for dc in range(2):
                        kvb = 4 + dc
                        S.add('pe', lambda e, kvb=kvb, h=h, dc=dc, s=s: e.matmul(ps[kvb][:, :512], kc_[s][:, h * 256 + dc * 128:h * 256 + (dc + 1) * 128], vc_[s][:, h * 512:(h + 1) * 512], start=True, stop=True),
                              reads=[('kc', s), ('vc', s)], writes=[('ps', kvb)])
                        Rs = R[:, h * 2 + dc, :]
                        if c == 0:
                            S.add('dve', lambda e, kvb=kvb, Rs=Rs, gC=gC: e.tensor_scalar(Rs, ps[kvb][:, :512], gC, None, ALU.mult), reads=[('ps', kvb)], writes=[('R', h, dc)])
                        else:
                            S.add('act', lambda e, Rs=Rs, gC=gC: e.mul(Rs, Rs, gC), reads=[('R', h, dc)], writes=[('R', h, dc)])
                            S.add('dve', lambda e, kvb=kvb, Rs=Rs, gC=gC: e.scalar_tensor_tensor(Rs, ps[kvb][:, :512], gC, Rs, ALU.mult, ALU.add), reads=[('ps', kvb), ('R', h, dc)], writes=[('R', h, dc)])
                        S.add('act', lambda e, Rs=Rs, h=h, dc=dc: e.copy(Rb[:, h * 2 + dc, :], Rs), reads=[('R', h, dc)], writes=[('Rb', h, dc)])
                k2 = k2i % 2
                k2i += 1
                ssa = sm[:, k2:k2 + 1]
                rsa = sm[:, 2 + k2:3 + k2]
                S.add('act', lambda e, ob_=ob_, ssa=ssa: e.activation(junk[:], ps[ob_][:, :512], AF.Square, accum_out=ssa), reads=[('ps', ob_)], writes=[('ssb', k2)])
                rstd_ops(C, ssa, rsa, 512, [('ssb', k2)], ('b', k2))
                S.add('dve', lambda e, ob_=ob_, k2=k2, rsa=rsa, h=h: e.scalar_tensor_tensor(tmp[k2][:], ps[ob_][:, :512], rsa, gnw[:, h * 512:(h + 1) * 512], ALU.mult, ALU.mult),
                      reads=[('ps', ob_), ('rs', ('b', k2)), 'bgnw'], writes=[('tmp', k2)])
                ko = (k2i - 1) % 4
                S.add('dve', lambda e, k2=k2, ko=ko, h=h, s=s: e.tensor_tensor(obf[ko][:], tmp[k2][:], gc_[s][:, h * 512:(h + 1) * 512], ALU.mult), reads=[('tmp', k2), ('gc', s)], writes=[('obf', ko)])

                def tail(ko=ko, h=h, c=c, oslot=oslot, par=(k2i - 1) % 2):
                    tbk = 6 + par
                    for cc in range(4):
                        S.add('pe', lambda e, tbk=tbk, cc=cc, ko=ko: e.transpose(psb[tbk][:, cc * 128:(cc + 1) * 128], obf[ko][:, cc * 128:(cc + 1) * 128], C.ident[:]), reads=[('obf', ko)], writes=[('ps', tbk)])
                    dst = ost[oslot][:, h * 4:(h + 1) * 4, (c % 2) * 128:(c % 2 + 1) * 128]
                    srcv = psb[tbk][:, 0:512].rearrange("p (c t) -> p c t", c=4)
                    if par == 0:
                        S.add('act', lambda e, dst=dst, srcv=srcv: e.copy(dst, srcv), reads=[('ps', tbk)], writes=[('ost', oslot, c % 2, h)])
                    else:
                        S.add('dve', lambda e, dst=dst, srcv=srcv: e.tensor_copy(dst, srcv), reads=[('ps', tbk)], writes=[('ost', oslot, c % 2, h)])

                rdq.push(tail)
                rdq.tick()
        if c % 2 == 1:
            rdq.push(lambda c=c, oslot=oslot: S.dma(oTv[:, :, (c - 1) * 128:(c + 1) * 128], ost[oslot][:], reads=[('ost', oslot, cc, h) for cc in range(2) for h in range(8)]))
    rdq.flush()
    A.pop()
    S.barrier()
    phase_outproj(C, oT, C.Wb['b_w_out'], 4096, x_src, x_dst)


def emit_T16(C, hsrc, rtoks, stt, j, toks, tag):
    S, psb = C.S, C.psb
    for half in range(2):
        bank = 6 + half
        for c in range(8):
            cc = half * 8 + c
            S.add('pe', lambda e, c=c, cc=cc, bank=bank: e.transpose(psb[bank][:, c * 128:(c + 1) * 128], hsrc[:, cc * 128:(cc + 1) * 128], C.ident[:]),
                  reads=rtoks, writes=[('ps', bank)])
        dst = stt[:, half * 8:(half + 1) * 8, j * 128:(j + 1) * 128]
        srcv = psb[bank][:, 0:1024].rearrange("p (c t) -> p c t", c=8)
        tk = (tag, j, half)
        toks.append(tk)
        if half == 0:
            S.add('act', lambda e, dst=dst, srcv=srcv: e.copy(dst, srcv), reads=[('ps', bank)], writes=[tk])
        else:
            S.add('dve', lambda e, dst=dst, srcv=srcv: e.tensor_copy(dst, srcv), reads=[('ps', bank)], writes=[tk])


def layer_dil(C, I, hT, x_src, x_dst, layer_idx):
    nc, S, A, ps, psb = C.nc, C.S, C.A, C.ps, C.psb
    qkT = nc.dram_tensor("d_qkT", [3, 32, 128, SL], BF16).ap()
    vv = nc.dram_tensor("d_v", [3, SL, D], BF16).ap()
    Og = nc.dram_tensor("d_Og", [3, SL, 16, 129], F32).ap()
    oT = nc.dram_tensor("d_oT", [D, SL], BF16).ap()
    specs = [('q', 4 * i) for i in range(4)] + [('k', 16 + 4 * i) for i in range(4)] + [('v', 512 * i) for i in range(4)]
    for g3 in range(3):
        A.push()
        wq = load_bcast(C, I['d_q_norm_w'][g3], 128, 'dwq')
        wk = load_bcast(C, I['d_k_norm_w'][g3], 128, 'dwk')
        epi, on_block, end_block, fin = make_qkv_epi(C, specs, qkT[g3], vv[g3], wq, wk)
        phase_linear(C, hT, C.Wb['d_w_in'], D, [(g3 * 6144 + i * 512, 512) for i in range(12)], epi, on_block=on_block, end_block=end_block, finish=fin)
        A.pop()
    A.push()
    qTh = [A.tile([128, SL], BF16, 'dqT') for _ in range(2)]
    kTh = [A.tile([128, SL], BF16, 'dkT') for _ in range(2)]
    Vh = [A.tile([128, 32, 129], BF16, 'dVh') for _ in range(2)]
    E = [A.tile([128, 512], BF16, 'dE') for _ in range(3)]
    Ost = [A.tile([128, 32, 129], F32, 'dOst') for _ in range(2)]
    for s in range(2):
        S.add('pool', lambda e, s=s: e.memset(Vh[s][:, :, 128:129], 1.0), writes=[('Vone', s)])
    scale = 128 ** -0.5
    it = 0
    sti = 0
    ei = 0
    oi = 0
    for g3, dil in enumerate([1, 4, 16]):
        nblk = SL // dil // 128
        for h in range(16):
            s = it % 2
            it += 1
            S.dma(qTh[s][:], qkT[g3, h], writes=[('qT', s)])
            S.dma(kTh[s][:], qkT[g3, 16 + h], writes=[('kT', s)])
            vsrc = vv[g3][:, h * 128:(h + 1) * 128].rearrange("(nb a r) d -> a r nb d", a=128, r=dil)
            for r in range(dil):
                S.dma(Vh[s][:, r * nblk:(r + 1) * nblk, 0:128], vsrc[:, r, :, :], writes=[('V', s, r)])
            vtoks = [('V', s, r) for r in range(dil)] + [('Vone', s)]
            qv = qTh[s][:].rearrange("d (nb a r) -> d r nb a", a=128, r=dil)
            kv = kTh[s][:].rearrange("d (nb a r) -> d r nb a", a=128, r=dil)
            units = [(r, nb0) for r in range(dil) for nb0 in range(0, nblk, 2)]

            def emit_st(u, s=s, kv=kv, qv=qv):
                nonlocal sti
                r, nb0 = units[u]
                sb = sti % 3
                sti += 1
                for uu in range(2):
                    nb = nb0 + uu
                    if nb > 0:
                        S.add('pe', lambda e, sb=sb, uu=uu, r=r, nb=nb: e.matmul(ps[sb][:, (2 * uu) * 128:(2 * uu + 1) * 128], kv[:, r, nb - 1, :], qv[:, r, nb, :], start=True, stop=True),
                              reads=[('kT', s), ('qT', s)], writes=[('ps', sb)])
                    S.add('pe', lambda e, sb=sb, uu=uu, r=r, nb=nb: e.matmul(ps[sb][:, (2 * uu + 1) * 128:(2 * uu + 2) * 128], kv[:, r, nb, :], qv[:, r, nb, :], start=True, stop=True),
                          reads=[('kT', s), ('qT', s)], writes=[('ps', sb)])
                return sb

            pend = emit_st(0)
            for ui, (r, nb0) in enumerate(units):
                sb = pend
                if ui + 1 < len(units):
                    pend = emit_st(ui + 1)
                k = ei % 3
                ei += 1
                S.add('act', lambda e, sb=sb, k=k: e.activation(E[k][:], ps[sb][:, :512], AF.Exp, scale=scale), reads=[('ps', sb)], writes=[('E', k)])
                S.add('dve', lambda e, k=k: e.tensor_tensor(E[k][:], E[k][:], C.mask4[:].rearrange("p a b -> p (a b)"), ALU.mult), reads=[('E', k)], writes=[('E', k)])
                ob = 4 + (oi % 2)
                oi += 1
                for u in range(2):
                    nb = nb0 + u
                    tau = r * nblk + nb
                    if nb > 0:
                        S.add('pe', lambda e, ob=ob, u=u, k=k, tau=tau, s=s: e.matmul(ps[ob][:, u * 129:(u + 1) * 129], E[k][:, (2 * u) * 128:(2 * u + 1) * 128], Vh[s][:, tau - 1, :], start=True, stop=False),
                              reads=[('E', k)] + vtoks, writes=[('ps', ob)])
                    S.add('pe', lambda e, ob=ob, u=u, k=k, tau=tau, nb=nb, s=s: e.matmul(ps[ob][:, u * 129:(u + 1) * 129], E[k][:, (2 * u + 1) * 128:(2 * u + 2) * 128], Vh[s][:, tau, :], start=(nb == 0), stop=True),
                          reads=[('E', k)] + vtoks, writes=[('ps', ob)])
                tau0 = r * nblk + nb0
                dst = Ost[s][:, tau0:tau0 + 2, :].rearrange("p a b -> p (a b)")
                if oi % 2 == 0:
                    S.add('act', lambda e, ob=ob, dst=dst: e.copy(dst, ps[ob][:, 0:258]), reads=[('ps', ob)], writes=[('Ost', s, r)])
                else:
                    S.add('dve', lambda e, ob=ob, dst=dst: e.tensor_copy(dst, ps[ob][:, 0:258]), reads=[('ps', ob)], writes=[('Ost', s, r)])
            odst = Og[g3][:, h, :].rearrange("(nb b r) c -> b r nb c", b=128, r=dil)
            for r in range(dil):
                S.dma(odst[:, r, :, :], Ost[s][:, r * nblk:(r + 1) * nblk, :], reads=[('Ost', s, r)])
    A.pop()
    S.barrier()
    A.push()
    Ot = [[A.tile([128, 16, 129], F32, 'dOt') for _ in range(3)] for _ in range(2)]
    rl = A.tile([128, 2, 16], F32, 'drl')
    ob_ = [A.tile([128, D], BF16, 'dob') for _ in range(2)]
    stt = [A.tile([128, 16, 512], BF16, 'dst') for _ in range(2)]
    oTv = oT.rearrange("(c p) t -> p c t", p=128)
    for g in range(8):
        gs = g % 2
        toks = []
        for j in range(4):
            t = g * 4 + j
            s = t % 2
            for g3 in range(3):
                S.dma(Ot[s][g3][:], Og[g3][t * 128:(t + 1) * 128], writes=[('Ot', s, g3)])
            a = Ot[s][0]
            S.add('pool', lambda e, s=s, a=a: e.tensor_tensor(a[:], a[:], Ot[s][1][:], ALU.add), reads=[('Ot', s, 0), ('Ot', s, 1)], writes=[('Ot', s, 0)])
            S.add('dve', lambda e, s=s, a=a: e.tensor_tensor(a[:], a[:], Ot[s][2][:], ALU.add), reads=[('Ot', s, 0), ('Ot', s, 2)], writes=[('Ot', s, 0)])
            S.add('dve', lambda e, s=s, a=a: e.reciprocal(rl[:, s, :], a[:, :, 128]), reads=[('Ot', s, 0)], writes=[('rl', s)])
            S.add('dve', lambda e, s=s, a=a: e.tensor_tensor(ob_[s][:].rearrange("p (h d) -> p h d", h=16), a[:, :, 0:128], bc(rl[:, s, :], [128, 16, 128], 2), ALU.mult),
                  reads=[('Ot', s, 0), ('rl', s)], writes=[('ob', s)])
            emit_T16(C, ob_[s], [('ob', s)], stt[gs], j, toks, ('dst', gs))
        S.dma(oTv[:, :, g * 512:(g + 1) * 512], stt[gs][:], reads=toks)
    A.pop()
    S.barrier()
    phase_outproj(C, oT, C.Wb['d_w_out'], D, x_src, x_dst)


def layer_ssd(C, I, hT, x_src, x_dst, layer_idx):
    nc, S, A, ps, psb = C.nc, C.S, C.A, C.ps, C.psb
    kd = "ExternalOutput" if getattr(C, 'dbg', False) else "Internal"
    szd = nc.dram_tensor("c_sz", [SL, 4096], BF16, kind=kd).ap()
    xstm = nc.dram_tensor("c_xstm", [SL, 4096], BF16, kind=kd).ap()
    Btm = nc.dram_tensor("c_Btm", [SL, 1024], BF16, kind=kd).ap()
    bcT = nc.dram_tensor("c_bcT", [2048, SL], BF16, kind=kd).ap()
    acsC = nc.dram_tensor("c_acsC", [16, 64 * 256], F32, kind=kd).ap()
    dttm = nc.dram_tensor("c_dttm", [SL, 64], F32, kind=kd).ap()
    acstm = nc.dram_tensor("c_acstm", [SL, 64], F32, kind=kd).ap()
    oT = nc.dram_tensor("c_oT", [4096, SL], BF16, kind=kd).ap()
    W = C.Wb['c_w_in']
    A.push()
    vst = [A.tile([128, 8, 512], BF16, 'cvst') for _ in range(2)]
    st = {'blk': 0}

    def on_block(tb, bi):
        st['slot'] = st['blk'] % 2
        st['blk'] += 1

    def epi(tb, bi, ti, bank, nsz):
        slot = st['slot']
        S.add('act', lambda e: e.activation(vst[slot][:, ti, :], ps[bank][:, :512], AF.Silu), reads=[('ps', bank)], writes=[('vst', slot, ti)])

    def end_block(tb, bi):
        slot = st['slot']
        dst = szd[tb * TB:(tb + 1) * TB, bi * 512:(bi + 1) * 512].rearrange("(t p) n -> p t n", p=128)
        S.dma(dst, vst[slot][:], reads=[('vst', slot, ti) for ti in range(8)])

    phase_linear(C, hT, W, D, [(i * 512, 512) for i in range(8)], epi, on_block=on_block, end_block=end_block)
    A.pop()
    A.push()
    dtT = A.tile([64, SL], F32, 'cdtT')
    cw = A.tile([128, 48, 4], F32, 'ccw')
    cb = A.tile([128, 48], F32, 'ccb')
    dtb = A.tile([64, 1], F32, 'cdtb')
    halo = A.tile([128, 48, 3], F32, 'chalo')
    S.dma(cw[:], I['c_conv_wT'].rearrange("(c p) k -> p c k", p=128), writes=['ccw'])
    S.dma(cb[:], I['c_conv_bT'], writes=['ccb'])
    S.dma(dtb[:], I['c_dt_bias'].rearrange("(p o) -> p o", o=1), writes=['cdtb'])
    S.add('pool', lambda e: e.memset(halo[:], 0.0), writes=[('halo', i) for i in range(48)])
    xr = [A.tile([128, 515], F32, 'cxr') for _ in range(2)]
    acc = [A.tile([128, 512], F32, 'cacc') for _ in range(2)]
    so = [A.tile([128, 512], BF16, 'cso') for _ in range(4)]
    dq = Deferred()
    tms = [A.tile([128, 8, 128], BF16, 'ctms') for _ in range(2)]
    e1 = A.tile([64, 512], F32, 'ce1')
    st2 = {'k': 0, 'u': 0}
    bcTv = bcT

    def epi_fm(tb, bi, j, th, bank, msz):
        if bi == 12:
            cols = slice(tb * TB + th * 512, tb * TB + (th + 1) * 512)
            S.add('act', lambda e: e.activation(e1[:], ps[bank][:64, :512], AF.Exp, bias=dtb[:, 0:1]), reads=[('ps', bank), 'cdtb'], writes=['ce1'])
            S.add('act', lambda e: e.activation(dtT[:, cols], e1[:], AF.Ln, bias=1.0), reads=['ce1'], writes=[('dtT', tb, th)])
            return
        cidx = bi * 4 + j
        cnt = st2['k']
        st2['k'] += 1
        k = cnt % 2
        ko = cnt % 4
        S.add('pool', lambda e: e.tensor_copy(xr[k][:, 0:3], halo[:, cidx, :]), reads=[('halo', cidx)], writes=[('xrh', k)])
        S.add('act', lambda e: e.copy(xr[k][:, 3:515], ps[bank][:, :512]), reads=[('ps', bank)], writes=[('xrb', k)])
        S.add('pool', lambda e: e.tensor_copy(halo[:, cidx, :], xr[k][:, 512:515]), reads=[('xrb', k), ('xrh', k)], writes=[('halo', cidx)])
        S.add('act', lambda e: e.mul(acc[k][:], xr[k][:, 0:512], cw[:, cidx, 0:1]), reads=[('xrh', k), ('xrb', k), 'ccw'], writes=[('acc', k)])
        for q in range(1, 4):
            S.add('dve', lambda e, q=q: e.scalar_tensor_tensor(acc[k][:], xr[k][:, q:q + 512], cw[:, cidx, q:q + 1], acc[k][:], ALU.mult, ALU.add),
                  reads=[('xrh', k), ('xrb', k), ('acc', k)], writes=[('acc', k)])
        S.add('act', lambda e: e.activation(so[ko][:], acc[k][:], AF.Silu, bias=cb[:, cidx:cidx + 1]), reads=[('acc', k), 'ccb'], writes=[('so', ko)])
        if cidx >= 32:
            r0 = (cidx - 32) * 128
            S.dma(bcTv[r0:r0 + 128, tb * TB + th * 512:tb * TB + (th + 1) * 512], so[ko][:], reads=[('so', ko)])
        if cidx < 40:
            if th == 0:
                st2['slot'] = st2['u'] % 2
                st2['u'] += 1
            slot = st2['slot']

            def tail():
                tbk = 6 + cnt % 2
                for c in range(4):
                    S.add('pe', lambda e, c=c: e.transpose(psb[tbk][:, c * 128:(c + 1) * 128], so[ko][:, c * 128:(c + 1) * 128], C.ident[:]), reads=[('so', ko)], writes=[('ps', tbk)])
                S.add('dve', lambda e: e.tensor_copy(tms[slot][:, th * 4:(th + 1) * 4, :], psb[tbk][:, 0:512].rearrange("p (c t) -> p c t", c=4)), reads=[('ps', tbk)], writes=[('tms', slot, th)])
                if th == 1:
                    rows = slice(tb * TB, (tb + 1) * TB)
                    if cidx < 32:
                        dst = xstm[rows, cidx * 128:(cidx + 1) * 128]
                    else:
                        dst = Btm[rows, (cidx - 32) * 128:(cidx - 31) * 128]
                    S.dma(dst.rearrange("(t p) n -> p t n", p=128), tms[slot][:], reads=[('tms', slot, 0), ('tms', slot, 1)])

            dq.push(tail)
        dq.tick()

    nblocks = [(4096 + i * 512, 512) for i in range(12)] + [(10240, 64)]
    phase_linear(C, hT, W, D, nblocks, epi_fm, fm=True, finish=dq.flush)
    alog = A.tile([64, 2], F32, 'calog')
    S.dma(alog[:, 0:1], I['c_a_log'].rearrange("(p o) -> p o", o=1), writes=['alog'])
    S.add('act', lambda e: e.activation(alog[:, 1:2], alog[:, 0:1], AF.Exp), reads=['alog'], writes=['aexp'])
    S.add('dve', lambda e: e.tensor_scalar(alog[:, 1:2], alog[:, 1:2], -1.0, None, ALU.mult), reads=['aexp'], writes=['aneg'])
    aT = A.tile([64, SL], F32, 'caT')
    acsT = A.tile([64, SL], F32, 'cacsT')
    onesf = A.tile([64, 256], F32, 'cones')
    S.add('pool', lambda e: e.memset(onesf[:], 1.0), writes=['cones'])
    S.add('dve', lambda e: e.tensor_scalar(aT[:], dtT[:], alog[:, 1:2], None, ALU.mult), reads=['aneg'], writes=['aT'])
    for c in range(16):
        cols = slice(c * 256, (c + 1) * 256)
        S.add('dve', lambda e, cols=cols: e.tensor_tensor_scan(acsT[:, cols], onesf[:], aT[:, cols], 0.0, ALU.mult, ALU.add), reads=['aT', 'cones'], writes=[('acsT', c)])
        S.dma(acsC[c].rearrange("(h i) -> h i", h=64), acsT[:, cols], reads=[('acsT', c)])
    tst = A.tile([128, 2, 32, 64], F32, 'ctst')
    for wi_, srcT in enumerate([dtT, acsT]):
        for t8 in range(4):
            bank = (wi_ * 4 + t8) % 4
            for q in range(8):
                t = t8 * 8 + q
                S.add('pe', lambda e, bank=bank, q=q, t=t, srcT=srcT: e.transpose(ps[bank][:, q * 64:(q + 1) * 64], srcT[:, t * 128:(t + 1) * 128], C.identf[:64, :64]),
                      reads=[('acsT', t // 2)], writes=[('ps', bank)])
            S.add('act', lambda e, bank=bank, wi_=wi_, t8=t8: e.copy(tst[:, wi_, t8 * 8:(t8 + 1) * 8, :], ps[bank][:, :512].rearrange("p (q h) -> p q h", q=8)),
                  reads=[('ps', bank)], writes=[('tst', wi_, t8)])
        dstd = dttm if wi_ == 0 else acstm
        S.dma(dstd.rearrange("(t p) h -> p t h", p=128), tst[:, wi_, :, :], reads=[('tst', wi_, t8) for t8 in range(4)])
    A.pop()
    S.barrier()
    A.push()
    nw = load_bcast(C, I['c_norm_w'], 4096, 'cnw')
    dsk = load_bcast(C, I['c_d_skip'], 64, 'cdsk')
    xs_c = [A.tile([128, 2, 4096], BF16, 'cxs')] * 2
    sz_c = A.tile([128, 2, 4096], BF16, 'csz')
    Bt_c = [A.tile([128, 2, 1024], BF16, 'cBt') for _ in range(2)]
    BT_c = [A.tile([128, 8, 256], BF16, 'cBT') for _ in range(2)]
    CT_c = [A.tile([128, 8, 256], BF16, 'cCT') for _ in range(2)]
    sm_c = [A.tile([128, 2, 2, 64], F32, 'csm') for _ in range(2)]
    alb = [A.tile([128, 64], F32, 'calb') for _ in range(2)]
    ar = [A.tile([128, 2048], F32, 'car') for _ in range(2)]
    eacs = A.tile([128, 2, 64], F32, 'ceacs')
    w2 = A.tile([128, 2, 64], F32, 'cw2')
    eal = A.tile([128, 64], F32, 'ceal')
    stf = A.tile([128, 8, 512], F32, 'cstf')
    stb = A.tile([128, 8, 512], BF16, 'cstb')
    xdt = [A.tile([128, 2, 512], BF16, 'cxdt') for _ in range(2)]
    xde = [A.tile([128, 2, 512], BF16, 'cxde') for _ in range(2)]
    Gsb = [A.tile([128, 384], F32, 'cG') for _ in range(2)]
    tLb = [A.tile([128, 8, 128], F32, 'ctLb') for _ in range(2)]
    M0 = [A.tile([128, 8, 256], BF16, 'cM0') for _ in range(2)]
    M1 = [A.tile([128, 8, 128], BF16, 'cM1') for _ in range(2)]
    yt = [A.tile([128, 512], F32, 'cyt') for _ in range(2)]
    t2 = [A.tile([128, 512], F32, 'ct2') for _ in range(2)]
    ob = [A.tile([128, 512], BF16, 'cob') for _ in range(2)]
    junk = A.tile([128, 512], BF16, 'cjunk')
    smn = A.tile([128, 4], F32, 'csmn')
    ost = A.tile([128, 32, 256], BF16, 'cost')
    BTv = bcT[0:1024, :].rearrange("(g p) t -> p g t", p=128)
    CTv = bcT[1024:2048, :].rearrange("(g p) t -> p g t", p=128)
    oTv = oT.rearrange("(c p) t -> p c t", p=128)
    gi = 0
    li_ = 0
    yi = 0
    for c in range(16):
        s = c % 2
        rows = slice(c * 256, (c + 1) * 256)
        S.dma(xs_c[s][:], xstm[rows, :].rearrange("(t p) n -> p t n", p=128), writes=[('xs', 0)])
        S.dma(sz_c[:], szd[rows, :].rearrange("(t p) n -> p t n", p=128), writes=['sz'])
        S.dma(Bt_c[s][:], Btm[rows, :].rearrange("(t p) n -> p t n", p=128), writes=[('Bt', s)])
        S.dma(BT_c[s][:], BTv[:, :, rows], writes=[('BT', s)])
        S.dma(CT_c[s][:], CTv[:, :, rows], writes=[('CT', s)])
        S.dma(sm_c[s][:, 0, :, :], dttm[rows, :].rearrange("(t p) h -> p t h", p=128), writes=[('dtc', s)])
        S.dma(sm_c[s][:, 1, :, :], acstm[rows, :].rearrange("(t p) h -> p t h", p=128), writes=[('acsc', s)])
        S.dma(alb[s][:], acstm[c * 256 + 255, :].partition_broadcast(128), writes=[('alb', s)])
        dt_c = sm_c[s][:, 0, :, :]
        acs_c = sm_c[s][:, 1, :, :]
        S.add('act', lambda e, acs_c=acs_c: e.activation(eacs[:], acs_c, AF.Exp), reads=[('acsc', s)], writes=['eacs'])
        S.add('dve', lambda e, acs_c=acs_c, s=s: e.tensor_tensor(w2[:], bc(alb[s][:], [128, 2, 64], 1), acs_c, ALU.subtract), reads=[('alb', s), ('acsc', s)], writes=['w2'])
        S.add('act', lambda e: e.activation(w2[:], w2[:], AF.Exp), reads=['w2'], writes=['w2'])
        S.add('dve', lambda e, dt_c=dt_c: e.tensor_tensor(w2[:], w2[:], dt_c, ALU.mult), reads=['w2', ('dtc', s)], writes=['w2'])
        S.add('act', lambda e, s=s: e.activation(eal[:], alb[s][:], AF.Exp), reads=[('alb', s)], writes=['eal'])
        stA = {}

        def stageA(g, c=c, s=s, dt_c=dt_c, acs_c=acs_c):
            nonlocal gi, li_
            sg = gi % 2
            k = gi % 2
            gi += 1
            S.dma(ar[sg][:], acsC[c, g * 2048:(g + 1) * 2048].partition_broadcast(128), writes=[('ar', sg)])
            gc = slice(g * 512, (g + 1) * 512)
            hs_ = slice(g * 8, (g + 1) * 8)
            xs4 = xs_c[s][:, :, gc].rearrange("p t (h q) -> p t h q", h=8)
            S.add('pool', lambda e, k=k, xs4=xs4, dt_c=dt_c, hs_=hs_: e.tensor_tensor(xdt[k][:].rearrange("p t (h q) -> p t h q", h=8), xs4, bc(dt_c[:, :, hs_], [128, 2, 8, 64], 3), ALU.mult),
                  reads=[('xs', 0), ('dtc', s)], writes=[('xdt', k)])
            S.add('pool', lambda e, k=k, xs4=xs4, hs_=hs_: e.tensor_tensor(xde[k][:].rearrange("p t (h q) -> p t h q", h=8), xs4, bc(w2[:, :, hs_], [128, 2, 8, 64], 3), ALU.mult),
                  reads=[('xs', 0), 'w2'], writes=[('xde', k)])
            S.add('pe', lambda e, s=s, g=g: e.matmul(ps[0][:, 0:256], BT_c[s][:, g, 0:128], CT_c[s][:, g, 0:256], start=True, stop=True), reads=[('BT', s), ('CT', s)], writes=[('ps', 0)])
            S.add('pe', lambda e, s=s, g=g: e.matmul(ps[0][:, 256:384], BT_c[s][:, g, 128:256], CT_c[s][:, g, 128:256], start=True, stop=True), reads=[('BT', s), ('CT', s)], writes=[('ps', 0)])
            S.add('dve', lambda e, k=k: e.tensor_tensor(Gsb[k][:], ps[0][:, 0:384], C.gmask[:], ALU.mult), reads=[('ps', 0)], writes=[('G', k)])
            kk = li_ % 2
            li_ += 1
            ar3 = ar[sg][:].rearrange("p (h i) -> p h i", h=8)
            acs0 = bc(acs_c[:, 0, hs_], [128, 8, 256], 2)
            acs1 = bc(acs_c[:, 1, hs_], [128, 8, 128], 2)
            S.add('dve', lambda e, kk=kk, ar3=ar3, acs1=acs1: e.tensor_tensor(tLb[kk][:], ar3[:, :, 128:256], acs1, ALU.subtract), reads=[('ar', sg), ('acsc', s)], writes=[('tLb', kk)])
            S.add('act', lambda e, kk=kk: e.activation(tLb[kk][:], tLb[kk][:], AF.Exp), reads=[('tLb', kk)], writes=[('tLb', kk)])
            S.add('dve', lambda e, kk=kk, k=k: e.scalar_tensor_tensor(M1[kk][:], tLb[kk][:], 1.0, bc(Gsb[k][:, 256:384], [128, 8, 128], 1), ALU.min, ALU.mult), reads=[('tLb', kk), ('G', k)], writes=[('M1', kk)])
            S.add('dve', lambda e, ar3=ar3, acs0=acs0: e.tensor_tensor(ar3, ar3, acs0, ALU.subtract), reads=[('ar', sg), ('acsc', s)], writes=[('ar', sg)])
            S.add('act', lambda e, sg=sg: e.activation(ar[sg][:], ar[sg][:], AF.Exp), reads=[('ar', sg)], writes=[('ar', sg)])
            S.add('dve', lambda e, kk=kk, k=k, ar3=ar3: e.scalar_tensor_tensor(M0[kk][:], ar3, 1.0, bc(Gsb[k][:, 0:256], [128, 8, 256], 1), ALU.min, ALU.mult), reads=[('ar', sg), ('G', k)], writes=[('M0', kk)])
            stA[g] = (k, kk, gc, hs_)

        stageA(0)
        for g in range(8):
            if g + 1 < 8:
                stageA(g + 1)
            k, kk, gc, hs_ = stA[g]
            if c > 0:
                for it in range(2):
                    S.add('pe', lambda e, it=it, s=s, g=g: e.matmul(ps[2 + it][:, :512], CT_c[s][:, g, it * 128:(it + 1) * 128], stb[:, g, :], start=True, stop=True),
                          reads=[('CT', s), ('stb', g)], writes=[('ps', 2 + it)])
            for hh in range(8):
                hq = slice(hh * 64, (hh + 1) * 64)
                S.add('pe', lambda e, kk=kk, k=k, hq=hq, hh=hh: e.matmul(ps[4][:, hq], M0[kk][:, hh, 0:128], xdt[k][:, 0, hq], start=True, stop=True), reads=[('M0', kk), ('xdt', k)], writes=[('ps', 4)])
                S.add('pe', lambda e, kk=kk, k=k, hq=hq, hh=hh: e.matmul(ps[5][:, hq], M0[kk][:, hh, 128:256], xdt[k][:, 0, hq], start=True, stop=False), reads=[('M0', kk), ('xdt', k)], writes=[('ps', 5)])
                S.add('pe', lambda e, kk=kk, k=k, hq=hq, hh=hh: e.matmul(ps[5][:, hq], M1[kk][:, hh, :], xdt[k][:, 1, hq], start=False, stop=True), reads=[('M1', kk), ('xdt', k)], writes=[('ps', 5)])
            for it in range(2):
                y2 = yi % 2
                yi += 1
                if c > 0:
                    S.add('dve', lambda e, it=it, y2=y2, hs_=hs_: e.tensor_tensor(yt[y2][:].rearrange("p (h q) -> p h q", h=8), ps[2 + it][:, :512].rearrange("p (h q) -> p h q", h=8), bc(eacs[:, it, hs_], [128, 8, 64], 2), ALU.mult),
                          reads=[('ps', 2 + it), 'eacs'], writes=[('yt', y2)])
                    S.add('dve', lambda e, it=it, y2=y2: e.tensor_tensor(yt[y2][:], ps[4 + it][:, :512], yt[y2][:], ALU.add), reads=[('ps', 4 + it), ('yt', y2)], writes=[('yt', y2)])
                else:
                    S.add('act', lambda e, it=it, y2=y2: e.copy(yt[y2][:], ps[4 + it][:, :512]), reads=[('ps', 4 + it)], writes=[('yt', y2)])
                S.add('pool', lambda e, it=it, y2=y2, s=s, gc=gc, hs_=hs_: e.tensor_tensor(t2[y2][:].rearrange("p (h q) -> p h q", h=8), xs_c[s][:, it, gc].rearrange("p (h q) -> p h q", h=8), bc(dsk[:, hs_], [128, 8, 64], 2), ALU.mult),
                      reads=[('xs', 0), 'cdsk'], writes=[('t2', y2)])
                S.add('pool', lambda e, y2=y2: e.tensor_tensor(yt[y2][:], yt[y2][:], t2[y2][:], ALU.add), reads=[('yt', y2), ('t2', y2)], writes=[('yt', y2)])
                S.add('dve', lambda e, it=it, y2=y2, gc=gc: e.tensor_tensor(yt[y2][:], yt[y2][:], sz_c[:, it, gc], ALU.mult), reads=[('yt', y2), 'sz'], writes=[('yt', y2)])
                ssa = smn[:, y2:y2 + 1]
                rsa = smn[:, 2 + y2:3 + y2]
                S.add('act', lambda e, y2=y2, ssa=ssa: e.activation(junk[:], yt[y2][:], AF.Square, accum_out=ssa), reads=[('yt', y2)], writes=[('ssc', y2)])
                rstd_ops(C, ssa, rsa, 512, [('ssc', y2)], ('c', y2))
                S.add('dve', lambda e, y2=y2, rsa=rsa, gc=gc: e.scalar_tensor_tensor(ob[y2][:], yt[y2][:], rsa, nw[:, gc], ALU.mult, ALU.mult), reads=[('yt', y2), ('rs', ('c', y2)), 'cnw'], writes=[('ob', y2)])
                tbk = 6 + y2
                for cc in range(4):
                    S.add('pe', lambda e, tbk=tbk, cc=cc, y2=y2: e.transpose(psb[tbk][:, cc * 128:(cc + 1) * 128], ob[y2][:, cc * 128:(cc + 1) * 128], C.ident[:]), reads=[('ob', y2)], writes=[('ps', tbk)])
                dst = ost[:, g * 4:(g + 1) * 4, it * 128:(it + 1) * 128]
                srcv = psb[tbk][:, 0:512].rearrange("p (c t) -> p c t", c=4)
                S.add('act', lambda e, dst=dst, srcv=srcv: e.copy(dst, srcv), reads=[('ps', tbk)], writes=[('ost', g, it)])
            if c < 15:
                for jt in range(2):
                    S.add('pe', lambda e, jt=jt, s=s, g=g, k=k: e.matmul(ps[1][:, :512], Bt_c[s][:, jt, g * 128:(g + 1) * 128], xde[k][:, jt, :], start=(jt == 0), stop=(jt == 1)),
                          reads=[('Bt', s), ('xde', k)], writes=[('ps', 1)])
                if c == 0:
                    S.add('act', lambda e, g=g: e.copy(stf[:, g, :], ps[1][:, :512]), reads=[('ps', 1)], writes=[('stf', g)])
                else:
                    S.add('pool', lambda e, g=g, hs_=hs_: e.tensor_tensor(stf[:, g, :].rearrange("p (h q) -> p h q", h=8), stf[:, g, :].rearrange("p (h q) -> p h q", h=8), bc(eal[:, hs_], [128, 8, 64], 2), ALU.mult),
                          reads=[('stf', g), 'eal'], writes=[('stf', g)])
                    S.add('dve', lambda e, g=g: e.tensor_tensor(stf[:, g, :], ps[1][:, :512], stf[:, g, :], ALU.add), reads=[('ps', 1), ('stf', g)], writes=[('stf', g)])
                S.add('act', lambda e, g=g: e.copy(stb[:, g, :], stf[:, g, :]), reads=[('stf', g)], writes=[('stb', g)])
        S.dma(oTv[:, :, rows], ost[:], reads=[('ost', g, it) for g in range(8) for it in range(2)])
    A.pop()
    S.barrier()
    phase_outproj(C, oT, C.Wb['c_w_out'], 4096, x_src, x_dst)


def kernel(**inputs):
    nc, _ = build((0, 1, 2, 3), True)
    in_maps = [make_in_map(inputs, b) for b in range(8)]
    res = run_bass_kernel_spmd(nc, in_maps, core_ids=list(range(8)))
    return np.stack([np.asarray(r['y'], dtype=np.float32) for r in res.results], 0)
```

```python
import numpy as np
from contextlib import ExitStack
import concourse.bass as bass
import concourse.mybir as mybir

F32 = mybir.dt.float32
BF16 = mybir.dt.bfloat16
AF = mybir.ActivationFunctionType
ALU = mybir.AluOpType
AX = mybir.AxisListType

SEM_CAP = 30000
N_DMA_SEMS = 8
SAME_ENG_SYNC = True


class Sched:
    def __init__(self, nc):
        self.nc = nc
        self.ops = []
        self.last_w = {}
        self.readers = {}
        self.since_barrier = []

    def add(self, eng, fn, reads=(), writes=(), dma=False):
        idx = len(self.ops)
        deps = set()
        for r in reads:
            w = self.last_w.get(r)
            if w is not None:
                deps.add(w)
        for r in writes:
            w = self.last_w.get(r)
            if w is not None:
                deps.add(w)
            deps.update(self.readers.get(r, ()))
        for r in reads:
            self.readers.setdefault(r, []).append(idx)
        for r in writes:
            self.last_w[r] = idx
            self.readers[r] = []
        deps.discard(idx)
        self.ops.append(dict(eng=eng, fn=fn, deps=deps, dma=dma, load=(dma and len(writes) > 0)))
        self.since_barrier.append(idx)
        return idx

    def dma(self, out, in_, reads=(), writes=(), q='sp', **kw):
        return self.add(q, lambda e: e.dma_start(out=out, in_=in_, **kw), reads, writes, dma=True)

    def barrier(self):
        lastc = {}
        prev = []
        for i in self.since_barrier:
            o = self.ops[i]
            if o['dma']:
                prev.append(i)
            elif o['fn'] is not None:
                lastc[o['eng']] = i
        prev += list(lastc.values())
        self.since_barrier = []
        self.last_w = {}
        self.readers = {}
        for eng in ['pe', 'act', 'dve', 'pool', 'sp']:
            idx = len(self.ops)
            self.ops.append(dict(eng=eng, fn=None, deps=set(prev), dma=False))
            self.since_barrier.append(idx)

    def emit(self):
        nc = self.nc
        ops = self.ops
        needed = [False] * len(ops)
        for i, op in enumerate(ops):
            for d in op['deps']:
                od = ops[d]
                if od['dma']:
                    needed[d] = True
                elif od['eng'] == op['eng'] and (od['eng'] == 'pe' or not SAME_ENG_SYNC):
                    continue
                else:
                    needed[d] = True
        order = {e: [] for e in ['pe', 'act', 'dve', 'pool', 'sp']}
        for i, op in enumerate(ops):
            order[op['eng']].append(i)
        for e in ('sp', 'pool'):
            out = []
            for i in order[e]:
                op = ops[i]
                pos = len(out)
                if op['dma'] and op.get('load'):
                    md = max(op['deps']) if op['deps'] else -1
                    while pos > 0:
                        p = out[pos - 1]
                        po = ops[p]
                        if po['dma'] and not po.get('load') and p > md:
                            pos -= 1
                        else:
                            break
                out.insert(pos, i)
            order[e] = out
        dma_n = {}
        dma_rr = {}
        events = [None] * len(ops)
        pre_wait = [None] * len(ops)
        for e, lst in order.items():
            cnt = 0
            for i in lst:
                op = ops[i]
                if op['dma']:
                    k = dma_rr.get(e, 0)
                    dma_rr[e] = (k + 1) % N_DMA_SEMS
                    key = ('dma', e, k)
                    n = dma_n.get(key, 0)
                    if n > 0:
                        pre_wait[i] = (key, 16 * n)
                    n += 1
                    if 16 * n > SEM_CAP:
                        raise RuntimeError('dma sem cap exceeded; raise N_DMA_SEMS')
                    dma_n[key] = n
                    events[i] = (key, 16 * n)
                elif needed[i]:
                    cnt += 1
                    epoch, val = divmod(cnt - 1, SEM_CAP)
                    events[i] = (('eng', e, epoch), val + 1)
        keys = []
        for ev in events:
            if ev is not None and ev[0] not in keys:
                keys.append(ev[0])
        self.stack = ExitStack()
        sems = {}
        for k in keys:
            sems[k] = self.stack.enter_context(nc.semaphore('_'.join(str(x) for x in k)))
        self.n_sems = len(keys)
        streams = {e: [] for e in order}
        n_wait = 0
        for e, lst in order.items():
            seen = {}
            for i in lst:
                op = ops[i]
                waits = {}
                if pre_wait[i] is not None:
                    k, v = pre_wait[i]
                    waits[k] = max(waits.get(k, 0), v)
                for d in op['deps']:
                    od = ops[d]
                    if not od['dma'] and od['eng'] == e and (e == 'pe' or not SAME_ENG_SYNC):
                        continue
                    ev = events[d]
                    assert ev is not None, (i, d)
                    k, v = ev
                    waits[k] = max(waits.get(k, 0), v)
                wl = []
                for k, v in waits.items():
                    if seen.get(k, 0) >= v:
                        continue
                    seen[k] = v
                    wl.append((sems[k], v))
                n_wait += len(wl)
                inc = None
                if events[i] is not None:
                    inc = (sems[events[i][0]], 16 if op['dma'] else 1)
                streams[e].append((wl, op['fn'], inc))
        self.n_wait = n_wait

        def run(engobj, lst):
            for wl, fn, inc in lst:
                for s, v in wl:
                    engobj.wait_ge(s, v)
                if fn is None:
                    continue
                ins = fn(engobj)
                if inc is not None:
                    ins.then_inc(inc[0], inc[1])

        with nc.Block() as block:
            @block.tensor
            def _(t):
                run(t, streams['pe'])

            @block.scalar
            def _(t):
                run(t, streams['act'])

            @block.vector
            def _(t):
                run(t, streams['dve'])

            @block.gpsimd
            def _(t):
                run(t, streams['pool'])

            @block.sync
            def _(t):
                run(t, streams['sp'])
        self.stack.close()


class SbufAlloc:
    def __init__(self, nc, base=16640, limit=229376 - 64):
        self.nc = nc
        self.base = base
        self.off = base
        self.limit = limit
        self.uid = 0
        self.marks = []

    def push(self):
        self.marks.append(self.off)

    def pop(self):
        self.off = self.marks.pop()

    def tile(self, shape, dtype, name='t'):
        esz = 4 if dtype == F32 else 2
        n = 1
        for s in shape[1:]:
            n *= s
        nbytes = (n * esz + 63) // 64 * 64
        self.uid += 1
        t = self.nc.alloc_sbuf_tensor_at(f'{name}_{self.uid}', list(shape), dtype, offset=self.off)
        self.off += nbytes
        assert self.off <= self.limit, (name, self.off)
        return t


import math
import numpy as np
from concourse.bass_utils import run_bass_kernel_spmd

SL = 4096
D = 2048
NT = 32
EPS = 1e-6
TB = 1024


class Ctx:
    pass


def bc(ap, shape, axis):
    return ap.unsqueeze(axis).to_broadcast(list(shape))


def setup_consts(C):
    nc, S, A = C.nc, C.S, C.A
    C.ps = [nc.alloc_psum_tensor(f"ps{i}", [128, 512], F32) for i in range(8)]
    C.psb = [p[:].bitcast(BF16) for p in C.ps]
    C.identf = A.tile([128, 128], F32, 'identf')
    C.ident = A.tile([128, 128], BF16, 'ident')
    C.mask4 = A.tile([128, 4, 128], BF16, 'mask4')
    C.mle4 = A.tile([128, 4, 128], BF16, 'mle4')
    C.mle = A.tile([128, 128], BF16, 'mle')
    C.gmask = A.tile([128, 384], F32, 'gmask')
    C.pad_ = A.tile([128, 128], F32, 'cpad')
    A.push()
    C.mle_f = A.tile([128, 128], F32, 'mlef')
    C.mge_f = A.tile([128, 128], F32, 'mgef')
    for t, pat, cm, op in [(C.identf, [[-1, 128]], 1, ALU.is_equal),
                           (C.mle_f, [[1, 128]], -1, ALU.is_ge),
                           (C.mge_f, [[-1, 128]], 1, ALU.is_ge)]:
        S.add('pool', lambda e, t=t: e.memset(t[:], 1.0), writes=[t.name])
        S.add('pool', lambda e, t=t, pat=pat, cm=cm, op=op: e.affine_select(t[:], t[:], pat, op, 0.0, base=0, channel_multiplier=cm),
              reads=[t.name], writes=[t.name])
    S.add('dve', lambda e: e.tensor_copy(C.ident[:], C.identf[:]), reads=[C.identf.name], writes=['c1'])
    S.add('dve', lambda e: e.tensor_copy(C.mle[:], C.mle_f[:]), reads=[C.mle_f.name], writes=['c2'])
    for k in range(4):
        src = C.mge_f if k % 2 == 0 else C.mle_f
        S.add('dve', lambda e, k=k, src=src: e.tensor_copy(C.mask4[:, k, :], src[:]), reads=[src.name], writes=[('c3', k)])
        S.add('dve', lambda e, k=k: e.tensor_copy(C.mle4[:, k, :], C.mle_f[:]), reads=[C.mle_f.name], writes=[('c4', k)])
    S.add('pool', lambda e: e.memset(C.gmask[:, 128:256], 1.0), writes=['c8'])
    S.add('dve', lambda e: e.tensor_copy(C.gmask[:, 0:128], C.mle_f[:]), reads=[C.mle_f.name], writes=['c9'])
    S.add('dve', lambda e: e.tensor_copy(C.gmask[:, 256:384], C.mle_f[:]), reads=[C.mle_f.name], writes=['c10'])
    S.barrier()
    A.pop()


class BgConv:
    def __init__(self, C):
        self.C = C
        self.jobs = []
        self.done = set()
        self.i = 0
        A = C.A
        self.f = [A.tile([128, 2048], F32, 'bgf') for _ in range(2)]
        self.b = [A.tile([128, 2048], BF16, 'bgb') for _ in range(2)]

    def add(self, name, src, dst):
        sv, dv = flat128(src), flat128(dst)
        F_ = sv.shape[1]
        steps = []
        for f0 in range(0, F_, 2048):
            steps.append((sv, dv, f0, min(2048, F_ - f0)))
        self.jobs.append([name, steps])

    def _one(self, eng='pool'):
        S = self.C.S
        name, steps = self.jobs[0]
        sv, dv, f0, fsz = steps.pop(0)
        s = self.i % 2
        self.i += 1
        f, b = self.f[s], self.b[s]
        S.dma(f[:, :fsz], sv[:, f0:f0 + fsz], writes=[('bgf', s)])
        if eng == 'act':
            S.add('act', lambda e: e.copy(b[:, :fsz], f[:, :fsz]), reads=[('bgf', s)], writes=[('bgb', s)])
        else:
            S.add(eng, lambda e: e.tensor_copy(b[:, :fsz], f[:, :fsz]), reads=[('bgf', s)], writes=[('bgb', s)])
        S.dma(dv[:, f0:f0 + fsz], b[:, :fsz], reads=[('bgb', s)])
        if not steps:
            self.done.add(name)
            self.jobs.pop(0)

    def step(self, n):
        for _ in range(n):
            if not self.jobs:
                return
            self._one('act')

    def require(self, name):
        if name in self.done:
            return
        k = 0
        while name not in self.done:
            self._one(['pool', 'act', 'dve'][k % 3])
            k += 1
        self.C.S.barrier()


def flat128(ap2d):
    return ap2d.rearrange("k n -> (k n)").rearrange("(p f) -> p f", p=128)


def phase_convert(C, pairs):
    S, A = C.S, C.A
    A.push()
    CH = 4096
    fs = [A.tile([128, CH], F32, 'cvf') for _ in range(3)]
    bs = [A.tile([128, CH], BF16, 'cvb') for _ in range(3)]
    i = 0
    for src, dst in pairs:
        sv, dv = flat128(src), flat128(dst)
        F_ = sv.shape[1]
        for f0 in range(0, F_, CH):
            fsz = min(CH, F_ - f0)
            s = i % 3
            eng = ['act', 'dve', 'pool'][i % 3]
            S.dma(fs[s][:, :fsz], sv[:, f0:f0 + fsz], writes=[('cf', s)])
            if eng == 'act':
                S.add('act', lambda e, s=s, fsz=fsz: e.copy(bs[s][:, :fsz], fs[s][:, :fsz]), reads=[('cf', s)], writes=[('cb', s)])
            else:
                S.add(eng, lambda e, s=s, fsz=fsz: e.tensor_copy(bs[s][:, :fsz], fs[s][:, :fsz]), reads=[('cf', s)], writes=[('cb', s)])
            S.dma(dv[:, f0:f0 + fsz], bs[s][:, :fsz], reads=[('cb', s)])
            i += 1
    A.pop()
    S.barrier()


def rstd_ops(C, ss, rs, n, reads, tag):
    S = C.S
    S.add('act', lambda e: e.activation(rs, ss, AF.Sqrt, bias=EPS, scale=1.0 / n), reads=reads, writes=[('rs', tag)])
    S.add('dve', lambda e: e.reciprocal(rs, rs), reads=[('rs', tag)], writes=[('rs', tag)])


def phase_norm(C, x_src, w_ap, hT):
    S, A, ps, psb = C.S, C.A, C.ps, C.psb
    A.push()
    wt = A.tile([128, D], F32, 'nw')
    S.dma(wt[:], w_ap.partition_broadcast(128), writes=['nw'])
    xs = [A.tile([128, D], F32, 'nx') for _ in range(3)]
    sq = A.tile([128, D], BF16, 'nsq')
    hs = [A.tile([128, D], BF16, 'nh') for _ in range(3)]
    st = [A.tile([128, 16, 512], BF16, 'nst') for _ in range(2)]
    sm = A.tile([128, 6], F32, 'nsm')
    hTv = hT.rearrange("(c p) t -> p c t", p=128)

    def stage_a(t):
        s = t % 3
        S.dma(xs[s][:], x_src[t * 128:(t + 1) * 128, :], writes=[('x', s)])
        S.add('act', lambda e, s=s: e.activation(sq[:], xs[s][:], AF.Square, accum_out=sm[:, s:s + 1]), reads=[('x', s)], writes=[('ss', s)])
        rstd_ops(C, sm[:, s:s + 1], sm[:, 3 + s:4 + s], D, [('ss', s)], ('n', s))
        S.add('dve', lambda e, s=s: e.scalar_tensor_tensor(hs[s][:], xs[s][:], sm[:, 3 + s:4 + s], wt[:], ALU.mult, ALU.mult),
              reads=[('x', s), ('rs', ('n', s)), 'nw'], writes=[('h', s)])

    toks = []
    stage_a(0)
    for t in range(NT):
        if t + 1 < NT:
            stage_a(t + 1)
        g, j = divmod(t, 4)
        gs = g % 2
        emit_T16(C, hs[t % 3], [('h', t % 3)], st[gs], j, toks, ('nst', gs))
        if j == 3:
            S.dma(hTv[:, :, g * 512:(g + 1) * 512], st[gs][:], reads=toks)
            toks = []
    A.pop()
    S.barrier()


def phase_linear(C, aT, Wb, K, nblocks, epi, on_tb=None, on_block=None, end_block=None, fm=False, lin_banks=(0, 1, 2, 3), finish=None, bg_steps=2):
    S, A, ps = C.S, C.A, C.ps
    KC = K // 128
    C.bg.require(C.Wn[id(Wb)])
    A.push()
    NA = 2 if KC == 16 else 1
    ats = [A.tile([128, KC, TB], BF16, 'lat') for _ in range(NA)]
    wt = [A.tile([128, KC, 512], BF16, 'lwt') for _ in range(2)]
    aTv = aT.rearrange("(c p) t -> p c t", p=128)
    Wv = Wb.rearrange("(c p) n -> p c n", p=128)
    C.lin_rr = 0
    wi = 0
    def load_at(tb):
        sa = tb % NA
        for kc in range(KC):
            S.dma(ats[sa][:, kc, :], aTv[:, kc, tb * TB:(tb + 1) * TB], writes=[('at', sa, kc)])

    load_at(0)
    for tb in range(SL // TB):
        sa = tb % NA
        at = ats[sa]
        if NA == 1 and tb > 0:
            load_at(tb)
        if NA == 2 and tb + 1 < SL // TB:
            load_at(tb + 1)
        if on_tb:
            on_tb(tb)
        for bi, (n0, nsz) in enumerate(nblocks):
            sw = wi % 2
            wi += 1
            S.dma(wt[sw][:, :, :nsz], Wv[:, :, n0:n0 + nsz], writes=[('wt', sw)])
            if on_block:
                on_block(tb, bi)
            if not fm:
                for ti in range(TB // 128):
                    bank = lin_banks[C.lin_rr % len(lin_banks)]
                    C.lin_rr += 1
                    for kc in range(KC):
                        S.add('pe', lambda e, bank=bank, kc=kc, ti=ti, sw=sw, nsz=nsz, at=at: e.matmul(ps[bank][:, :nsz], at[:, kc, ti * 128:(ti + 1) * 128], wt[sw][:, kc, :nsz], start=(kc == 0), stop=(kc == KC - 1)),
                              reads=[('at', sa, kc), ('wt', sw)], writes=[('ps', bank)])
                    epi(tb, bi, ti, bank, nsz)
            else:
                for j in range((nsz + 127) // 128):
                    msz = min(128, nsz - j * 128)
                    for th in range(TB // 512):
                        bank = lin_banks[C.lin_rr % len(lin_banks)]
                        C.lin_rr += 1
                        for kc in range(KC):
                            S.add('pe', lambda e, bank=bank, kc=kc, th=th, sw=sw, j=j, msz=msz, at=at: e.matmul(ps[bank][:msz, :512], wt[sw][:, kc, j * 128:j * 128 + msz], at[:, kc, th * 512:(th + 1) * 512], start=(kc == 0), stop=(kc == KC - 1)),
                                  reads=[('at', sa, kc), ('wt', sw)], writes=[('ps', bank)])
                        epi(tb, bi, j, th, bank, msz)
            C.bg.step(bg_steps)
            if end_block:
                end_block(tb, bi)
    if finish:
        finish()
    A.pop()
    S.barrier()


def phase_outproj(C, oT, Wb, K, x_src, x_dst):
    S, A, ps = C.S, C.A, C.ps
    A.push()
    xp = [A.tile([128, 8, 512], F32, 'xp') for _ in range(2)]
    st = {}

    def on_block(tb, bi):
        s = (tb * 4 + bi) % 2
        st['s'] = s
        src = x_src[tb * TB:(tb + 1) * TB, bi * 512:(bi + 1) * 512].rearrange("(t p) n -> p t n", p=128)
        S.dma(xp[s][:], src, writes=[('xp', s, ti) for ti in range(8)])

    def epi(tb, bi, ti, bank, nsz):
        s = st['s']
        S.add('dve', lambda e: e.tensor_tensor(xp[s][:, ti, :], ps[bank][:, :512], xp[s][:, ti, :], ALU.add),
              reads=[('ps', bank), ('xp', s, ti)], writes=[('xp', s, ti)])

    def end_block(tb, bi):
        s = st['s']
        dst = x_dst[tb * TB:(tb + 1) * TB, bi * 512:(bi + 1) * 512].rearrange("(t p) n -> p t n", p=128)
        S.dma(dst, xp[s][:], reads=[('xp', s, ti) for ti in range(8)])

    phase_linear(C, oT, Wb, K, [(i * 512, 512) for i in range(4)], epi, on_block=on_block, end_block=end_block, bg_steps=0)
    A.pop()


def phase_mlp(C, hT, W1b, W2b, x_io):
    S, A, ps = C.S, C.A, C.ps
    C.bg.require(C.Wn[id(W1b)])
    C.bg.require(C.Wn[id(W2b)])
    A.push()
    at = A.tile([128, 16, TB], BF16, 'mat')
    acc = A.tile([128, 8, D], F32, 'macc')
    w1 = [A.tile([128, 16, 512], BF16, 'mw1') for _ in range(2)]
    w2 = [A.tile([128, 4, D], BF16, 'mw2') for _ in range(2)]
    uT = [A.tile([128, 4, TB], BF16, 'muT') for _ in range(2)]
    rl = [A.tile([128, 512], F32, 'mrl')]
    hTv = hT.rearrange("(c p) t -> p c t", p=128)
    W1v = W1b.rearrange("(c p) n -> p c n", p=128)
    W2v = W2b.rearrange("(c p) n -> p c n", p=128)
    rr = 0
    ri = 0
    for tb in range(SL // TB):
        for kc in range(16):
            S.dma(at[:, kc, :], hTv[:, kc, tb * TB:(tb + 1) * TB], writes=[('at', kc)])
        for ti in range(8):
            r0 = tb * TB + ti * 128
            S.dma(acc[:, ti, :], x_io[r0:r0 + 128, :], writes=[('acc', ti, dc) for dc in range(4)])
        for fg in range(16):
            s = fg % 2
            S.dma(w1[s][:], W1v[:, :, fg * 512:(fg + 1) * 512], writes=[('w1', s)])
            S.dma(w2[s][:], W2v[:, fg * 4:(fg + 1) * 4, :], writes=[('w2', s)])
            C.bg.step(3)
            for fc in range(4):
                for th in range(2):
                    bank = rr % 4
                    rr += 1
                    for kc in range(16):
                        S.add('pe', lambda e, bank=bank, kc=kc, s=s, fc=fc, th=th: e.matmul(ps[bank][:, :512], w1[s][:, kc, fc * 128:(fc + 1) * 128], at[:, kc, th * 512:(th + 1) * 512], start=(kc == 0), stop=(kc == 15)),
                              reads=[('at', kc), ('w1', s)], writes=[('ps', bank)])
                    k = 0
                    ri += 1
                    S.add('act', lambda e, bank=bank, k=k: e.activation(rl[k][:], ps[bank][:, :512], AF.Relu), reads=[('ps', bank)], writes=[('rl', k)])
                    S.add('dve', lambda e, bank=bank, k=k, s=s, fc=fc, th=th: e.tensor_tensor(uT[s][:, fc, th * 512:(th + 1) * 512], ps[bank][:, :512], rl[k][:], ALU.mult),
                          reads=[('ps', bank), ('rl', k)], writes=[('uT', s, fc, th)])
            for ti in range(8):
                for dc in range(4):
                    bank = rr % 4
                    rr += 1
                    for fc in range(4):
                        S.add('pe', lambda e, bank=bank, s=s, fc=fc, ti=ti, dc=dc: e.matmul(ps[bank][:, :512], uT[s][:, fc, ti * 128:(ti + 1) * 128], w2[s][:, fc, dc * 512:(dc + 1) * 512], start=(fc == 0), stop=(fc == 3)),
                              reads=[('uT', s, fc, ti // 4), ('w2', s)], writes=[('ps', bank)])
                    S.add('dve', lambda e, bank=bank, ti=ti, dc=dc: e.tensor_tensor(acc[:, ti, dc * 512:(dc + 1) * 512], ps[bank][:, :512], acc[:, ti, dc * 512:(dc + 1) * 512], ALU.add),
                          reads=[('ps', bank), ('acc', ti, dc)], writes=[('acc', ti, dc)])
        for ti in range(8):
            r0 = tb * TB + ti * 128
            S.dma(x_io[r0:r0 + 128, :], acc[:, ti, :], reads=[('acc', ti, dc) for dc in range(4)])
    A.pop()
    S.barrier()


LAG = 2


class Deferred:
    def __init__(self):
        self.q = []
        self.u = 0

    def push(self, fn):
        self.q.append((self.u, fn))

    def tick(self):
        self.u += 1
        while self.q and self.u - self.q[0][0] >= LAG:
            self.q.pop(0)[1]()

    def flush(self):
        while self.q:
            self.q.pop(0)[1]()


def make_qkv_epi(C, specs, qkT, vdst, wq, wk):
    S, A, ps, psb = C.S, C.A, C.ps, C.psb
    sq = [A.tile([128, 512], F32, 'esq') for _ in range(3)]
    tn = [A.tile([128, 512], F32, 'etn') for _ in range(3)]
    qn = [A.tile([128, 512], BF16, 'eqn') for _ in range(4)]
    sm = A.tile([128, 24], F32, 'esm')
    qst = [A.tile([128, 4, TB], BF16, 'eqst') for _ in range(2)]
    vst = [A.tile([128, 8, 512], BF16, 'evst') for _ in range(2)]
    st = {'k': 0, 'blk': 0}
    dq = Deferred()
    qkTv = qkT.rearrange("m d t -> d m t")

    def on_block(tb, bi):
        st['slot'] = st['blk'] % 2
        st['blk'] += 1

    def epi(tb, bi, ti, bank, nsz):
        kind, a0 = specs[bi]
        slot = st['slot']
        if kind == 'v':
            S.add('act', lambda e: e.copy(vst[slot][:, ti, :], ps[bank][:, :512]), reads=[('ps', bank)], writes=[('vst', slot, ti)])
            dq.tick()
            return
        wn = wq if kind == 'q' else wk
        cnt = st['k']
        st['k'] += 1
        k = cnt % 3
        kq = cnt % 4
        S.add('act', lambda e: e.activation(sq[k][:], ps[bank][:, :512], AF.Square), reads=[('ps', bank)], writes=[('sq', k)])
        ssa = sm[:, k * 8:k * 8 + 4]
        rsa = sm[:, k * 8 + 4:k * 8 + 8]
        S.add('dve', lambda e: e.tensor_reduce(ssa, sq[k][:].rearrange("p (g d) -> p g d", g=4), AX.X, ALU.add), reads=[('sq', k)], writes=[('ss', k)])
        rstd_ops(C, ssa, rsa, 128, [('ss', k)], ('e', k))
        S.add('dve', lambda e: e.tensor_tensor(tn[k][:].rearrange("p (g d) -> p g d", g=4), ps[bank][:, :512].rearrange("p (g d) -> p g d", g=4), bc(rsa, [128, 4, 128], 2), ALU.mult),
              reads=[('ps', bank), ('rs', ('e', k))], writes=[('tn', k)])
        S.add('pool', lambda e: e.tensor_tensor(qn[kq][:].rearrange("p (g d) -> p g d", g=4), tn[k][:].rearrange("p (g d) -> p g d", g=4), bc(wn[:], [128, 4, 128], 1), ALU.mult),
              reads=[('tn', k)], writes=[('qn', kq)])

        def tail():
            tbk = 6 + cnt % 2
            for c in range(4):
                S.add('pe', lambda e, c=c: e.transpose(psb[tbk][:, c * 128:(c + 1) * 128], qn[kq][:, c * 128:(c + 1) * 128], C.ident[:]), reads=[('qn', kq)], writes=[('ps', tbk)])
            dst = qst[slot][:, :, ti * 128:(ti + 1) * 128]
            srcv = psb[tbk][:, 0:512].rearrange("p (c t) -> p c t", c=4)
            if cnt % 2 == 0:
                S.add('act', lambda e: e.copy(dst, srcv), reads=[('ps', tbk)], writes=[('qst', slot, ti)])
            else:
                S.add('dve', lambda e: e.tensor_copy(dst, srcv), reads=[('ps', tbk)], writes=[('qst', slot, ti)])

        dq.push(tail)
        dq.tick()

    def end_block(tb, bi):
        kind, a0 = specs[bi]
        slot = st['slot']
        if kind == 'v':
            dst = vdst[tb * TB:(tb + 1) * TB, a0:a0 + 512].rearrange("(t p) n -> p t n", p=128)
            S.dma(dst, vst[slot][:], reads=[('vst', slot, ti) for ti in range(8)])
        else:
            dq.push(lambda: S.dma(qkTv[:, a0:a0 + 4, tb * TB:(tb + 1) * TB], qst[slot][:], reads=[('qst', slot, ti) for ti in range(8)]))

    return epi, on_block, end_block, dq.flush


def load_bcast(C, ap1d, n, name):
    t = C.A.tile([128, n], F32, name)
    C.S.dma(t[:], ap1d.partition_broadcast(128), writes=[name])
    return t


def layer_diff(C, I, hT, x_src, x_dst, layer_idx):
    nc, S, A, ps, psb = C.nc, C.S, C.A, C.ps, C.psb
    lam_init = 0.8 - 0.6 * math.exp(-0.3 * layer_idx)
    qkT = nc.dram_tensor("a_qkT", [32, 128, SL], BF16).ap()
    vv = nc.dram_tensor("a_v", [SL, D], BF16).ap()
    oT = nc.dram_tensor("a_oT", [D, SL], BF16).ap()
    A.push()
    wq = load_bcast(C, I['a_q_norm_w'], 128, 'awq')
    wk = load_bcast(C, I['a_k_norm_w'], 128, 'awk')
    specs = [('q', 4 * i) for i in range(4)] + [('k', 16 + 4 * i) for i in range(4)] + [('v', 512 * i) for i in range(4)]
    epi, on_block, end_block, fin = make_qkv_epi(C, specs, qkT, vv, wq, wk)
    phase_linear(C, hT, C.Wb['a_w_in'], D, [(i * 512, 512) for i in range(12)], epi, on_block=on_block, end_block=end_block, finish=fin)
    A.pop()
    A.push()
    lt = A.tile([128, 4, 128], F32, 'alt')
    for i, nm in enumerate(['a_lambda_q1', 'a_lambda_k1', 'a_lambda_q2', 'a_lambda_k2']):
        S.dma(lt[:, i, :], I[nm].partition_broadcast(128), writes=[('lt', i)])
    lsm = A.tile([128, 8], F32, 'alsm')
    ljunk = A.tile([128, 128], F32, 'alj')
    for i in range(2):
        S.add('dve', lambda e, i=i: e.tensor_tensor(ljunk[:], lt[:, 2 * i, :], lt[:, 2 * i + 1, :], ALU.mult), reads=[('lt', 2 * i), ('lt', 2 * i + 1)], writes=['lj'])
        S.add('dve', lambda e, i=i: e.tensor_reduce(lsm[:, i:i + 1], ljunk[:], AX.X, ALU.add), reads=['lj'], writes=[('ld', i)])
        S.add('act', lambda e, i=i: e.activation(lsm[:, 2 + i:3 + i], lsm[:, i:i + 1], AF.Exp), reads=[('ld', i)], writes=[('le', i)])
    S.add('dve', lambda e: e.tensor_tensor(lsm[:, 4:5], lsm[:, 3:4], lsm[:, 2:3], ALU.subtract), reads=[('le', 0), ('le', 1)], writes=['l4'])
    S.add('dve', lambda e: e.tensor_scalar(lsm[:, 5:6], lsm[:, 4:5], -lam_init, None, ALU.add), reads=['l4'], writes=['neglam'])
    neglam = lsm[:, 5:6]
    wsub = load_bcast(C, I['a_subln_w'], 256, 'awsub')
    S.add('dve', lambda e: e.tensor_scalar(wsub[:], wsub[:], 1.0 - lam_init, None, ALU.mult), reads=['awsub'], writes=['awsub'])
    qT = [A.tile([128, 2, SL], BF16, 'aqT') for _ in range(2)]
    kT = [A.tile([128, 2, SL], BF16, 'akT') for _ in range(2)]
    Va = [A.tile([128, 32, 257], BF16, 'aVa') for _ in range(2)]
    E = [A.tile([128, 512], BF16, 'aE') for _ in range(5)]
    on0 = [A.tile([128, 256], F32, 'aon0') for _ in range(4)]
    dif = [A.tile([128, 256], F32, 'adif') for _ in range(2)]
    ob = [A.tile([128, 256], BF16, 'aob') for _ in range(2)]
    junk = A.tile([128, 256], BF16, 'ajunk')
    sm = A.tile([128, 16], F32, 'asm')
    ost = [A.tile([128, 2, 512], BF16, 'aost') for _ in range(2)]
    qkTv = qkT.rearrange("m d t -> d m t")
    oTv = oT.rearrange("(c p) t -> p c t", p=128)
    scale = 128 ** -0.5
    for s in range(2):
        S.add('pool', lambda e, s=s: e.memset(Va[s][:, :, 256:257], 1.0), writes=[('Vone', s)])
    ei = 0
    sti = 0
    di = 0
    bi_ = 0
    for hp in range(8):
        s = hp % 2
        S.dma(qT[s][:], qkTv[:, 2 * hp:2 * hp + 2, :], writes=[('qT', s)])
        S.dma(kT[s][:], qkTv[:, 16 + 2 * hp:16 + 2 * hp + 2, :], writes=[('kT', s)])
        S.dma(Va[s][:, :, 0:256], vv[:, hp * 256:(hp + 1) * 256].rearrange("(t p) n -> p t n", p=128), writes=[('V', s)])
        for qb in range(8):
            oslot = bi_ % 2
            bi_ += 1
            for m in range(2):
                nkt = 4 * (qb + 1)
                pend = {}

                def emit_st(kt, s=s, m=m, qb=qb):
                    nonlocal sti, ei
                    sb = 4 + (sti % 3)
                    sti += 1
                    S.add('pe', lambda e, sb=sb, kt=kt, s=s, m=m, qb=qb: e.matmul(ps[sb][:, :512], kT[s][:, m, kt * 128:(kt + 1) * 128], qT[s][:, m, qb * 512:(qb + 1) * 512], start=True, stop=True),
                          reads=[('kT', s), ('qT', s)], writes=[('ps', sb)])
                    pend[kt] = sb

                emit_st(0)
                if nkt > 1:
                    emit_st(1)
                for kt in range(nkt):
                    if kt + 2 < nkt:
                        emit_st(kt + 2)
                    sb = pend.pop(kt)
                    k = ei % 5
                    ei += 1
                    S.add('act', lambda e, sb=sb, k=k: e.activation(E[k][:], ps[sb][:, :512], AF.Exp, scale=scale), reads=[('ps', sb)], writes=[('E', k)])
                    jd = kt - 4 * qb
                    if jd >= 0:
                        S.add('pool', lambda e, k=k, jd=jd: e.tensor_tensor(E[k][:, jd * 128:(jd + 1) * 128], E[k][:, jd * 128:(jd + 1) * 128], C.mle[:], ALU.mult), reads=[('E', k)], writes=[('E', k)])
                    for j in range(4):
                        if jd > j:
                            continue
                        last_kt = 4 * qb + j
                        S.add('pe', lambda e, j=j, k=k, s=s, kt=kt, last_kt=last_kt: e.matmul(ps[j][:, :257], E[k][:, j * 128:(j + 1) * 128], Va[s][:, kt, :], start=(kt == 0), stop=(kt == last_kt)),
                              reads=[('E', k), ('V', s), ('Vone', s)], writes=[('ps', j)])
                for j in range(4):
                    rl_ = sm[:, j:j + 1]
                    S.add('dve', lambda e, j=j, rl_=rl_: e.reciprocal(rl_, ps[j][:, 256:257]), reads=[('ps', j)], writes=[('rl', j)])
                    if m == 0:
                        S.add('dve', lambda e, j=j, rl_=rl_: e.tensor_scalar(on0[j][:], ps[j][:, 0:256], rl_, None, ALU.mult), reads=[('ps', j), ('rl', j)], writes=[('on0', j)])
                    else:
                        d = di % 2
                        di += 1
                        rl2 = sm[:, 4 + j:5 + j]
                        S.add('dve', lambda e, rl2=rl2, rl_=rl_: e.tensor_tensor(rl2, rl_, neglam, ALU.mult), reads=[('rl', j), 'neglam'], writes=[('rl2', j)])
                        S.add('dve', lambda e, j=j, d=d, rl2=rl2: e.scalar_tensor_tensor(dif[d][:], ps[j][:, 0:256], rl2, on0[j][:], ALU.mult, ALU.add),
                              reads=[('ps', j), ('rl2', j), ('on0', j)], writes=[('dif', d)])
                        ssd = sm[:, 8 + d:9 + d]
                        rsd = sm[:, 10 + d:11 + d]
                        S.add('act', lambda e, d=d, ssd=ssd: e.activation(junk[:], dif[d][:], AF.Square, accum_out=ssd), reads=[('dif', d)], writes=[('ssd', d)])
                        rstd_ops(C, ssd, rsd, 256, [('ssd', d)], ('a', d))
                        S.add('dve', lambda e, d=d, rsd=rsd: e.scalar_tensor_tensor(ob[d][:], dif[d][:], rsd, wsub[:], ALU.mult, ALU.mult), reads=[('dif', d), ('rs', ('a', d)), 'awsub'], writes=[('ob', d)])
                        for c in range(2):
                            S.add('pe', lambda e, d=d, c=c: e.transpose(psb[7][:, c * 128:(c + 1) * 128], ob[d][:, c * 128:(c + 1) * 128], C.ident[:]), reads=[('ob', d)], writes=[('ps', 7)])
                        S.add('act', lambda e, oslot=oslot, j=j: e.copy(ost[oslot][:, :, j * 128:(j + 1) * 128], psb[7][:, 0:256].rearrange("p (c t) -> p c t", c=2)),
                              reads=[('ps', 7)], writes=[('ost', oslot, j)])
            S.dma(oTv[:, 2 * hp:2 * hp + 2, qb * 512:(qb + 1) * 512], ost[oslot][:], reads=[('ost', oslot, j) for j in range(4)])
    A.pop()
    S.barrier()
    phase_outproj(C, oT, C.Wb['a_w_out'], D, x_src, x_dst)


WSHAPES = {
    'a_w_in': (D, 3 * D), 'a_w_out': (D, D),
    'b_w_in': (D, 12288), 'b_w_out': (4096, D),
    'c_w_in': (D, 10304), 'c_w_out': (4096, D),
    'd_w_in': (D, 18432), 'd_w_out': (D, D),
}
SMALL = {
    'a_q_norm_w': [128], 'a_k_norm_w': [128], 'a_lambda_q1': [128], 'a_lambda_k1': [128], 'a_lambda_q2': [128],
    'a_lambda_k2': [128], 'a_subln_w': [256], 'b_gn_w': [4096], 'c_conv_wT': [6144, 4], 'c_conv_bT': [128, 48],
    'c_dt_bias': [64], 'c_a_log': [64], 'c_d_skip': [64], 'c_norm_w': [4096], 'd_q_norm_w': [3, 128], 'd_k_norm_w': [3, 128],
    'r_cos': [SL, 256], 'r_ssin': [SL, 256], 'r_qs': [128, 8], 'r_ks': [128, 8],
}
LAYER_W = {0: ['a_w_in', 'a_w_out'], 1: ['b_w_in', 'b_w_out'], 2: ['c_w_in', 'c_w_out'], 3: ['d_w_in', 'd_w_out']}
LAYER_S = {0: ['a_q_norm_w', 'a_k_norm_w', 'a_lambda_q1', 'a_lambda_k1', 'a_lambda_q2', 'a_lambda_k2', 'a_subln_w'],
           1: ['b_gn_w', 'r_cos', 'r_ssin', 'r_qs', 'r_ks'],
           2: ['c_conv_wT', 'c_conv_bT', 'c_dt_bias', 'c_a_log', 'c_d_skip', 'c_norm_w'],
           3: ['d_q_norm_w', 'd_k_norm_w']}


def build(layers=(0, 1, 2, 3), mlp=True, dbg=False):
    nc = bass.Bass("TRN2", target_bir_lowering=False)
    C = Ctx()
    C.dbg = dbg
    C.nc = nc
    C.S = Sched(nc)
    C.A = SbufAlloc(nc)
    I = {}

    def inp(name, shape):
        I[name] = nc.dram_tensor(name, list(shape), F32, kind="ExternalInput").ap()

    inp('x', [SL, D])
    inp('norm1_w', [4, D])
    inp('norm2_w', [4, D])
    y = nc.dram_tensor("y", [SL, D], F32, kind="ExternalOutput").ap()
    C.Wb = {}
    C.Wn = {}
    pairs = []
    for li in layers:
        for nm in LAYER_W[li]:
            inp(nm, WSHAPES[nm])
            dst = nc.dram_tensor(nm + '_bf', list(WSHAPES[nm]), BF16).ap()
            C.Wb[nm] = dst
            C.Wn[id(dst)] = nm
            pairs.append((I[nm], dst))
        for nm in LAYER_S[li]:
            inp(nm, SMALL[nm])
        if mlp:
            inp(f'mlp_w1_{li}', [D, 4 * D])
            inp(f'mlp_w2_{li}', [4 * D, D])
            for nm, shp in [(f'mlp_w1_{li}', [D, 4 * D]), (f'mlp_w2_{li}', [4 * D, D])]:
                dst = nc.dram_tensor(nm + '_bf', shp, BF16).ap()
                C.Wb[nm] = dst
                C.Wn[id(dst)] = nm
                pairs.append((I[nm], dst))
    setup_consts(C)
    C.bg = BgConv(C)
    for src, dst in pairs:
        C.bg.add(C.Wn[id(dst)], src, dst)
    hT = nc.dram_tensor("hT", [D, SL], BF16).ap()
    x_src = I['x']
    for li in layers:
        phase_norm(C, x_src, I['norm1_w'][li], hT)
        [layer_diff, layer_ret, layer_ssd, layer_dil][li](C, I, hT, x_src, y, li)
        x_src = y
        if mlp:
            phase_norm(C, y, I['norm2_w'][li], hT)
            phase_mlp(C, hT, C.Wb[f'mlp_w1_{li}'], C.Wb[f'mlp_w2_{li}'], y)
    C.S.emit()
    C.I = I
    return nc, C


def host_consts():
    out = {}
    theta = np.repeat(1.0 / (10000.0 ** np.linspace(0.0, 1.0, 128, dtype=np.float32)), 2).astype(np.float32)
    ang = (np.arange(SL, dtype=np.float32)[:, None] * theta[None, :]).astype(np.float32)
    cos = np.cos(ang.astype(np.float64)).astype(np.float32)
    sin = np.sin(ang.astype(np.float64)).astype(np.float32)
    ssin = sin.copy()
    ssin[:, 0::2] *= -1.0
    out['r_cos'] = cos
    out['r_ssin'] = ssin
    gam = 1.0 - 2.0 ** (-5.0 - np.arange(8, dtype=np.float64))
    i1 = np.arange(128, dtype=np.float64)[:, None] + 1.0
    out['r_qs'] = (gam[None, :] ** i1).astype(np.float32)
    out['r_ks'] = ((gam[None, :] ** (-i1)) * (256 ** -0.5)).astype(np.float32)
    return out


def make_in_map(inputs, b, layers=(0, 1, 2, 3), mlp=True):
    hc = host_consts()
    m = {'x': np.ascontiguousarray(inputs['x'][b]), 'norm1_w': np.asarray(inputs['norm1_w']), 'norm2_w': np.asarray(inputs['norm2_w'])}
    for li in layers:
        for nm in LAYER_W[li]:
            m[nm] = np.asarray(inputs[nm])
        for nm in LAYER_S[li]:
            if nm in hc:
                m[nm] = hc[nm]
            elif nm == 'c_conv_wT':
                m[nm] = np.ascontiguousarray(np.asarray(inputs['c_conv_w']).T)
            elif nm == 'c_conv_bT':
                m[nm] = np.ascontiguousarray(np.asarray(inputs['c_conv_b']).reshape(48, 128).T)
            else:
                m[nm] = np.asarray(inputs[nm])
        if mlp:
            m[f'mlp_w1_{li}'] = np.ascontiguousarray(inputs['mlp_w1'][li])
            m[f'mlp_w2_{li}'] = np.ascontiguousarray(inputs['mlp_w2'][li])
    return m


def layer_ret(C, I, hT, x_src, x_dst, layer_idx):
    nc, S, A, ps, psb = C.nc, C.S, C.A, C.ps, C.psb
    qT = nc.dram_tensor("b_qT", [D, SL], BF16).ap()
    kT = nc.dram_tensor("b_kT", [D, SL], BF16).ap()
    ktm = nc.dram_tensor("b_ktm", [SL, D], BF16).ap()
    vtm = nc.dram_tensor("b_vtm", [SL, 4096], BF16).ap()
    sgt = nc.dram_tensor("b_sgt", [SL, 4096], BF16).ap()
    oT = nc.dram_tensor("b_oT", [4096, SL], BF16).ap()
    qTv = qT.rearrange("(c p) t -> p c t", p=128)
    kTv = kT.rearrange("(c p) t -> p c t", p=128)
    oTv = oT.rearrange("(c p) t -> p c t", p=128)
    A.push()
    qs = A.tile([128, 8], F32, 'bqs')
    ks = A.tile([128, 8], F32, 'bks')
    S.dma(qs[:], I['r_qs'], writes=['bqs'])
    S.dma(ks[:], I['r_ks'], writes=['bks'])
    cst = A.tile([128, 8, 256], F32, 'bcos')
    sst = A.tile([128, 8, 256], F32, 'bsin')
    t1 = [A.tile([128, 512], F32, 'bt1') for _ in range(3)]
    t2 = [A.tile([128, 512], F32, 'bt2') for _ in range(3)]
    qn = [A.tile([128, 512], BF16, 'bqn') for _ in range(4)]
    dq = Deferred()
    qst = [A.tile([128, 4, TB], BF16, 'bqst') for _ in range(2)]
    vst = [A.tile([128, 8, 512], BF16, 'bvst') for _ in range(2)]
    st = {'k': 0, 'blk': 0}

    def on_tb(tb):
        S.dma(cst[:], I['r_cos'][tb * TB:(tb + 1) * TB, :].rearrange("(t p) n -> p t n", p=128), writes=['bcos'])
        S.dma(sst[:], I['r_ssin'][tb * TB:(tb + 1) * TB, :].rearrange("(t p) n -> p t n", p=128), writes=['bsin'])

    def on_block(tb, bi):
        st['slot'] = st['blk'] % 2
        st['blk'] += 1

    def epi(tb, bi, ti, bank, nsz):
        slot = st['slot']
        if bi >= 8:
            if bi < 16:
                S.add('act', lambda e: e.copy(vst[slot][:, ti, :], ps[bank][:, :512]), reads=[('ps', bank)], writes=[('vst', slot, ti)])
            else:
                S.add('act', lambda e: e.activation(vst[slot][:, ti, :], ps[bank][:, :512], AF.Silu), reads=[('ps', bank)], writes=[('vst', slot, ti)])
            dq.tick()
            return
        isq = bi < 4
        sc = qs if isq else ks
        scn = 'bqs' if isq else 'bks'
        cnt = st['k']
        st['k'] += 1
        k = cnt % 3
        kq = cnt % 4
        pv3 = ps[bank][:, :512].rearrange("p (h d) -> p h d", h=2)
        pv4 = ps[bank][:, :512].rearrange("p (h i two) -> p h i two", h=2, two=2)
        t14 = t2[k][:].rearrange("p (h i two) -> p h i two", h=2, two=2)
        cosb = bc(cst[:, ti, :], [128, 2, 256], 1)
        sn3 = sst[:, ti, :].rearrange("p (i two) -> p i two", two=2)
        S.add('dve', lambda e: e.tensor_tensor(t1[k][:].rearrange("p (h d) -> p h d", h=2), pv3, cosb, ALU.mult), reads=[('ps', bank), 'bcos'], writes=[('t1', k)])
        S.add('dve', lambda e: e.tensor_tensor(t14[:, :, :, 0], pv4[:, :, :, 1], bc(sn3[:, :, 0], [128, 2, 128], 1), ALU.mult), reads=[('ps', bank), 'bsin'], writes=[('t2a', k)])
        S.add('dve', lambda e: e.tensor_tensor(t14[:, :, :, 1], pv4[:, :, :, 0], bc(sn3[:, :, 1], [128, 2, 128], 1), ALU.mult), reads=[('ps', bank), 'bsin'], writes=[('t2b', k)])
        S.add('pool', lambda e: e.tensor_tensor(t1[k][:], t1[k][:], t2[k][:], ALU.add), reads=[('t1', k), ('t2a', k), ('t2b', k)], writes=[('t1', k)])
        for hh in range(2):
            h = (bi % 4) * 2 + hh
            S.add('act', lambda e, hh=hh, h=h: e.mul(qn[kq][:, hh * 256:(hh + 1) * 256], t1[k][:, hh * 256:(hh + 1) * 256], sc[:, h:h + 1]), reads=[('t1', k), scn], writes=[('qn', kq, hh)])
        if not isq:
            S.add('pool', lambda e: e.tensor_copy(vst[slot][:, ti, :], qn[kq][:]), reads=[('qn', kq, 0), ('qn', kq, 1)], writes=[('vst', slot, ti)])

        def tail():
            tbk = 6 + cnt % 2
            for c in range(4):
                S.add('pe', lambda e, c=c: e.transpose(psb[tbk][:, c * 128:(c + 1) * 128], qn[kq][:, c * 128:(c + 1) * 128], C.ident[:]), reads=[('qn', kq, c // 2)], writes=[('ps', tbk)])
            dst = qst[slot][:, :, ti * 128:(ti + 1) * 128]
            srcv = psb[tbk][:, 0:512].rearrange("p (c t) -> p c t", c=4)
            S.add('dve', lambda e: e.tensor_copy(dst, srcv), reads=[('ps', tbk)], writes=[('qst', slot, ti)])

        dq.push(tail)
        dq.tick()

    def end_block(tb, bi):
        slot = st['slot']
        rows = slice(tb * TB, (tb + 1) * TB)
        if bi < 8:
            dstT = (qTv if bi < 4 else kTv)[:, (bi % 4) * 4:(bi % 4) * 4 + 4, rows]
            dq.push(lambda: S.dma(dstT, qst[slot][:], reads=[('qst', slot, ti) for ti in range(8)]))
        if bi >= 4:
            if bi < 8:
                dst = ktm[rows, (bi - 4) * 512:(bi - 3) * 512]
            elif bi < 16:
                dst = vtm[rows, (bi - 8) * 512:(bi - 7) * 512]
            else:
                dst = sgt[rows, (bi - 16) * 512:(bi - 15) * 512]
            S.dma(dst.rearrange("(t p) n -> p t n", p=128), vst[slot][:], reads=[('vst', slot, ti) for ti in range(8)])

    phase_linear(C, hT, C.Wb['b_w_in'], D, [(i * 512, 512) for i in range(24)], epi, on_tb=on_tb, on_block=on_block, end_block=end_block, finish=dq.flush)
    A.pop()
    A.push()
    gnw = load_bcast(C, I['b_gn_w'], 4096, 'bgnw')
    qTc = [A.tile([128, 16, 128], BF16, 'bqTc') for _ in range(2)]
    kTc = [A.tile([128, 16, 128], BF16, 'bkTc') for _ in range(2)]
    kc_ = [A.tile([128, D], BF16, 'bkc') for _ in range(2)]
    vc_ = [A.tile([128, 4096], BF16, 'bvc') for _ in range(2)]
    gc_ = [A.tile([128, 4096], BF16, 'bgc') for _ in range(2)]
    R = A.tile([128, 16, 512], F32, 'bR')
    Rb = A.tile([128, 16, 512], BF16, 'bRb')
    SM = [A.tile([128, 512], BF16, 'bSM') for _ in range(2)]
    tmp = [A.tile([128, 512], F32, 'btmp') for _ in range(2)]
    obf = [A.tile([128, 512], BF16, 'bobf') for _ in range(4)]
    rdq = Deferred()
    junk = A.tile([128, 512], BF16, 'bjunk')
    sm = A.tile([128, 8], F32, 'bsm')
    ost = [A.tile([128, 32, 256], BF16, 'bost') for _ in range(2)]
    ki = 0
    k2i = 0
    def loads(c):
        s = c % 2
        cols = slice(c * 128, (c + 1) * 128)
        S.dma(qTc[s][:], qTv[:, :, cols], writes=[('qTc', s)])
        S.dma(kTc[s][:], kTv[:, :, cols], writes=[('kTc', s)])
        S.dma(kc_[s][:], ktm[cols, :], writes=[('kc', s)])
        S.dma(vc_[s][:], vtm[cols, :], writes=[('vc', s)])
        S.dma(gc_[s][:], sgt[cols, :], writes=[('gc', s)])

    def emit_ST(c, hg):
        s = c % 2
        b = hg % 2
        for hh in range(4):
            h = hg * 4 + hh
            for dc in range(2):
                S.add('pe', lambda e, b=b, hh=hh, h=h, dc=dc, s=s: e.matmul(ps[b][:, hh * 128:(hh + 1) * 128], kTc[s][:, h * 2 + dc, :], qTc[s][:, h * 2 + dc, :], start=(dc == 0), stop=(dc == 1)),
                      reads=[('kTc', s), ('qTc', s)], writes=[('ps', b)])
        return b

    loads(0)
    pend = emit_ST(0, 0)
    for c in range(32):
        s = c % 2
        oslot = (c // 2) % 2
        for hg in range(2):
            b = pend
            if hg == 0:
                pend = emit_ST(c, 1)
            elif c + 1 < 32:
                loads(c + 1)
                pend = emit_ST(c + 1, 0)
            k = ki % 2
            ki += 1
            S.add('dve', lambda e, b=b, k=k: e.tensor_tensor(SM[k][:], ps[b][:, :512], C.mle4[:].rearrange("p a b -> p (a b)"), ALU.mult), reads=[('ps', b)], writes=[('SM', k)])
            for hh in range(4):
                h = hg * 4 + hh
                gC = (1.0 - 2.0 ** (-5.0 - h)) ** 128
                ob_ = 2 + (h % 2)
                S.add('pe', lambda e, ob_=ob_, k=k, hh=hh, h=h, s=s, c=c: e.matmul(ps[ob_][:, :512], SM[k][:, hh * 128:(hh + 1) * 128], vc_[s][:, h * 512:(h + 1) * 512], start=True, stop=(c == 0)),
                      reads=[('SM', k), ('vc', s)], writes=[('ps', ob_)])
                if c > 0:
                    for dc in range(2):
                        S.add('pe', lambda e, ob_=ob_, h=h, dc=dc, s=s: e.matmul(ps[ob_][:, :512], qTc[s][:, h * 2 + dc, :], Rb[:, h * 2 + dc, :], start=False, stop=(dc == 1)),
                              reads=[('qTc', s), ('Rb', h, dc)], writes=[('ps', ob_)])
                if c < 31:
                    for dc in range(2):
                        kvb = 4 + dc
                        S.add('pe', lambda e, kvb=kvb, h=h, dc=dc, s=s: e.matmul(ps[kvb][:, :512], kc_[s][:, h * 256 + dc * 128:h * 256 + (dc + 1) * 128], vc_[s][:, h * 512:(h + 1) * 512], start=True, stop=True),
                              reads=[('kc', s), ('vc', s)], writes=[('ps', kvb)])
                        Rs = R[:, h * 2 + dc, :]
                        if c == 0:
                            S.add('dve', lambda e, kvb=kvb, Rs=Rs, gC=gC: e.tensor_scalar(Rs, ps[kvb][:, :512], gC, None, ALU.mult), reads=[('ps', kvb)], writes=[('R', h, dc)])
                        else:
                            S.add('act', lambda e, Rs=Rs, gC=gC: e.mul(Rs, Rs, gC), reads=[('R', h, dc)], writes=[('R', h, dc)])
                            S.add('dve', lambda e, kvb=kvb, Rs=Rs, gC=gC: e.scalar_tensor_tensor(Rs, ps[kvb][:, :512], gC, Rs, ALU.mult, ALU.add), reads=[('ps', kvb), ('R', h, dc)], writes=[('R', h, dc)])
                        S.add('act', lambda e, Rs=Rs, h=h, dc=dc: e.copy(Rb[:, h * 2 + dc, :], Rs), reads=[('R', h, dc)], writes=[('Rb', h, dc)])
                k2 = k2i % 2
                k2i += 1
                ssa = sm[:, k2:k2 + 1]
                rsa = sm[:, 2 + k2:3 + k2]
                S.add('act', lambda e, ob_=ob_, ssa=ssa: e.activation(junk[:], ps[ob_][:, :512], AF.Square, accum_out=ssa), reads=[('ps', ob_)], writes=[('ssb', k2)])
                rstd_ops(C, ssa, rsa, 512, [('ssb', k2)], ('b', k2))
                S.add('dve', lambda e, ob_=ob_, k2=k2, rsa=rsa, h=h: e.scalar_tensor_tensor(tmp[k2][:], ps[ob_][:, :512], rsa, gnw[:, h * 512:(h + 1) * 512], ALU.mult, ALU.mult),
                      reads=[('ps', ob_), ('rs', ('b', k2)), 'bgnw'], writes=[('tmp', k2)])
                ko = (k2i - 1) % 4
                S.add('dve', lambda e, k2=k2, ko=ko, h=h, s=s: e.tensor_tensor(obf[ko][:], tmp[k2][:], gc_[s][:, h * 512:(h + 1) * 512], ALU.mult), reads=[('tmp', k2), ('gc', s)], writes=[('obf', ko)])

                def tail(ko=ko, h=h, c=c, oslot=oslot, par=(k2i - 1) % 2):
                    tbk = 6 + par
                    for cc in range(4):
                        S.add('pe', lambda e, tbk=tbk, cc=cc, ko=ko: e.transpose(psb[tbk][:, cc * 128:(cc + 1) * 128], obf[ko][:, cc * 128:(cc + 1) * 128], C.ident[:]), reads=[('obf', ko)], writes=[('ps', tbk)])
                    dst = ost[oslot][:, h * 4:(h + 1) * 4, (c % 2) * 128:(c % 2 + 1) * 128]
                    srcv = psb[tbk][:, 0:512].rearrange("p (c t) -> p c t", c=4)
                    if par == 0:
                        S.add('act', lambda e, dst=dst, srcv=srcv: e.copy(dst, srcv), reads=[('ps', tbk)], writes=[('ost', oslot, c % 2, h)])
                    else:
                        S.add('dve', lambda e, dst=dst, srcv=srcv: e.tensor_copy(dst, srcv), reads=[('ps', tbk)], writes=[('ost', oslot, c % 2, h)])

                rdq.push(tail)
                rdq.tick()
        if c % 2 == 1:
            rdq.push(lambda c=c, oslot=oslot: S.dma(oTv[:, :, (c - 1) * 128:(c + 1) * 128], ost[oslot][:], reads=[('ost', oslot, cc, h) for cc in range(2) for h in range(8)]))
    rdq.flush()
    A.pop()
    S.barrier()
    phase_outproj(C, oT, C.Wb['b_w_out'], 4096, x_src, x_dst)


def emit_T16(C, hsrc, rtoks, stt, j, toks, tag):
    S, psb = C.S, C.psb
    for half in range(2):
        bank = 6 + half
        for c in range(8):
            cc = half * 8 + c
            S.add('pe', lambda e, c=c, cc=cc, bank=bank: e.transpose(psb[bank][:, c * 128:(c + 1) * 128], hsrc[:, cc * 128:(cc + 1) * 128], C.ident[:]),
                  reads=rtoks, writes=[('ps', bank)])
        dst = stt[:, half * 8:(half + 1) * 8, j * 128:(j + 1) * 128]
        srcv = psb[bank][:, 0:1024].rearrange("p (c t) -> p c t", c=8)
        tk = (tag, j, half)
        toks.append(tk)
        if half == 0:
            S.add('act', lambda e, dst=dst, srcv=srcv: e.copy(dst, srcv), reads=[('ps', bank)], writes=[tk])
        else:
            S.add('dve', lambda e, dst=dst, srcv=srcv: e.tensor_copy(dst, srcv), reads=[('ps', bank)], writes=[tk])


def layer_dil(C, I, hT, x_src, x_dst, layer_idx):
    nc, S, A, ps, psb = C.nc, C.S, C.A, C.ps, C.psb
    qkT = nc.dram_tensor("d_qkT", [3, 32, 128, SL], BF16).ap()
    vv = nc.dram_tensor("d_v", [3, SL, D], BF16).ap()
    Og = nc.dram_tensor("d_Og", [3, SL, 16, 129], F32).ap()
    oT = nc.dram_tensor("d_oT", [D, SL], BF16).ap()
    specs = [('q', 4 * i) for i in range(4)] + [('k', 16 + 4 * i) for i in range(4)] + [('v', 512 * i) for i in range(4)]
    for g3 in range(3):
        A.push()
        wq = load_bcast(C, I['d_q_norm_w'][g3], 128, 'dwq')
        wk = load_bcast(C, I['d_k_norm_w'][g3], 128, 'dwk')
        epi, on_block, end_block, fin = make_qkv_epi(C, specs, qkT[g3], vv[g3], wq, wk)
        phase_linear(C, hT, C.Wb['d_w_in'], D, [(g3 * 6144 + i * 512, 512) for i in range(12)], epi, on_block=on_block, end_block=end_block, finish=fin)
        A.pop()
    A.push()
    qTh = [A.tile([128, SL], BF16, 'dqT') for _ in range(2)]
    kTh = [A.tile([128, SL], BF16, 'dkT') for _ in range(2)]
    Vh = [A.tile([128, 32, 129], BF16, 'dVh') for _ in range(2)]
    E = [A.tile([128, 512], BF16, 'dE') for _ in range(3)]
    Ost = [A.tile([128, 32, 129], F32, 'dOst') for _ in range(2)]
    for s in range(2):
        S.add('pool', lambda e, s=s: e.memset(Vh[s][:, :, 128:129], 1.0), writes=[('Vone', s)])
    scale = 128 ** -0.5
    it = 0
    sti = 0
    ei = 0
    oi = 0
    for g3, dil in enumerate([1, 4, 16]):
        nblk = SL // dil // 128
        for h in range(16):
            s = it % 2
            it += 1
            S.dma(qTh[s][:], qkT[g3, h], writes=[('qT', s)])
            S.dma(kTh[s][:], qkT[g3, 16 + h], writes=[('kT', s)])
            vsrc = vv[g3][:, h * 128:(h + 1) * 128].rearrange("(nb a r) d -> a r nb d", a=128, r=dil)
            for r in range(dil):
                S.dma(Vh[s][:, r * nblk:(r + 1) * nblk, 0:128], vsrc[:, r, :, :], writes=[('V', s, r)])
            vtoks = [('V', s, r) for r in range(dil)] + [('Vone', s)]
            qv = qTh[s][:].rearrange("d (nb a r) -> d r nb a", a=128, r=dil)
            kv = kTh[s][:].rearrange("d (nb a r) -> d r nb a", a=128, r=dil)
            units = [(r, nb0) for r in range(dil) for nb0 in range(0, nblk, 2)]

            def emit_st(u, s=s, kv=kv, qv=qv):
                nonlocal sti
                r, nb0 = units[u]
                sb = sti % 3
                sti += 1
                for uu in range(2):
                    nb = nb0 + uu
                    if nb > 0:
                        S.add('pe', lambda e, sb=sb, uu=uu, r=r, nb=nb: e.matmul(ps[sb][:, (2 * uu) * 128:(2 * uu + 1) * 128], kv[:, r, nb - 1, :], qv[:, r, nb, :], start=True, stop=True),
                              reads=[('kT', s), ('qT', s)], writes=[('ps', sb)])
                    S.add('pe', lambda e, sb=sb, uu=uu, r=r, nb=nb: e.matmul(ps[sb][:, (2 * uu + 1) * 128:(2 * uu + 2) * 128], kv[:, r, nb, :], qv[:, r, nb, :], start=True, stop=True),
                          reads=[('kT', s), ('qT', s)], writes=[('ps', sb)])
                return sb

            pend = emit_st(0)
            for ui, (r, nb0) in enumerate(units):
                sb = pend
                if ui + 1 < len(units):
                    pend = emit_st(ui + 1)
                k = ei % 3
                ei += 1
                S.add('act', lambda e, sb=sb, k=k: e.activation(E[k][:], ps[sb][:, :512], AF.Exp, scale=scale), reads=[('ps', sb)], writes=[('E', k)])
                S.add('dve', lambda e, k=k: e.tensor_tensor(E[k][:], E[k][:], C.mask4[:].rearrange("p a b -> p (a b)"), ALU.mult), reads=[('E', k)], writes=[('E', k)])
                ob = 4 + (oi % 2)
                oi += 1
                for u in range(2):
                    nb = nb0 + u
                    tau = r * nblk + nb
                    if nb > 0:
                        S.add('pe', lambda e, ob=ob, u=u, k=k, tau=tau, s=s: e.matmul(ps[ob][:, u * 129:(u + 1) * 129], E[k][:, (2 * u) * 128:(2 * u + 1) * 128], Vh[s][:, tau - 1, :], start=True, stop=False),
                              reads=[('E', k)] + vtoks, writes=[('ps', ob)])
                    S.add('pe', lambda e, ob=ob, u=u, k=k, tau=tau, nb=nb, s=s: e.matmul(ps[ob][:, u * 129:(u + 1) * 129], E[k][:, (2 * u + 1) * 128:(2 * u + 2) * 128], Vh[s][:, tau, :], start=(nb == 0), stop=True),
                          reads=[('E', k)] + vtoks, writes=[('ps', ob)])
                tau0 = r * nblk + nb0
                dst = Ost[s][:, tau0:tau0 + 2, :].rearrange("p a b -> p (a b)")
                if oi % 2 == 0:
                    S.add('act', lambda e, ob=ob, dst=dst: e.copy(dst, ps[ob][:, 0:258]), reads=[('ps', ob)], writes=[('Ost', s, r)])
                else:
                    S.add('dve', lambda e, ob=ob, dst=dst: e.tensor_copy(dst, ps[ob][:, 0:258]), reads=[('ps', ob)], writes=[('Ost', s, r)])
            odst = Og[g3][:, h, :].rearrange("(nb b r) c -> b r nb c", b=128, r=dil)
            for r in range(dil):
                S.dma(odst[:, r, :, :], Ost[s][:, r * nblk:(r + 1) * nblk, :], reads=[('Ost', s, r)])
    A.pop()
    S.barrier()
    A.push()
    Ot = [[A.tile([128, 16, 129], F32, 'dOt') for _ in range(3)] for _ in range(2)]
    rl = A.tile([128, 2, 16], F32, 'drl')
    ob_ = [A.tile([128, D], BF16, 'dob') for _ in range(2)]
    stt = [A.tile([128, 16, 512], BF16, 'dst') for _ in range(2)]
    oTv = oT.rearrange("(c p) t -> p c t", p=128)
    for g in range(8):
        gs = g % 2
        toks = []
        for j in range(4):
            t = g * 4 + j
            s = t % 2
            for g3 in range(3):
                S.dma(Ot[s][g3][:], Og[g3][t * 128:(t + 1) * 128], writes=[('Ot', s, g3)])
            a = Ot[s][0]
            S.add('pool', lambda e, s=s, a=a: e.tensor_tensor(a[:], a[:], Ot[s][1][:], ALU.add), reads=[('Ot', s, 0), ('Ot', s, 1)], writes=[('Ot', s, 0)])
            S.add('dve', lambda e, s=s, a=a: e.tensor_tensor(a[:], a[:], Ot[s][2][:], ALU.add), reads=[('Ot', s, 0), ('Ot', s, 2)], writes=[('Ot', s, 0)])
            S.add('dve', lambda e, s=s, a=a: e.reciprocal(rl[:, s, :], a[:, :, 128]), reads=[('Ot', s, 0)], writes=[('rl', s)])
            S.add('dve', lambda e, s=s, a=a: e.tensor_tensor(ob_[s][:].rearrange("p (h d) -> p h d", h=16), a[:, :, 0:128], bc(rl[:, s, :], [128, 16, 128], 2), ALU.mult),
                  reads=[('Ot', s, 0), ('rl', s)], writes=[('ob', s)])
            emit_T16(C, ob_[s], [('ob', s)], stt[gs], j, toks, ('dst', gs))
        S.dma(oTv[:, :, g * 512:(g + 1) * 512], stt[gs][:], reads=toks)
    A.pop()
    S.barrier()
    phase_outproj(C, oT, C.Wb['d_w_out'], D, x_src, x_dst)


def layer_ssd(C, I, hT, x_src, x_dst, layer_idx):
    nc, S, A, ps, psb = C.nc, C.S, C.A, C.ps, C.psb
    kd = "ExternalOutput" if getattr(C, 'dbg', False) else "Internal"
    szd = nc.dram_tensor("c_sz", [SL, 4096], BF16, kind=kd).ap()
    xstm = nc.dram_tensor("c_xstm", [SL, 4096], BF16, kind=kd).ap()
    Btm = nc.dram_tensor("c_Btm", [SL, 1024], BF16, kind=kd).ap()
    bcT = nc.dram_tensor("c_bcT", [2048, SL], BF16, kind=kd).ap()
    acsC = nc.dram_tensor("c_acsC", [16, 64 * 256], F32, kind=kd).ap()
    dttm = nc.dram_tensor("c_dttm", [SL, 64], F32, kind=kd).ap()
    acstm = nc.dram_tensor("c_acstm", [SL, 64], F32, kind=kd).ap()
    oT = nc.dram_tensor("c_oT", [4096, SL], BF16, kind=kd).ap()
    W = C.Wb['c_w_in']
    A.push()
    vst = [A.tile([128, 8, 512], BF16, 'cvst') for _ in range(2)]
    st = {'blk': 0}

    def on_block(tb, bi):
        st['slot'] = st['blk'] % 2
        st['blk'] += 1

    def epi(tb, bi, ti, bank, nsz):
        slot = st['slot']
        S.add('act', lambda e: e.activation(vst[slot][:, ti, :], ps[bank][:, :512], AF.Silu), reads=[('ps', bank)], writes=[('vst', slot, ti)])

    def end_block(tb, bi):
        slot = st['slot']
        dst = szd[tb * TB:(tb + 1) * TB, bi * 512:(bi + 1) * 512].rearrange("(t p) n -> p t n", p=128)
        S.dma(dst, vst[slot][:], reads=[('vst', slot, ti) for ti in range(8)])

    phase_linear(C, hT, W, D, [(i * 512, 512) for i in range(8)], epi, on_block=on_block, end_block=end_block)
    A.pop()
    A.push()
    dtT = A.tile([64, SL], F32, 'cdtT')
    cw = A.tile([128, 48, 4], F32, 'ccw')
    cb = A.tile([128, 48], F32, 'ccb')
    dtb = A.tile([64, 1], F32, 'cdtb')
    halo = A.tile([128, 48, 3], F32, 'chalo')
    S.dma(cw[:], I['c_conv_wT'].rearrange("(c p) k -> p c k", p=128), writes=['ccw'])
    S.dma(cb[:], I['c_conv_bT'], writes=['ccb'])
    S.dma(dtb[:], I['c_dt_bias'].rearrange("(p o) -> p o", o=1), writes=['cdtb'])
    S.add('pool', lambda e: e.memset(halo[:], 0.0), writes=[('halo', i) for i in range(48)])
    xr = [A.tile([128, 515], F32, 'cxr') for _ in range(2)]
    acc = [A.tile([128, 512], F32, 'cacc') for _ in range(2)]
    so = [A.tile([128, 512], BF16, 'cso') for _ in range(4)]
    dq = Deferred()
    tms = [A.tile([128, 8, 128], BF16, 'ctms') for _ in range(2)]
    e1 = A.tile([64, 512], F32, 'ce1')
    st2 = {'k': 0, 'u': 0}
    bcTv = bcT

    def epi_fm(tb, bi, j, th, bank, msz):
        if bi == 12:
            cols = slice(tb * TB + th * 512, tb * TB + (th + 1) * 512)
            S.add('act', lambda e: e.activation(e1[:], ps[bank][:64, :512], AF.Exp, bias=dtb[:, 0:1]), reads=[('ps', bank), 'cdtb'], writes=['ce1'])
            S.add('act', lambda e: e.activation(dtT[:, cols], e1[:], AF.Ln, bias=1.0), reads=['ce1'], writes=[('dtT', tb, th)])
            return
        cidx = bi * 4 + j
        cnt = st2['k']
        st2['k'] += 1
        k = cnt % 2
        ko = cnt % 4
        S.add('pool', lambda e: e.tensor_copy(xr[k][:, 0:3], halo[:, cidx, :]), reads=[('halo', cidx)], writes=[('xrh', k)])
        S.add('act', lambda e: e.copy(xr[k][:, 3:515], ps[bank][:, :512]), reads=[('ps', bank)], writes=[('xrb', k)])
        S.add('pool', lambda e: e.tensor_copy(halo[:, cidx, :], xr[k][:, 512:515]), reads=[('xrb', k), ('xrh', k)], writes=[('halo', cidx)])
        S.add('act', lambda e: e.mul(acc[k][:], xr[k][:, 0:512], cw[:, cidx, 0:1]), reads=[('xrh', k), ('xrb', k), 'ccw'], writes=[('acc', k)])
        for q in range(1, 4):
            S.add('dve', lambda e, q=q: e.scalar_tensor_tensor(acc[k][:], xr[k][:, q:q + 512], cw[:, cidx, q:q + 1], acc[k][:], ALU.mult, ALU.add),
                  reads=[('xrh', k), ('xrb', k), ('acc', k)], writes=[('acc', k)])
        S.add('act', lambda e: e.activation(so[ko][:], acc[k][:], AF.Silu, bias=cb[:, cidx:cidx + 1]), reads=[('acc', k), 'ccb'], writes=[('so', ko)])
        if cidx >= 32:
            r0 = (cidx - 32) * 128
            S.dma(bcTv[r0:r0 + 128, tb * TB + th * 512:tb * TB + (th + 1) * 512], so[ko][:], reads=[('so', ko)])
        if cidx < 40:
            if th == 0:
                st2['slot'] = st2['u'] % 2
                st2['u'] += 1
            slot = st2['slot']

            def tail():
                tbk = 6 + cnt % 2
                for c in range(4):
                    S.add('pe', lambda e, c=c: e.transpose(psb[tbk][:, c * 128:(c + 1) * 128], so[ko][:, c * 128:(c + 1) * 128], C.ident[:]), reads=[('so', ko)], writes=[('ps', tbk)])
                S.add('dve', lambda e: e.tensor_copy(tms[slot][:, th * 4:(th + 1) * 4, :], psb[tbk][:, 0:512].rearrange("p (c t) -> p c t", c=4)), reads=[('ps', tbk)], writes=[('tms', slot, th)])
                if th == 1:
                    rows = slice(tb * TB, (tb + 1) * TB)
                    if cidx < 32:
                        dst = xstm[rows, cidx * 128:(cidx + 1) * 128]
                    else:
                        dst = Btm[rows, (cidx - 32) * 128:(cidx - 31) * 128]
                    S.dma(dst.rearrange("(t p) n -> p t n", p=128), tms[slot][:], reads=[('tms', slot, 0), ('tms', slot, 1)])

            dq.push(tail)
        dq.tick()

    nblocks = [(4096 + i * 512, 512) for i in range(12)] + [(10240, 64)]
    phase_linear(C, hT, W, D, nblocks, epi_fm, fm=True, finish=dq.flush)
    alog = A.tile([64, 2], F32, 'calog')
    S.dma(alog[:, 0:1], I['c_a_log'].rearrange("(p o) -> p o", o=1), writes=['alog'])
    S.add('act', lambda e: e.activation(alog[:, 1:2], alog[:, 0:1], AF.Exp), reads=['alog'], writes=['aexp'])
    S.add('dve', lambda e: e.tensor_scalar(alog[:, 1:2], alog[:, 1:2], -1.0, None, ALU.mult), reads=['aexp'], writes=['aneg'])
    aT = A.tile([64, SL], F32, 'caT')
    acsT = A.tile([64, SL], F32, 'cacsT')
    onesf = A.tile([64, 256], F32, 'cones')
    S.add('pool', lambda e: e.memset(onesf[:], 1.0), writes=['cones'])
    S.add('dve', lambda e: e.tensor_scalar(aT[:], dtT[:], alog[:, 1:2], None, ALU.mult), reads=['aneg'], writes=['aT'])
    for c in range(16):
        cols = slice(c * 256, (c + 1) * 256)
        S.add('dve', lambda e, cols=cols: e.tensor_tensor_scan(acsT[:, cols], onesf[:], aT[:, cols], 0.0, ALU.mult, ALU.add), reads=['aT', 'cones'], writes=[('acsT', c)])
        S.dma(acsC[c].rearrange("(h i) -> h i", h=64), acsT[:, cols], reads=[('acsT', c)])
    tst = A.tile([128, 2, 32, 64], F32, 'ctst')
    for wi_, srcT in enumerate([dtT, acsT]):
        for t8 in range(4):
            bank = (wi_ * 4 + t8) % 4
            for q in range(8):
                t = t8 * 8 + q
                S.add('pe', lambda e, bank=bank, q=q, t=t, srcT=srcT: e.transpose(ps[bank][:, q * 64:(q + 1) * 64], srcT[:, t * 128:(t + 1) * 128], C.identf[:64, :64]),
                      reads=[('acsT', t // 2)], writes=[('ps', bank)])
            S.add('act', lambda e, bank=bank, wi_=wi_, t8=t8: e.copy(tst[:, wi_, t8 * 8:(t8 + 1) * 8, :], ps[bank][:, :512].rearrange("p (q h) -> p q h", q=8)),
                  reads=[('ps', bank)], writes=[('tst', wi_, t8)])
        dstd = dttm if wi_ == 0 else acstm
        S.dma(dstd.rearrange("(t p) h -> p t h", p=128), tst[:, wi_, :, :], reads=[('tst', wi_, t8) for t8 in range(4)])
    A.pop()
    S.barrier()
    A.push()
    nw = load_bcast(C, I['c_norm_w'], 4096, 'cnw')
    dsk = load_bcast(C, I['c_d_skip'], 64, 'cdsk')
    xs_c = [A.tile([128, 2, 4096], BF16, 'cxs')] * 2
    sz_c = A.tile([128, 2, 4096], BF16, 'csz')
    Bt_c = [A.tile([128, 2, 1024], BF16, 'cBt') for _ in range(2)]
    BT_c = [A.tile([128, 8, 256], BF16, 'cBT') for _ in range(2)]
    CT_c = [A.tile([128, 8, 256], BF16, 'cCT') for _ in range(2)]
    sm_c = [A.tile([128, 2, 2, 64], F32, 'csm') for _ in range(2)]
    alb = [A.tile([128, 64], F32, 'calb') for _ in range(2)]
    ar = [A.tile([128, 2048], F32, 'car') for _ in range(2)]
    eacs = A.tile([128, 2, 64], F32, 'ceacs')
    w2 = A.tile([128, 2, 64], F32, 'cw2')
    eal = A.tile([128, 64], F32, 'ceal')
    stf = A.tile([128, 8, 512], F32, 'cstf')
    stb = A.tile([128, 8, 512], BF16, 'cstb')
    xdt = [A.tile([128, 2, 512], BF16, 'cxdt') for _ in range(2)]
    xde = [A.tile([128, 2, 512], BF16, 'cxde') for _ in range(2)]
    Gsb = [A.tile([128, 384], F32, 'cG') for _ in range(2)]
    tLb = [A.tile([128, 8, 128], F32, 'ctLb') for _ in range(2)]
    M0 = [A.tile([128, 8, 256], BF16, 'cM0') for _ in range(2)]
    M1 = [A.tile([128, 8, 128], BF16, 'cM1') for _ in range(2)]
    yt = [A.tile([128, 512], F32, 'cyt') for _ in range(2)]
    t2 = [A.tile([128, 512], F32, 'ct2') for _ in range(2)]
    ob = [A.tile([128, 512], BF16, 'cob') for _ in range(2)]
    junk = A.tile([128, 512], BF16, 'cjunk')
    smn = A.tile([128, 4], F32, 'csmn')
    ost = A.tile([128, 32, 256], BF16, 'cost')
    BTv = bcT[0:1024, :].rearrange("(g p) t -> p g t", p=128)
    CTv = bcT[1024:2048, :].rearrange("(g p) t -> p g t", p=128)
    oTv = oT.rearrange("(c p) t -> p c t", p=128)
    gi = 0
    li_ = 0
    yi = 0
    for c in range(16):
        s = c % 2
        rows = slice(c * 256, (c + 1) * 256)
        S.dma(xs_c[s][:], xstm[rows, :].rearrange("(t p) n -> p t n", p=128), writes=[('xs', 0)])
        S.dma(sz_c[:], szd[rows, :].rearrange("(t p) n -> p t n", p=128), writes=['sz'])
        S.dma(Bt_c[s][:], Btm[rows, :].rearrange("(t p) n -> p t n", p=128), writes=[('Bt', s)])
        S.dma(BT_c[s][:], BTv[:, :, rows], writes=[('BT', s)])
        S.dma(CT_c[s][:], CTv[:, :, rows], writes=[('CT', s)])
        S.dma(sm_c[s][:, 0, :, :], dttm[rows, :].rearrange("(t p) h -> p t h", p=128), writes=[('dtc', s)])
        S.dma(sm_c[s][:, 1, :, :], acstm[rows, :].rearrange("(t p) h -> p t h", p=128), writes=[('acsc', s)])
        S.dma(alb[s][:], acstm[c * 256 + 255, :].partition_broadcast(128), writes=[('alb', s)])
        dt_c = sm_c[s][:, 0, :, :]
        acs_c = sm_c[s][:, 1, :, :]
        S.add('act', lambda e, acs_c=acs_c: e.activation(eacs[:], acs_c, AF.Exp), reads=[('acsc', s)], writes=['eacs'])
        S.add('dve', lambda e, acs_c=acs_c, s=s: e.tensor_tensor(w2[:], bc(alb[s][:], [128, 2, 64], 1), acs_c, ALU.subtract), reads=[('alb', s), ('acsc', s)], writes=['w2'])
        S.add('act', lambda e: e.activation(w2[:], w2[:], AF.Exp), reads=['w2'], writes=['w2'])
        S.add('dve', lambda e, dt_c=dt_c: e.tensor_tensor(w2[:], w2[:], dt_c, ALU.mult), reads=['w2', ('dtc', s)], writes=['w2'])
        S.add('act', lambda e, s=s: e.activation(eal[:], alb[s][:], AF.Exp), reads=[('alb', s)], writes=['eal'])
        stA = {}

        def stageA(g, c=c, s=s, dt_c=dt_c, acs_c=acs_c):
            nonlocal gi, li_
            sg = gi % 2
            k = gi % 2
            gi += 1
            S.dma(ar[sg][:], acsC[c, g * 2048:(g + 1) * 2048].partition_broadcast(128), writes=[('ar', sg)])
            gc = slice(g * 512, (g + 1) * 512)
            hs_ = slice(g * 8, (g + 1) * 8)
            xs4 = xs_c[s][:, :, gc].rearrange("p t (h q) -> p t h q", h=8)
            S.add('pool', lambda e, k=k, xs4=xs4, dt_c=dt_c, hs_=hs_: e.tensor_tensor(xdt[k][:].rearrange("p t (h q) -> p t h q", h=8), xs4, bc(dt_c[:, :, hs_], [128, 2, 8, 64], 3), ALU.mult),
                  reads=[('xs', 0), ('dtc', s)], writes=[('xdt', k)])
            S.add('pool', lambda e, k=k, xs4=xs4, hs_=hs_: e.tensor_tensor(xde[k][:].rearrange("p t (h q) -> p t h q", h=8), xs4, bc(w2[:, :, hs_], [128, 2, 8, 64], 3), ALU.mult),
                  reads=[('xs', 0), 'w2'], writes=[('xde', k)])
            S.add('pe', lambda e, s=s, g=g: e.matmul(ps[0][:, 0:256], BT_c[s][:, g, 0:128], CT_c[s][:, g, 0:256], start=True, stop=True), reads=[('BT', s), ('CT', s)], writes=[('ps', 0)])
            S.add('pe', lambda e, s=s, g=g: e.matmul(ps[0][:, 256:384], BT_c[s][:, g, 128:256], CT_c[s][:, g, 128:256], start=True, stop=True), reads=[('BT', s), ('CT', s)], writes=[('ps', 0)])
            S.add('dve', lambda e, k=k: e.tensor_tensor(Gsb[k][:], ps[0][:, 0:384], C.gmask[:], ALU.mult), reads=[('ps', 0)], writes=[('G', k)])
            kk = li_ % 2
            li_ += 1
            ar3 = ar[sg][:].rearrange("p (h i) -> p h i", h=8)
            acs0 = bc(acs_c[:, 0, hs_], [128, 8, 256], 2)
            acs1 = bc(acs_c[:, 1, hs_], [128, 8, 128], 2)
            S.add('dve', lambda e, kk=kk, ar3=ar3, acs1=acs1: e.tensor_tensor(tLb[kk][:], ar3[:, :, 128:256], acs1, ALU.subtract), reads=[('ar', sg), ('acsc', s)], writes=[('tLb', kk)])
            S.add('act', lambda e, kk=kk: e.activation(tLb[kk][:], tLb[kk][:], AF.Exp), reads=[('tLb', kk)], writes=[('tLb', kk)])
            S.add('dve', lambda e, kk=kk, k=k: e.scalar_tensor_tensor(M1[kk][:], tLb[kk][:], 1.0, bc(Gsb[k][:, 256:384], [128, 8, 128], 1), ALU.min, ALU.mult), reads=[('tLb', kk), ('G', k)], writes=[('M1', kk)])
            S.add('dve', lambda e, ar3=ar3, acs0=acs0: e.tensor_tensor(ar3, ar3, acs0, ALU.subtract), reads=[('ar', sg), ('acsc', s)], writes=[('ar', sg)])
            S.add('act', lambda e, sg=sg: e.activation(ar[sg][:], ar[sg][:], AF.Exp), reads=[('ar', sg)], writes=[('ar', sg)])
            S.add('dve', lambda e, kk=kk, k=k, ar3=ar3: e.scalar_tensor_tensor(M0[kk][:], ar3, 1.0, bc(Gsb[k][:, 0:256], [128, 8, 256], 1), ALU.min, ALU.mult), reads=[('ar', sg), ('G', k)], writes=[('M0', kk)])
            stA[g] = (k, kk, gc, hs_)

        stageA(0)
        for g in range(8):
            if g + 1 < 8:
                stageA(g + 1)
            k, kk, gc, hs_ = stA[g]
            if c > 0:
                for it in range(2):
                    S.add('pe', lambda e, it=it, s=s, g=g: e.matmul(ps[2 + it][:, :512], CT_c[s][:, g, it * 128:(it + 1) * 128], stb[:, g, :], start=True, stop=True),
                          reads=[('CT', s), ('stb', g)], writes=[('ps', 2 + it)])
            for hh in range(8):
                hq = slice(hh * 64, (hh + 1) * 64)
                S.add('pe', lambda e, kk=kk, k=k, hq=hq, hh=hh: e.matmul(ps[4][:, hq], M0[kk][:, hh, 0:128], xdt[k][:, 0, hq], start=True, stop=True), reads=[('M0', kk), ('xdt', k)], writes=[('ps', 4)])
                S.add('pe', lambda e, kk=kk, k=k, hq=hq, hh=hh: e.matmul(ps[5][:, hq], M0[kk][:, hh, 128:256], xdt[k][:, 0, hq], start=True, stop=False), reads=[('M0', kk), ('xdt', k)], writes=[('ps', 5)])
                S.add('pe', lambda e, kk=kk, k=k, hq=hq, hh=hh: e.matmul(ps[5][:, hq], M1[kk][:, hh, :], xdt[k][:, 1, hq], start=False, stop=True), reads=[('M1', kk), ('xdt', k)], writes=[('ps', 5)])
            for it in range(2):
                y2 = yi % 2
                yi += 1
                if c > 0:
                    S.add('dve', lambda e, it=it, y2=y2, hs_=hs_: e.tensor_tensor(yt[y2][:].rearrange("p (h q) -> p h q", h=8), ps[2 + it][:, :512].rearrange("p (h q) -> p h q", h=8), bc(eacs[:, it, hs_], [128, 8, 64], 2), ALU.mult),
                          reads=[('ps', 2 + it), 'eacs'], writes=[('yt', y2)])
                    S.add('dve', lambda e, it=it, y2=y2: e.tensor_tensor(yt[y2][:], ps[4 + it][:, :512], yt[y2][:], ALU.add), reads=[('ps', 4 + it), ('yt', y2)], writes=[('yt', y2)])
                else:
                    S.add('act', lambda e, it=it, y2=y2: e.copy(yt[y2][:], ps[4 + it][:, :512]), reads=[('ps', 4 + it)], writes=[('yt', y2)])
                S.add('pool', lambda e, it=it, y2=y2, s=s, gc=gc, hs_=hs_: e.tensor_tensor(t2[y2][:].rearrange("p (h q) -> p h q", h=8), xs_c[s][:, it, gc].rearrange("p (h q) -> p h q", h=8), bc(dsk[:, hs_], [128, 8, 64], 2), ALU.mult),
                      reads=[('xs', 0), 'cdsk'], writes=[('t2', y2)])
                S.add('pool', lambda e, y2=y2: e.tensor_tensor(yt[y2][:], yt[y2][:], t2[y2][:], ALU.add), reads=[('yt', y2), ('t2', y2)], writes=[('yt', y2)])
                S.add('dve', lambda e, it=it, y2=y2, gc=gc: e.tensor_tensor(yt[y2][:], yt[y2][:], sz_c[:, it, gc], ALU.mult), reads=[('yt', y2), 'sz'], writes=[('yt', y2)])
                ssa = smn[:, y2:y2 + 1]
                rsa = smn[:, 2 + y2:3 + y2]
                S.add('act', lambda e, y2=y2, ssa=ssa: e.activation(junk[:], yt[y2][:], AF.Square, accum_out=ssa), reads=[('yt', y2)], writes=[('ssc', y2)])
                rstd_ops(C, ssa, rsa, 512, [('ssc', y2)], ('c', y2))
                S.add('dve', lambda e, y2=y2, rsa=rsa, gc=gc: e.scalar_tensor_tensor(ob[y2][:], yt[y2][:], rsa, nw[:, gc], ALU.mult, ALU.mult), reads=[('yt', y2), ('rs', ('c', y2)), 'cnw'], writes=[('ob', y2)])
                tbk = 6 + y2
                for cc in range(4):
                    S.add('pe', lambda e, tbk=tbk, cc=cc, y2=y2: e.transpose(psb[tbk][:, cc * 128:(cc + 1) * 128], ob[y2][:, cc * 128:(cc + 1) * 128], C.ident[:]), reads=[('ob', y2)], writes=[('ps', tbk)])
                dst = ost[:, g * 4:(g + 1) * 4, it * 128:(it + 1) * 128]
                srcv = psb[tbk][:, 0:512].rearrange("p (c t) -> p c t", c=4)
                S.add('act', lambda e, dst=dst, srcv=srcv: e.copy(dst, srcv), reads=[('ps', tbk)], writes=[('ost', g, it)])
            if c < 15:
                for jt in range(2):
                    S.add('pe', lambda e, jt=jt, s=s, g=g, k=k: e.matmul(ps[1][:, :512], Bt_c[s][:, jt, g * 128:(g + 1) * 128], xde[k][:, jt, :], start=(jt == 0), stop=(jt == 1)),
                          reads=[('Bt', s), ('xde', k)], writes=[('ps', 1)])
                if c == 0:
                    S.add('act', lambda e, g=g: e.copy(stf[:, g, :], ps[1][:, :512]), reads=[('ps', 1)], writes=[('stf', g)])
                else:
                    S.add('pool', lambda e, g=g, hs_=hs_: e.tensor_tensor(stf[:, g, :].rearrange("p (h q) -> p h q", h=8), stf[:, g, :].rearrange("p (h q) -> p h q", h=8), bc(eal[:, hs_], [128, 8, 64], 2), ALU.mult),
                          reads=[('stf', g), 'eal'], writes=[('stf', g)])
                    S.add('dve', lambda e, g=g: e.tensor_tensor(stf[:, g, :], ps[1][:, :512], stf[:, g, :], ALU.add), reads=[('ps', 1), ('stf', g)], writes=[('stf', g)])
                S.add('act', lambda e, g=g: e.copy(stb[:, g, :], stf[:, g, :]), reads=[('stf', g)], writes=[('stb', g)])
        S.dma(oTv[:, :, rows], ost[:], reads=[('ost', g, it) for g in range(8) for it in range(2)])
    A.pop()
    S.barrier()
    phase_outproj(C, oT, C.Wb['c_w_out'], 4096, x_src, x_dst)


def kernel(**inputs):
    nc, _ = build((0, 1, 2, 3), True)
    in_maps = [make_in_map(inputs, b) for b in range(8)]
    res = run_bass_kernel_spmd(nc, in_maps, core_ids=list(range(8)))
    return np.stack([np.asarray(r['y'], dtype=np.float32) for r in res.results], 0)
```
